# Optimizing a Trainium2 kernel written in Bass

```python
import math
import jax, jax.numpy as jnp
from jax import lax
import numpy as np

D_MODEL = 1024
BATCH = 4
SEQ = 8192
DEPTH = 1
DEC_BATCH = 128
DEC_SEQ = 4
PAST_LEN = 8192
PAGE_SIZE = 128

D_CONV = D_MODEL
CONV_WIDTH = 3
HEAD_DIM = 64
HEADS_PER_GROUP = 8
DILATED_GROUPS = ((128, 1), (512, 4), (2048, 16))
N_GROUPS = 3
D_ATT = HEADS_PER_GROUP * HEAD_DIM
D_QKV = N_GROUPS * D_ATT
D_FF = 4 * D_MODEL
N_BUCKETS = 32
MAX_DISTANCE = 2048
BLOCK = 128
N_MOD = 6
RMS_EPS = 1e-6
ATT_SCALE = HEAD_DIM ** -0.5
PROJ_SIZES = (D_CONV, D_CONV, D_CONV, D_QKV, D_QKV, D_QKV, D_MODEL, D_MODEL)
D_PROJ = 3 * D_CONV + 3 * D_QKV + 2 * D_MODEL

kernel_name = "gated_conv_dilated_swa_hybrid_step"


def rms_norm(x, g):
    xf = x.astype(jnp.float32)
    y = xf * lax.rsqrt(jnp.mean(xf * xf, axis=-1, keepdims=True) + RMS_EPS)
    return (y * g.astype(jnp.float32)).astype(x.dtype)


def t5_causal_bucket(dist):
    max_exact = N_BUCKETS // 2
    ratio = jnp.maximum(dist, 1).astype(jnp.float32) / max_exact
    large = max_exact + (jnp.log(ratio) / math.log(MAX_DISTANCE / max_exact)
                         * (N_BUCKETS - max_exact)).astype(jnp.int32)
    large = jnp.minimum(large, N_BUCKETS - 1)
    return jnp.where(dist < max_exact, dist, large)


def group_bias(rel_bias, g):
    window, dil = DILATED_GROUPS[g]
    steps = jnp.arange(window // dil + 1)
    b = rel_bias[t5_causal_bucket(steps * dil), g * HEADS_PER_GROUP:(g + 1) * HEADS_PER_GROUP]
    return b.T.astype(jnp.float32)


def softmax_stats(logits, valid):
    logits = jnp.where(valid, logits, -jnp.inf)
    m = jnp.max(logits, axis=-1, keepdims=True)
    p = jnp.exp(logits - m)
    s = jnp.sum(p, axis=-1, keepdims=True)
    return p, s, m + jnp.log(s)


def dilated_attn_prompt(q, k, v, bias, dil):
    nb, ns, nh, ne = q.shape
    wk = bias.shape[-1] - 1
    span = dil * BLOCK
    s_pad = -(-ns // span) * span
    nblk = s_pad // span

    def to_blocks(t):
        t = jnp.pad(t, ((0, 0), (0, s_pad - ns), (0, 0), (0, 0)))
        t = t.reshape(nb, nblk, BLOCK, dil, nh, ne)
        return t.transpose(0, 3, 1, 2, 4, 5)

    def with_prev(t):
        prev = jnp.pad(t[:, :, :-1], ((0, 0), (0, 0), (1, 0), (0, 0), (0, 0), (0, 0)))
        return jnp.concatenate([prev, t], axis=3)

    qb = to_blocks(q)
    kk = with_prev(to_blocks(k))
    vv = with_prev(to_blocks(v))
    i = jnp.arange(BLOCK)[:, None]
    j = jnp.arange(2 * BLOCK)[None, :]
    off = i + BLOCK - j
    band = (off >= 0) & (off <= wk)
    first = (jnp.arange(nblk)[:, None, None] > 0) | (j >= BLOCK)
    valid = (band[None] & first)[:, None]
    b = bias[:, jnp.clip(off, 0, wk)]
    logits = jnp.einsum('brnihe,brnjhe->brnhij', qb, kk,
                        preferred_element_type=jnp.float32) * ATT_SCALE + b
    p, s, lse = softmax_stats(logits, valid)
    o = jnp.einsum('brnhij,brnjhe->brnihe', p, vv.astype(jnp.float32)) / jnp.swapaxes(s, 3, 4)
    o = o.transpose(0, 2, 3, 1, 4, 5).reshape(nb, s_pad, nh, ne)[:, :ns]
    lse = jnp.swapaxes(lse[..., 0], 3, 4).transpose(0, 2, 3, 1, 4).reshape(nb, s_pad, nh)[:, :ns]
    return o, lse


def dilated_attn_decode(q, k, v, cache_k, cache_v, bias, window, dil):
    nt = q.shape[1]
    wb = cache_k.shape[1]
    wk = bias.shape[-1] - 1
    k_all = jnp.concatenate([cache_k, k], axis=1)
    v_all = jnp.concatenate([cache_v, v], axis=1)
    idx = wb + jnp.arange(nt)[:, None] - dil * jnp.arange(wk + 1)[None, :]
    valid = idx >= 0
    idc = jnp.clip(idx, 0, None)
    kg = k_all[:, idc]
    vg = v_all[:, idc]
    logits = jnp.einsum('bthe,btkhe->bhtk', q, kg,
                        preferred_element_type=jnp.float32) * ATT_SCALE + bias[:, None, :]
    p, s, lse = softmax_stats(logits, valid)
    o = jnp.einsum('bhtk,btkhe->bthe', p, vg.astype(jnp.float32)) / jnp.swapaxes(s, 1, 2)
    lse = jnp.swapaxes(lse[..., 0], 1, 2)
    keep = min(window, wb + nt)
    return o, lse, k_all[:, -keep:], v_all[:, -keep:]


def hybrid_layer(x, c, conv_state, kv_cache, rel_bias, norm1_g, norm2_g, w_ada, b_ada,
                 w_in, conv_w, q_norm_g, k_norm_g, w_conv_out, w_attn_out, w_o,
                 w_mlp_in, w_mlp_out):
    nb, ns, _ = x.shape
    decode = conv_state is not None
    mod = jnp.dot(jax.nn.silu(c), w_ada) + b_ada
    shift1, scale1, gate1, shift2, scale2, gate2 = jnp.split(mod[:, None, :], N_MOD, axis=-1)
    xn = rms_norm(x, norm1_g) * (1 + scale1) + shift1
    split_at = np.cumsum(PROJ_SIZES)[:-1].tolist()
    h, b_gate, c_gate, q, k, v, g_conv, g_att = jnp.split(jnp.dot(xn, w_in), split_at, axis=-1)

    u = c_gate * h
    if decode:
        u_ext = jnp.concatenate([conv_state, u], axis=1)
    else:
        u_ext = jnp.pad(u, ((0, 0), (CONV_WIDTH - 1, 0), (0, 0)))
    y_conv = conv_w[0] * u_ext[:, 0:ns]
    for tap in range(1, CONV_WIDTH):
        y_conv = y_conv + conv_w[tap] * u_ext[:, tap:tap + ns]
    new_conv = u_ext[:, -(CONV_WIDTH - 1):]
    conv_out = jnp.dot(b_gate * y_conv, w_conv_out)

    shp = (nb, ns, N_GROUPS, HEADS_PER_GROUP, HEAD_DIM)
    q = rms_norm(q.reshape(shp), q_norm_g)
    k = rms_norm(k.reshape(shp), k_norm_g)
    v = v.reshape(shp)
    outs, lses, new_kv = [], [], []
    for g, (window, dil) in enumerate(DILATED_GROUPS):
        bias = group_bias(rel_bias, g)
        if decode:
            o, lse, kb, vb = dilated_attn_decode(q[:, :, g], k[:, :, g], v[:, :, g],
                                                 kv_cache[2 * g], kv_cache[2 * g + 1],
                                                 bias, window, dil)
        else:
            o, lse = dilated_attn_prompt(q[:, :, g], k[:, :, g], v[:, :, g], bias, dil)
            keep = min(window, ns)
            kb, vb = k[:, -keep:, g], v[:, -keep:, g]
        outs.append(o)
        lses.append(lse)
        new_kv.extend([kb, vb])
    wts = jax.nn.softmax(jnp.stack(lses), axis=0)
    o = jnp.sum(wts[..., None] * jnp.stack(outs), axis=0)
    attn_out = jnp.dot(o.reshape(nb, ns, D_ATT).astype(x.dtype), w_attn_out)

    mixed = jax.nn.sigmoid(g_conv) * conv_out + jax.nn.sigmoid(g_att) * attn_out
    x = x + gate1 * jnp.dot(mixed, w_o)
    xn2 = rms_norm(x, norm2_g) * (1 + scale2) + shift2
    hid = jax.nn.relu(jnp.dot(xn2, w_mlp_in))
    x = x + gate2 * jnp.dot(hid * hid, w_mlp_out)
    return x, new_conv, new_kv


def setup_inputs(seed: int = 0) -> dict:
    key = jax.random.key(seed)
    ks = jax.random.split(key, 32)
    f32 = jnp.float32

    def nrm(k, shape, scale=1.0):
        return jax.random.normal(k, shape, f32) * scale

    wb = [min(w, PAST_LEN) for (w, _) in DILATED_GROUPS]
    kv = lambda kk, n: nrm(kk, (DEPTH, DEC_BATCH, n, HEADS_PER_GROUP, HEAD_DIM))
    return {
        "x_prompt": nrm(ks[0], (BATCH, SEQ, D_MODEL)),
        "x_sample": nrm(ks[1], (DEC_BATCH, DEC_SEQ, D_MODEL)),
        "c_prompt": nrm(ks[2], (BATCH, D_MODEL)),
        "c_sample": nrm(ks[3], (DEC_BATCH, D_MODEL)),
        "state_conv": nrm(ks[4], (DEPTH, DEC_BATCH, CONV_WIDTH - 1, D_CONV)),
        "cache_k1": kv(ks[5], wb[0]),
        "cache_v1": kv(ks[6], wb[0]),
        "cache_k2": kv(ks[7], wb[1]),
        "cache_v2": kv(ks[8], wb[1]),
        "cache_k3": kv(ks[9], wb[2]),
        "cache_v3": kv(ks[10], wb[2]),
        "rel_bias": nrm(ks[11], (N_BUCKETS, N_GROUPS * HEADS_PER_GROUP), 0.5),
        "norm1_g": 1.0 + nrm(ks[12], (DEPTH, D_MODEL), 0.02),
        "norm2_g": 1.0 + nrm(ks[13], (DEPTH, D_MODEL), 0.02),
        "w_ada": nrm(ks[14], (DEPTH, D_MODEL, N_MOD * D_MODEL), 0.5 * D_MODEL ** -0.5),
        "b_ada": nrm(ks[15], (DEPTH, N_MOD * D_MODEL), 0.02),
        "w_in": nrm(ks[16], (DEPTH, D_MODEL, D_PROJ), D_MODEL ** -0.5),
        "conv_w": nrm(ks[17], (DEPTH, CONV_WIDTH, D_CONV), CONV_WIDTH ** -0.5),
        "q_norm_g": 1.0 + nrm(ks[18], (DEPTH, HEAD_DIM), 0.02),
        "k_norm_g": 1.0 + nrm(ks[19], (DEPTH, HEAD_DIM), 0.02),
        "w_conv_out": nrm(ks[20], (DEPTH, D_CONV, D_MODEL), D_CONV ** -0.5),
        "w_attn_out": nrm(ks[21], (DEPTH, D_ATT, D_MODEL), D_ATT ** -0.5),
        "w_o": nrm(ks[22], (DEPTH, D_MODEL, D_MODEL), D_MODEL ** -0.5),
        "w_mlp_in": nrm(ks[23], (DEPTH, D_MODEL, D_FF), D_MODEL ** -0.5),
        "w_mlp_out": nrm(ks[24], (DEPTH, D_FF, D_MODEL), D_FF ** -0.5),
    }


def reference(x_prompt, x_sample, c_prompt, c_sample, state_conv, cache_k1, cache_v1,
              cache_k2, cache_v2, cache_k3, cache_v3, rel_bias, norm1_g, norm2_g, w_ada,
              b_ada, w_in, conv_w, q_norm_g, k_norm_g, w_conv_out, w_attn_out, w_o,
              w_mlp_in, w_mlp_out):
    yp, ys = x_prompt, x_sample
    p_states = [[] for _ in range(1 + 2 * N_GROUPS)]
    s_states = [[] for _ in range(1 + 2 * N_GROUPS)]
    caches = (cache_k1, cache_v1, cache_k2, cache_v2, cache_k3, cache_v3)
    for l in range(DEPTH):
        lw = (norm1_g[l], norm2_g[l], w_ada[l], b_ada[l], w_in[l], conv_w[l], q_norm_g[l],
              k_norm_g[l], w_conv_out[l], w_attn_out[l], w_o[l], w_mlp_in[l], w_mlp_out[l])
        yp, pc, pkv = hybrid_layer(yp, c_prompt, None, None, rel_bias, *lw)
        ys, sc, skv = hybrid_layer(ys, c_sample, state_conv[l],
                                   [cc[l] for cc in caches], rel_bias, *lw)
        for lst, a in zip(p_states, [pc] + pkv):
            lst.append(a)
        for lst, a in zip(s_states, [sc] + skv):
            lst.append(a)
    p_conv, p_k1, p_v1, p_k2, p_v2, p_k3, p_v3 = [jnp.stack(a) for a in p_states]
    s_conv, s_k1, s_v1, s_k2, s_v2, s_k3, s_v3 = [jnp.stack(a) for a in s_states]
    return (yp, ys, p_conv, p_k1, p_v1, p_k2, p_v2, p_k3, p_v3,
            s_conv, s_k1, s_v1, s_k2, s_v2, s_k3, s_v3)
```

```python
import math
from contextlib import ExitStack
import numpy as np
import concourse.bass as bass
import concourse.mybir as mybir
from concourse.ap import AP
from concourse.bass_utils import run_bass_kernel_spmd

F32, BF16 = mybir.dt.float32, mybir.dt.bfloat16
AF = mybir.ActivationFunctionType
ALU = mybir.AluOpType

NCORE = 8
D = 1024
T = 512
NH = 2048
NTOK = 4096
NT = NTOK // T
NHT = NH // T
SB_ = 16
NS = 64
DPROJ = 9728
EPS = 1e-6
WIN = (128, 512, 2048)
DIL = (1, 4, 16)


class Res:
    __slots__ = ("name", "w", "r", "const")

    def __init__(self, name, const=False):
        self.name, self.w, self.r, self.const = name, None, [], const


class Op:
    __slots__ = ("eng", "fn", "deps", "flag", "rank", "dma", "sem", "semval")


ENGS = ["pe", "act", "dve", "pool", "sp"]


class Sched:
    def __init__(self):
        self.q = {e: [] for e in ENGS}
        self.strict = False

    def op(self, eng, fn, reads=(), writes=(), dma=False):
        o = Op()
        o.eng, o.fn, o.dma, o.flag, o.deps, o.rank, o.sem, o.semval = eng, fn, dma, False, [], 0, None, 0
        dl = []
        for R in reads:
            if R.w is not None:
                dl.append((R.w, True))
        for R in writes:
            if R.w is not None:
                dl.append((R.w, False))
            dl.extend((x, False) for x in R.r)
        for d, raw in dl:
            if d is o:
                continue
            if d.eng == eng and not d.dma and not dma:
                if not (self.strict and raw and eng != "pe"):
                    continue
            if d not in o.deps:
                o.deps.append(d)
                d.flag = True
        for R in reads:
            if not R.const:
                R.r.append(o)
        for R in writes:
            R.w = o
            R.r = []
        self.q[eng].append(o)
        return o

    def emit(self, nc, es, block):
        NDS = 12
        csem = {e: es.enter_context(nc.semaphore("c_" + e)) for e in ENGS}
        dsem = {e: [es.enter_context(nc.semaphore("d_%s%d" % (e, i))) for i in range(NDS)] for e in ENGS}
        for e in ENGS:
            k = 0
            kd = 0
            for o in self.q[e]:
                if o.dma:
                    o.flag = True
                    o.sem = dsem[e][kd % NDS]
                    o.semval = 16 * (kd // NDS + 1)
                    kd += 1
                elif o.flag:
                    k += 1
                    o.rank = k

        def run(e, eng):
            waited = {}
            for o in self.q[e]:
                for d in o.deps:
                    if d.dma:
                        key, sem, val = id(d.sem), d.sem, d.semval
                    else:
                        key, sem, val = d.eng, csem[d.eng], d.rank
                    if waited.get(key, 0) >= val:
                        continue
                    waited[key] = val
                    eng.wait_ge(sem, val)
                if o.fn is None:
                    continue
                ins = o.fn(eng)
                if o.flag:
                    if o.dma:
                        ins.then_inc(o.sem, 16)
                    else:
                        ins.then_inc(csem[e], 1)

        block.tensor(lambda eng: run("pe", eng))
        block.scalar(lambda eng: run("act", eng))
        block.vector(lambda eng: run("dve", eng))
        block.gpsimd(lambda eng: run("pool", eng))
        block.sync(lambda eng: run("sp", eng))


def t5_bucket(dist):
    dist = np.asarray(dist, np.int64)
    max_exact = 16
    ratio = np.maximum(dist, 1).astype(np.float32) / np.float32(max_exact)
    large = max_exact + (np.log(ratio) / np.float32(math.log(2048 / max_exact)) * np.float32(32 - max_exact)).astype(np.int32)
    large = np.minimum(large, 31)
    return np.where(dist < max_exact, dist, large)


def build_program():
    import os
    STAGE = int(os.environ.get("KSTAGE", "9"))
    NOPROMPT = int(os.environ.get("KNOPROMPT", "0"))
    out_ops = []
    nc = bass.Bass("TRN2", target_bir_lowering=False)
    S = Sched()
    es = ExitStack()

    def din(name, shape):
        return nc.dram_tensor(name, list(shape), F32, kind="ExternalInput").ap()

    def dout(name, shape):
        return nc.dram_tensor(name, list(shape), F32, kind="ExternalOutput").ap()

    def dtmp(name, shape, dt):
        return nc.dram_tensor(name, list(shape), dt, kind="Internal").ap()

    def sb(name, shape, dt):
        return es.enter_context(nc.sbuf_tensor(name, list(shape), dt))

    def psum(name, shape, dt=F32):
        return es.enter_context(nc.psum_tensor(name, list(shape), dt))

    xT = din("xT", [D, NH + NTOK])
    xsT = din("xsT", [D, NS])
    cT = din("cT", [128, 8, 17])
    stT = din("stT", [128, 8, SB_, 2])
    b_adaT = din("b_adaT", [128, 48])
    n1g_d = din("n1g", [128, 8])
    n2g_d = din("n2g", [128, 8])
    convw_d = din("convw", [128, 8, 3])
    qkg_d = din("qkg", [128, 2])
    flag_d = din("flagv", [128, 1])
    vones_d = din("vones", [128, 5, 64])
    onehot_d = din("onehot", [32, 387])
    relb_d = din("rel_bias", [32, 24])
    ident_d = din("ident", [128, 128])
    sel_d = din("sel", [24, 12, 128])
    w_ada = din("w_ada", [D, 6144])
    w_in = din("w_in", [D, DPROJ])
    w_co = din("w_conv_out", [D, D])
    w_ao = din("w_attn_out", [512, D])
    w_o = din("w_o", [D, D])
    w_mi = din("w_mlp_in", [D, 4096])
    w_mo = din("w_mlp_out", [4096, D])
    ck = [din("ck%d" % g, [SB_, WIN[g], 512]) for g in range(3)]
    cv = [din("cv%d" % g, [SB_, WIN[g], 512]) for g in range(3)]
    kcT = [din("kc0T", [SB_, 512, 1, 128]), din("kc1T", [SB_, 512, 4, 128]), din("kc2T", [SB_, 512, 4, 128])]

    yT = dout("yT", [D, NTOK])
    ysT = dout("ysT", [D, NS])
    pconvT = dout("pconvT", [128, 8, 2])
    sconvT = dout("sconvT", [128, 8, SB_, 2])
    pkT = [dout("pk%dT" % g, [512, DIL[g], 128]) for g in range(3)]
    pvT = [dout("pv%dT" % g, [512, DIL[g], 128]) for g in range(3)]
    sk = [dout("sk%d" % g, [SB_, WIN[g], 512]) for g in range(3)]
    sv = [dout("sv%d" % g, [SB_, WIN[g], 512]) for g in range(3)]

    DBG = int(os.environ.get("KDBG", "0"))
    dbg = dout("dbg", [128, 32, 64]) if DBG else None
    dbg_n = [0]

    def dump(ap, res, name, cast=False):
        if not DBG:
            return
        k = dbg_n[0]
        dbg_n[0] += 1
        n = ap.shape[1]
        print("DBGSLOT", k, name, n)
        out_ops.append(S.op("pool" if cast else "sp", lambda e: e.dma_start(out=dbg[0:ap.shape[0], k, 0:n], in_=ap), reads=list(res), dma=True))

    wb_in = dtmp("wb_in", [D, DPROJ], BF16)
    wb_co = dtmp("wb_co", [D, D], BF16)
    wb_ao = dtmp("wb_ao", [512, D], BF16)
    wb_o = dtmp("wb_o", [D, D], BF16)
    wb_mi = dtmp("wb_mi", [D, 4096], BF16)
    wb_mo = dtmp("wb_mo", [4096, D], BF16)
    ebz = dtmp("ebz", [24, 128 * 512], F32)

    identb = sb("identb", [128, 128], BF16)
    onesb = sb("onesb", [128, 128], BF16)
    blk64 = sb("blk64", [128, 128], BF16)
    vones = sb("vones_s", [128, 5, 64], BF16)
    flagv = sb("flag_s", [128, 1], F32)
    epsv = sb("epsv", [128, 2], F32)
    Eall = sb("Eall", [128, 24, 256], BF16)
    modT = sb("modT", [128, 48, 17], F32)
    A1 = sb("A1", [128, 8, 17], F32)
    A2 = sb("A2", [128, 8, 17], F32)
    n1g = sb("n1g_s", [128, 8], F32)
    n2g = sb("n2g_s", [128, 8], F32)
    convw = sb("convw_s", [128, 8, 3], F32)
    qkg = sb("qkg_s", [128, 2], F32)
    badat = sb("badat", [128, 48], F32)
    ebn = sb("ebn", [128, 12, 4], F32)
    ucarry = sb("ucarry", [128, 8, 2], F32)
    KTW = (640, 1024, 2560)
    KT = [sb("KT%d" % g, [128, 4, KTW[g]], BF16) for g in range(3)]
    VT = [sb("VT%d" % g, [128, 4, KTW[g]], BF16) for g in range(3)]
    xTs = sb("xTs", [128, 8, T], F32)
    xnT = sb("xnT", [128, 8, T], BF16)
    sqb = [sb("sqb%d" % i, [128, T], BF16) for i in range(2)]
    rstd = sb("rstd", [128, T], F32)
    tmpf = [sb("tmpf%d" % i, [128, T], F32) for i in range(2)]
    A1a = sb("arenaA1", [128, 12 * T], BF16)
    A2a = sb("arenaA2", [128, 8 * T], BF16)
    A3a = sb("arenaA3", [128, 4 * T], F32)
    A4a = sb("arenaA4", [128, 4 * T], BF16)
    ue = [sb("ue0", [128, T + 2], F32)]
    mixT = sb("mixT", [128, 8, T], BF16)
    es_ = [sb("es%d" % i, [128, 512], F32) for i in range(2)]
    pt_ = [sb("pt%d" % i, [128, 512], BF16) for i in range(2)]
    vt_ = [sb("vt%d" % i, [128, 128], BF16) for i in range(4)]
    AS = [sb("AS0", [128, 2, T], F32)]
    oT = A2a[:, 0:4 * T].rearrange("p (c t) -> p c t", c=4)
    NWB = 3
    wring = [sb("wring%d" % i, [128, 4096], BF16) for i in range(NWB)]
    A1f = A1a[:].bitcast(F32)
    A2f = A2a[:].bitcast(F32)
    selS = A1f[0:24, 0:1536].rearrange("p (a b) -> p a b", a=12)
    oneh = A1f[0:32, 1536:1923]
    ebs = A1f[0:24, 1924:2311]
    zt = A1f[0:24, 2312:2824]
    cTs = A2f[:, 0:136].rearrange("p (a b) -> p a b", a=8)
    relb = A2f[0:32, 136:160]
    csil = A2a[:, 400:536].rearrange("p (a b) -> p a b", a=8)
    ues = sb("ues", [128, SB_, 6], F32)
    Es = sb("Es", [128, 96], BF16)
    KTn = sb("KTn", [128, 3, 4, NS], BF16)
    VTn = sb("VTn", [128, 3, 4, NS], BF16)
    pn = sb("pn", [128, NS], F32)
    prodb = sb("prodb", [128, NS], BF16)
    ASf = AS[0][:].rearrange("p a t -> p (a t)")
    accn = ASf[:, 0:256].rearrange("p (c n) -> p c n", c=4)
    sn = ASf[:, 256:512].rearrange("p (c n) -> p c n", c=4)
    accb = ASf[:, 512:768].rearrange("p (c n) -> p c n", c=4)
    sbb = ASf[:, 768:1024].rearrange("p (c n) -> p c n", c=4)
    QbdA = KT[1][:].rearrange("p c w -> p (c w)")
    QbdB = VT[1][:].rearrange("p c w -> p (c w)")

    def Qbd_ap(g):
        base = QbdA[:, g * 2048:(g + 1) * 2048] if g < 2 else QbdB[:, 0:2048]
        return base.rearrange("p (c n h) -> p c n h", c=4, h=8)

    mmps = [psum("mmps%d" % i, [128, 512]) for i in range(3)]
    nsps = psum("nsps", [128, 512])
    stp = psum("stp", [128, 2, 512])
    stps = [stp[:, 0, :], stp[:, 0, :]]
    pvps = psum("pvps", [128, 512])
    trps = psum("trps", [128, 1024], BF16)

    def R(n, const=False):
        return Res(n, const)

    r_const = R("const", True)
    r_mm = [R("mm%d" % i) for i in range(3)]
    r_ns = R("ns")
    r_st0 = R("st")
    r_st = [r_st0, r_st0]
    r_pv = R("pv")
    r_tr0 = R("tr")
    r_tr = [r_tr0, r_tr0]
    r_w = [R("w%d" % i) for i in range(NWB)]
    r_x = [R("x%d" % c) for c in range(8)]
    r_xn = [R("xn%d" % c) for c in range(8)]
    r_sq = [R("sq0"), R("sq1")]
    r_rstd = R("rstd")
    r_tmpf = [R("tf0"), R("tf1")]
    r_q = [[R("q%d_%d" % (g, c)) for c in range(4)] for g in range(3)]
    r_kt = [[R("kt%d_%d" % (g, c)) for c in range(4)] for g in range(3)]
    r_vt = [[R("vt%d_%d" % (g, c)) for c in range(4)] for g in range(3)]
    r_sg = [R("sg%d" % c) for c in range(8)]
    r_by = [R("by%d" % c) for c in range(8)]
    r_h = [R("h%d" % c) for c in range(4)]
    r_hid = [R("hid%d" % c) for c in range(32)]
    r_ue = [R("ue0")]
    r_mix = [R("mix%d" % c) for c in range(8)]
    r_es = [R("es0"), R("es1")]
    r_pt = [R("pt0"), R("pt1")]
    r_vtile = [R("vtile%d" % i) for i in range(4)]
    r_as = [R("as0")]
    r_o = [R("o%d" % c) for c in range(4)]
    r_uc = R("ucarry")
    r_mod = R("mod")
    r_E = R("E")
    r_misc = R("misc")
    r_wb = {k: R("wb_" + k) for k in ("in", "co", "ao", "o", "mi", "mo")}
    r_ebz = R("ebz")
    r_samp = R("samp")
    r_qbd = R("qbd")
    r_init = R("init")
    cnt = {"mm": 0, "st": 0, "w": 0, "sq": 0, "tf": 0, "es": 0, "vt": 0, "tr": 0, "as": 0, "ue": 0, "bal": 0}

    def nxt(key, n):
        v = cnt[key] % n
        cnt[key] += 1
        return v

    def qT_ap(g, c):
        return A1a[:, (g * 4 + c) * T:(g * 4 + c + 1) * T]

    def sgatt_ap(c):
        return A1a[:, c * T:(c + 1) * T]

    def by_ap(c):
        return A2a[:, c * T:(c + 1) * T]

    def h_ap(c):
        return A3a[:, c * T:(c + 1) * T]

    def hid_ap(k):
        if k < 12:
            return A1a[:, k * T:(k + 1) * T]
        if k < 20:
            return A2a[:, (k - 12) * T:(k - 11) * T]
        if k < 28:
            kk = k - 20
            return A3a[:, (kk // 2) * T:(kk // 2 + 1) * T].bitcast(BF16)[:, (kk % 2) * T:(kk % 2 + 1) * T]
        return A4a[:, (k - 28) * T:(k - 27) * T]

    def hid_alias(k):
        if k < 12:
            g, c = divmod(k, 4)
            return [r_q[g][c]] + ([r_sg[k]] if k < 8 else [])
        if k < 20:
            return [r_by[k - 12]] + ([r_o[k - 12]] if k < 16 else [])
        if k < 28:
            return [r_h[(k - 20) // 2]]
        return []

    def inherit(newR, oldRs):
        for o in oldRs:
            if o.w is not None:
                newR.r.append(o.w)
            newR.r.extend(o.r)

    def cast_flat(dst, src, nelem, rres, tag):
        rows = nelem // 2048
        d2 = dst.rearrange("a b -> (a b)").rearrange("(r c) -> r c", c=2048)
        s2 = src.rearrange("a b -> (a b)").rearrange("(r c) -> r c", c=2048)
        step = 512
        for r0 in range(0, rows, step):
            r1 = min(rows, r0 + step)
            S.op("pool", lambda e, r0=r0, r1=r1: e.dma_start(out=d2[r0:r1, :], in_=s2[r0:r1, :]),
                 writes=[rres], dma=True)

    def ld(eng, dst, src, wres, rres=()):
        return S.op(eng, lambda e: e.dma_start(out=dst, in_=src), reads=list(rres), writes=list(wres), dma=True)

    ld("sp", n1g[:], n1g_d[:, :], [r_misc])
    ld("sp", n2g[:], n2g_d[:, :], [r_misc])
    ld("sp", convw[:], convw_d[:, :, :], [r_misc])
    ld("sp", qkg[:], qkg_d[:, :], [r_misc])
    ld("sp", flagv[:], flag_d[:, :], [r_misc])
    ld("sp", badat[:], b_adaT[:, :], [r_misc])
    ld("pool", identb[:], ident_d[:, :], [r_misc])
    ld("pool", vones[:], vones_d[:, :, :], [r_misc])
    S.op("pool", lambda e: e.memset(onesb[:], 1.0), writes=[r_misc])
    S.op("pool", lambda e: e.memset(epsv[:, 0:1], float(D * EPS)), writes=[r_misc])
    S.op("pool", lambda e: e.memset(epsv[:, 1:2], float(64 * EPS)), writes=[r_misc])
    S.op("pool", lambda e: e.memset(blk64[:], 0.0), writes=[r_misc])
    S.op("pool", lambda e: e.memset(blk64[0:64, 0:64], 1.0), writes=[r_misc])
    S.op("pool", lambda e: e.memset(blk64[64:128, 64:128], 1.0), writes=[r_misc])
    S.op("pool", lambda e: e.memset(zt[:], 0.0), writes=[r_misc])
    S.op("pool", lambda e: e.memset(ucarry[:], 0.0), writes=[r_uc])
    S.op("dve", lambda e: e.tensor_scalar(out=qkg[:], in0=qkg[:], scalar1=8.0, scalar2=None, op0=ALU.mult),
         reads=[r_misc], writes=[r_misc])

    cast_flat(wb_in, w_in, D * DPROJ, r_wb["in"], "in")

    ld("sp", relb, relb_d[:, :], [r_misc])
    ld("sp", oneh[:], onehot_d[:, :], [r_misc])
    ld("sp", selS[:], sel_d[:, :, :], [r_misc])
    S.op("pe", lambda e: e.matmul(mmps[0][0:24, 0:387], lhsT=relb, rhs=oneh[:], start=True, stop=True),
         reads=[r_misc], writes=[r_mm[0]])
    S.op("act", lambda e: e.activation(out=ebs[:], in_=mmps[0][0:24, 0:387], func=AF.Exp),
         reads=[r_mm[0]], writes=[r_E])
    for g in range(3):
        S.op("sp", lambda e, g=g: e.dma_start(out=zt[8 * g:8 * g + 8, 128:257],
                                             in_=ebs[8 * g:8 * g + 8, 129 * g:129 * g + 129]),
             reads=[r_E, r_misc], writes=[r_init], dma=True)
    S.op("sp", lambda e: e.dma_start(out=ebz[:, :].rearrange("a (j c) -> a j c", c=512),
                                     in_=zt.unsqueeze(1).to_broadcast([24, 128, 512])),
         reads=[r_init], writes=[r_ebz], dma=True)
    for g in range(3):
        src = AP(ebz.tensor, 8 * g * 65536 + 128, [[511, 128], [65536, 8], [1, 256]])
        S.op("pool", lambda e, g=g, src=src: e.dma_start(out=Eall[:, 8 * g:8 * g + 8, :], in_=src),
             reads=[r_ebz], writes=[r_E], dma=True)
    for g in range(3):
        for c in range(4):
            S.op("pe", lambda e, g=g, c=c: e.matmul(nsps[:, (g * 4 + c) * 4:(g * 4 + c) * 4 + 4],
                                                     lhsT=selS[:, g * 4 + c, :], rhs=ebs[:, 129 * g:129 * g + 4],
                                                     start=True, stop=True),
                 reads=[r_E, r_misc], writes=[r_ns])
    S.op("act", lambda e: e.activation(out=ebn[:].rearrange("p a b -> p (a b)"), in_=nsps[:, 0:48], func=AF.Copy),
         reads=[r_ns], writes=[r_samp])

    ld("sp", cTs[:], cT[:, :, :], [r_misc])
    S.op("act", lambda e: e.activation(out=csil[:], in_=cTs[:], func=AF.Silu), reads=[r_misc], writes=[r_mod])
    for blk in range(12):
        slot = nxt("w", NWB)
        wt = wring[slot][:].rearrange("p (c f) -> p c f", c=8)
        src = w_ada[:, blk * 512:(blk + 1) * 512].rearrange("(c p) f -> p c f", p=128)
        S.op("pool", lambda e, wt=wt, src=src: e.dma_start(out=wt, in_=src), writes=[r_w[slot]], dma=True)
        for j in range(4):
            k = blk * 4 + j
            b = nxt("mm", 3)

            def f(e, wt=wt, j=j, b=b):
                for c in range(8):
                    ins = e.matmul(mmps[b][:, 0:17], lhsT=wt[:, c, j * 128:(j + 1) * 128], rhs=csil[:, c, :],
                                   start=(c == 0), stop=(c == 7))
                return ins
            S.op("pe", f, reads=[r_w[slot], r_mod], writes=[r_mm[b]])
            S.op("act", lambda e, k=k, b=b: e.activation(out=modT[:, k, :], in_=mmps[b][:, 0:17], func=AF.Identity,
                                                        bias=badat[:, k:k + 1], scale=1.0),
                 reads=[r_mm[b], r_misc], writes=[r_mod])
    for (Ax, gx, off) in ((A1, n1g, 8), (A2, n2g, 32)):
        S.op("dve", lambda e, Ax=Ax, off=off: e.tensor_scalar(out=Ax[:], in0=modT[:, off:off + 8, :], scalar1=1.0,
                                                             scalar2=32.0, op0=ALU.add, op1=ALU.mult),
             reads=[r_mod], writes=[r_mod])
        S.op("dve", lambda e, Ax=Ax, gx=gx: e.tensor_tensor(out=Ax[:], in0=Ax[:],
                                                           in1=gx[:].unsqueeze(2).to_broadcast([128, 8, 17]),
                                                           op=ALU.mult),
             reads=[r_mod, r_misc], writes=[r_mod])
    cast_flat(wb_co, w_co, D * D, r_wb["co"], "co")
    cast_flat(wb_ao, w_ao, 512 * D, r_wb["ao"], "ao")
    cast_flat(wb_o, w_o, D * D, r_wb["o"], "o")
    cast_flat(wb_mi, w_mi, D * 4096, r_wb["mi"], "mi")
    cast_flat(wb_mo, w_mo, 4096 * D, r_wb["mo"], "mo")

    copy_jobs = []
    for g in (2, 1, 0):
        for b in range(SB_):
            for (dst, src) in ((sk[g], ck[g]), (sv[g], cv[g])):
                copy_jobs.append((dst[b, 0:WIN[g] - 4, :], src[b, 4:WIN[g], :]))

    def issue_copies(n):
        for _ in range(n):
            if copy_jobs:
                d_, s_ = copy_jobs.pop(0)
                out_ops.append(S.op("sp", lambda e, d_=d_, s_=s_: e.dma_start(out=d_, in_=s_), dma=True))

    def wload(kind, idx):
        slot = nxt("w", NWB)
        if kind == "in":
            src = wb_in[:, idx * 512:(idx + 1) * 512].rearrange("(c p) f -> p c f", p=128)
            view = wring[slot][:].rearrange("p (c f) -> p c f", c=8)
        elif kind in ("co", "o"):
            wsrc = wb_co if kind == "co" else wb_o
            src = wsrc[:, idx * 512:(idx + 1) * 512].rearrange("(c p) f -> p c f", p=128)
            view = wring[slot][:].rearrange("p (c f) -> p c f", c=8)
        elif kind == "ao":
            src = wb_ao[:, :].rearrange("(c p) f -> p c f", p=128)
            view = wring[slot][:].rearrange("p (c f) -> p c f", c=4)
        elif kind == "mi":
            src = wb_mi[:, idx * 512:(idx + 1) * 512].rearrange("(c p) f -> p c f", p=128)
            view = wring[slot][:].rearrange("p (c f) -> p c f", c=8)
        else:
            src = wb_mo[:, idx * 128:(idx + 1) * 128].rearrange("(c p) f -> p c f", p=128)
            view = wring[slot][:].rearrange("p (c f) -> p c f", c=32)
        S.op("sp", lambda e, view=view, src=src: e.dma_start(out=view, in_=src),
             reads=[r_wb[kind]], writes=[r_w[slot]], dma=True)
        if STAGE >= 5:
            issue_copies(1 if cnt["w"] % 4 == 0 else 0)
        return view, r_w[slot]

    def mm_group(wview, j, nk, rhs_fn, rhs_res, wres, N):
        b = nxt("mm", 3)

        def f(e):
            for c in range(nk):
                ins = e.matmul(mmps[b][:, 0:N], lhsT=wview[:, c, j * 128:(j + 1) * 128], rhs=rhs_fn(c),
                               start=(c == 0), stop=(c == nk - 1))
            return ins
        S.op("pe", f, reads=[wres] + list(rhs_res), writes=[r_mm[b]])
        return b

    def v3(ap):
        return ap.rearrange("p (b t) -> p b t", t=4)

    def mb(tile, k):
        return tile[:, k, 1:17].unsqueeze(2).to_broadcast([128, SB_, 4])

    def norm_stage(N, A_sc, shift_sc, sample, Atok=None, shtok=None):
        for c in range(8):
            i = nxt("sq", 2)
            S.op("act", lambda e, c=c, i=i: e.activation(out=sqb[i][:, 0:N], in_=xTs[:, c, 0:N], func=AF.Square),
                 reads=[r_x[c]], writes=[r_sq[i]])
            S.op("pe", lambda e, c=c, i=i: e.matmul(nsps[:, 0:N], lhsT=onesb[:], rhs=sqb[i][:, 0:N],
                                                     start=(c == 0), stop=(c == 7)),
                 reads=[r_sq[i], r_misc], writes=[r_ns] if c == 0 else [r_ns])
        S.op("act", lambda e: e.activation(out=rstd[:, 0:N], in_=nsps[:, 0:N], func=AF.Sqrt, bias=epsv[:, 0:1], scale=1.0),
             reads=[r_ns, r_misc], writes=[r_rstd])
        S.op("dve", lambda e: e.reciprocal(out=rstd[:, 0:N], in_=rstd[:, 0:N]), reads=[r_rstd], writes=[r_rstd])
        for c in range(8):
            i = nxt("tf", 2)
            if not sample:
                S.op("dve", lambda e, c=c, i=i: e.scalar_tensor_tensor(out=tmpf[i][:, 0:N], in0=xTs[:, c, 0:N],
                                                                       scalar=A_sc[:, c, 0:1], in1=rstd[:, 0:N],
                                                                       op0=ALU.mult, op1=ALU.mult),
                     reads=[r_x[c], r_rstd, r_mod], writes=[r_tmpf[i]])
                S.op("act", lambda e, c=c, i=i: e.activation(out=xnT[:, c, 0:N], in_=tmpf[i][:, 0:N], func=AF.Identity,
                                                             bias=shift_sc[:, c, 0:1], scale=1.0),
                     reads=[r_tmpf[i], r_mod], writes=[r_xn[c]])
            else:
                S.op("dve", lambda e, c=c, i=i: e.tensor_tensor(out=tmpf[i][:, 0:N], in0=xTs[:, c, 0:N],
                                                                in1=rstd[:, 0:N], op=ALU.mult),
                     reads=[r_x[c], r_rstd], writes=[r_tmpf[i]])
                S.op("dve", lambda e, c=c, i=i: e.tensor_tensor(out=v3(tmpf[i][:, 0:N]), in0=v3(tmpf[i][:, 0:N]),
                                                                in1=mb(Atok, c), op=ALU.mult),
                     reads=[r_mod], writes=[r_tmpf[i]])
                S.op("dve", lambda e, c=c, i=i: e.tensor_tensor(out=v3(xnT[:, c, 0:N]), in0=v3(tmpf[i][:, 0:N]),
                                                                in1=mb(modT, shtok + c), op=ALU.add),
                     reads=[r_tmpf[i], r_mod], writes=[r_xn[c]])

    def kv_dst(buf, g, c, N, sample):
        if sample:
            return (KTn if buf is KT else VTn)[:, g, c, :], None
        if g == 0:
            return buf[0][:, c, 128:128 + T], None
        d = DIL[g]
        L = T // d
        hist = 128
        v = buf[g][:, c, :].rearrange("p (r l) -> p r l", r=d)[:, :, hist:hist + L]
        return v.rearrange("p r l -> p l r"), (L, d)

    def src_view(ap2d, perm):
        if perm is None:
            return ap2d
        L, d = perm
        return ap2d.rearrange("p (l r) -> p l r", r=d)

    def qk_block(wv, wres, g, which, N, sample, qdst=None):
        for j in range(4):
            b = mm_group(wv, j, 8, lambda c: xnT[:, c, 0:N], r_xn, wres, N)
            i = nxt("sq", 2)
            S.op("act", lambda e, b=b, i=i: e.activation(out=sqb[i][:, 0:N], in_=mmps[b][:, 0:N], func=AF.Square),
                 reads=[r_mm[b]], writes=[r_sq[i]])
            S.op("pe", lambda e, i=i: e.matmul(nsps[:, 0:N], lhsT=blk64[:], rhs=sqb[i][:, 0:N], start=True, stop=True),
                 reads=[r_sq[i], r_misc], writes=[r_ns])
            S.op("act", lambda e: e.activation(out=rstd[:, 0:N], in_=nsps[:, 0:N], func=AF.Sqrt, bias=epsv[:, 1:2], scale=1.0),
                 reads=[r_ns, r_misc], writes=[r_rstd])
            S.op("dve", lambda e: e.reciprocal(out=rstd[:, 0:N], in_=rstd[:, 0:N]), reads=[r_rstd], writes=[r_rstd])
            if which == 0:
                if sample:
                    dst, perm = qT_ap(g, j)[:, 0:N], None
                else:
                    if g == 0:
                        dst, perm = qT_ap(g, j), None
                    else:
                        d = DIL[g]
                        dst = qT_ap(g, j).rearrange("p (r l) -> p l r", r=d)
                        perm = (T // d, d)
                wr = [r_q[g][j]]
            else:
                dst, perm = kv_dst(KT, g, j, N, sample)
                wr = [r_kt[g][j]]
            S.op("dve", lambda e, b=b, dst=dst, perm=perm, which=which: e.scalar_tensor_tensor(
                out=dst, in0=src_view(mmps[b][:, 0:N], perm), scalar=qkg[:, which:which + 1],
                in1=src_view(rstd[:, 0:N], perm), op0=ALU.mult, op1=ALU.mult),
                reads=[r_mm[b], r_rstd, r_misc], writes=wr)

    def v_block(wv, wres, g, N, sample, halo):
        for j in range(4):
            b = mm_group(wv, j, 8, lambda c: xnT[:, c, 0:N], r_xn, wres, N)
            dst, perm = kv_dst(VT, g, j, N, sample)
            if halo:
                S.op("act", lambda e, b=b, dst=dst, perm=perm: e.activation(out=dst, in_=src_view(mmps[b][:, 0:N], perm),
                                                                            func=AF.Identity, scale=flagv[:, 0:1]),
                     reads=[r_mm[b], r_misc], writes=[r_vt[g][j]])
            else:
                S.op("act", lambda e, b=b, dst=dst, perm=perm: e.activation(out=dst, in_=src_view(mmps[b][:, 0:N], perm),
                                                                            func=AF.Copy),
                     reads=[r_mm[b]], writes=[r_vt[g][j]])

    def shift_hist():
        for g in range(3):
            for buf, rr in ((KT, r_kt), (VT, r_vt)):
                for c in range(4):
                    if g == 0:
                        S.op("pool", lambda e, buf=buf, c=c: e.tensor_copy(out=buf[0][:, c, 0:128], in_=buf[0][:, c, T:T + 128]),
                             reads=[rr[0][c]], writes=[rr[0][c]])
                    else:
                        d = DIL[g]
                        L = T // d
                        v = buf[g][:, c, :].rearrange("p (r l) -> p r l", r=d)
                        S.op("pool", lambda e, v=v, L=L: e.tensor_copy(out=v[:, :, 0:128], in_=v[:, :, L:L + 128]),
                             reads=[rr[g][c]], writes=[rr[g][c]])

    def load_x(src_ap, N):
        for c in range(8):
            S.op("act", lambda e, c=c: e.dma_start(out=xTs[:, c, 0:N], in_=src_ap[c * 128:(c + 1) * 128, :]),
                 writes=[r_x[c]], dma=True)

    def conv_branch(N, sample, halo_only=False):
        for half in range(2):
            wv_h, wr_h = wload("in", 0 + half)
            hb = []
            for j in range(4):
                if halo_only:
                    b = mm_group(wv_h, j, 8, lambda c: xnT[:, c, T - 2:T], r_xn, wr_h, 2)
                    S.op("act", lambda e, b=b, j=j: e.activation(out=h_ap(j)[:, 0:2], in_=mmps[b][:, 0:2], func=AF.Copy),
                         reads=[r_mm[b]], writes=[r_h[j]])
                else:
                    b = mm_group(wv_h, j, 8, lambda c: xnT[:, c, 0:N], r_xn, wr_h, N)
                    S.op("act", lambda e, b=b, j=j: e.activation(out=h_ap(j)[:, 0:N], in_=mmps[b][:, 0:N], func=AF.Copy),
                         reads=[r_mm[b]], writes=[r_h[j]])
            wv_c, wr_c = wload("in", 4 + half)
            for j in range(4):
                cc = half * 4 + j
                if halo_only:
                    b = mm_group(wv_c, j, 8, lambda c: xnT[:, c, T - 2:T], r_xn, wr_c, 2)
                    S.op("dve", lambda e, b=b, j=j, cc=cc: e.scalar_tensor_tensor(
                        out=ucarry[:, cc, :], in0=mmps[b][:, 0:2], scalar=flagv[:, 0:1], in1=h_ap(j)[:, 0:2],
                        op0=ALU.mult, op1=ALU.mult), reads=[r_mm[b], r_h[j], r_misc], writes=[r_uc])
                    continue
                b = mm_group(wv_c, j, 8, lambda c: xnT[:, c, 0:N], r_xn, wr_c, N)
                if not sample:
                    u = nxt("ue", 1)
                    S.op("pool", lambda e, u=u, cc=cc: e.tensor_copy(out=ue[u][:, 0:2], in_=ucarry[:, cc, :]),
                         reads=[r_uc], writes=[r_ue[u]])
                    S.op("dve", lambda e, b=b, j=j, u=u: e.tensor_tensor(out=ue[u][:, 2:2 + N], in0=mmps[b][:, 0:N],
                                                                         in1=h_ap(j)[:, 0:N], op=ALU.mult),
                         reads=[r_mm[b], r_h[j]], writes=[r_ue[u]])
                    S.op("pool", lambda e, u=u, cc=cc: e.tensor_copy(out=ucarry[:, cc, :], in_=ue[u][:, N:N + 2]),
                         reads=[r_ue[u]], writes=[r_uc])
                    y = h_ap(j)[:, 0:N]
                    S.op("pool", lambda e, u=u, cc=cc, y=y: e.tensor_scalar(out=y, in0=ue[u][:, 0:N], scalar1=convw[:, cc, 0:1],
                                                                            scalar2=None, op0=ALU.mult),
                         reads=[r_ue[u], r_misc], writes=[r_h[j]])
                    for tap in (1, 2):
                        S.op("dve", lambda e, u=u, cc=cc, y=y, tap=tap: e.scalar_tensor_tensor(
                            out=y, in0=ue[u][:, tap:tap + N], scalar=convw[:, cc, tap:tap + 1], in1=y,
                            op0=ALU.mult, op1=ALU.add), reads=[r_ue[u], r_misc], writes=[r_h[j]])
                else:
                    S.op("sp", lambda e, cc=cc: e.dma_start(out=ues[:, :, 0:2], in_=stT[:, cc, :, :]),
                         writes=[r_ue[0]], dma=True)
                    S.op("dve", lambda e, b=b, j=j: e.tensor_tensor(
                        out=ues[:, :, 2:6], in0=mmps[b][:, 0:N].rearrange("p (b t) -> p b t", t=4),
                        in1=h_ap(j)[:, 0:N].rearrange("p (b t) -> p b t", t=4), op=ALU.mult),
                        reads=[r_mm[b], r_h[j]], writes=[r_ue[0]])
                    out_ops.append(S.op("sp", lambda e, cc=cc: e.dma_start(out=sconvT[:, cc, :, :], in_=ues[:, :, 4:6]),
                                        reads=[r_ue[0]], dma=True))
                    y = h_ap(j)[:, 0:N].rearrange("p (b t) -> p b t", t=4)
                    S.op("pool", lambda e, cc=cc, y=y: e.tensor_scalar(out=y, in0=ues[:, :, 0:4], scalar1=convw[:, cc, 0:1],
                                                                       scalar2=None, op0=ALU.mult),
                         reads=[r_ue[0], r_misc], writes=[r_h[j]])
                    for tap in (1, 2):
                        S.op("dve", lambda e, cc=cc, y=y, tap=tap: e.scalar_tensor_tensor(
                            out=y, in0=ues[:, :, tap:tap + 4], scalar=convw[:, cc, tap:tap + 1], in1=y,
                            op0=ALU.mult, op1=ALU.add), reads=[r_ue[0], r_misc], writes=[r_h[j]])
            if halo_only:
                continue
            wv_b, wr_b = wload("in", 2 + half)
            for j in range(4):
                cc = half * 4 + j
                b = mm_group(wv_b, j, 8, lambda c: xnT[:, c, 0:N], r_xn, wr_b, N)
                S.op("dve", lambda e, b=b, j=j, cc=cc: e.tensor_tensor(out=by_ap(cc)[:, 0:N], in0=mmps[b][:, 0:N],
                                                                      in1=h_ap(j)[:, 0:N], op=ALU.mult),
                     reads=[r_mm[b], r_h[j]], writes=[r_by[cc]])

    def evac_vt(src_ps, dst, rsrc, rdst, npart):
        if nxt("bal", 2) == 0:
            S.op("act", lambda e: e.activation(out=dst, in_=src_ps, func=AF.Copy), reads=[rsrc], writes=[rdst])
        else:
            S.op("dve", lambda e: e.tensor_copy(out=dst, in_=src_ps), reads=[rsrc], writes=[rdst])

    def attn_block(g, c, qap, kchunks, vchunks, nq, as_i, as_view, first, ones_list):
        st = nxt("st", 2)
        W = 2 * nq

        def fs(e):
            for hh in range(2):
                pr = slice(hh * 64, hh * 64 + 64)
                for ci, (kap, nk) in enumerate(kchunks):
                    ins = e.matmul(stp[0:nk, hh, ci * nq: ci * nq + nq], lhsT=kap[pr, :], rhs=qap[pr, :],
                                   start=True, stop=True)
            return ins
        S.op("pe", fs, reads=[r_q[g][c], r_kt[g][c]], writes=[r_st[st]])
        ei = nxt("es", 2)
        nk0 = kchunks[0][1]
        stv = stp[:, :, 0:W].rearrange("p h (k q) -> p h k q", k=2)
        esv = es_[ei][:, 0:2 * W].rearrange("p (h k q) -> p h k q", h=2, k=2)
        ptv = pt_[ei][:, 0:2 * W].rearrange("p (h k q) -> p h k q", h=2, k=2)
        gh = 8 * g + 2 * c
        if nk0 == 128:
            S.op("act", lambda e: e.activation(out=es_[ei][:, 0:2 * W].rearrange("p (h w) -> p h w", h=2), in_=stp[:, :, 0:W],
                                               func=AF.Exp, scale=0.125),
                 reads=[r_st[st]], writes=[r_es[ei]])
            Ev = Eall[:, gh:gh + 2, :].rearrange("p h (k q) -> p h k q", k=2)[:, :, :, 0:nq]
            S.op("dve", lambda e: e.tensor_tensor(out=ptv, in0=esv, in1=Ev, op=ALU.mult),
                 reads=[r_es[ei], r_E], writes=[r_pt[ei]])
        else:
            S.op("act", lambda e: e.activation(out=esv[:, :, 1, :], in_=stv[:, :, 1, :], func=AF.Exp, scale=0.125),
                 reads=[r_st[st]], writes=[r_es[ei]])
            S.op("act", lambda e: e.activation(out=esv[0:nk0, :, 0, :], in_=stv[0:nk0, :, 0, :], func=AF.Exp, scale=0.125),
                 reads=[r_st[st]], writes=[r_es[ei]])
            S.op("dve", lambda e: e.tensor_tensor(out=ptv[:, :, 1, :], in0=esv[:, :, 1, :],
                                                  in1=Eall[:, gh:gh + 2, 128:128 + nq], op=ALU.mult),
                 reads=[r_es[ei], r_E], writes=[r_pt[ei]])
            S.op("dve", lambda e: e.tensor_tensor(out=ptv[0:nk0, :, 0, :], in0=esv[0:nk0, :, 0, :],
                                                  in1=Eall[0:nk0, gh:gh + 2, 0:nq], op=ALU.mult),
                 reads=[r_es[ei], r_E], writes=[r_pt[ei]])
        vts = []
        use_act = (nxt("bal", 2) == 0)
        for ci, (vap, nk) in enumerate(vchunks):
            S.op("pe", lambda e, vap=vap, nk=nk, ci=ci: e.transpose(trps[0:nk, ci * 512: ci * 512 + 128], vap, identb[:]),
                 reads=[r_vt[g][c], r_misc], writes=[r_tr0])
        for ci, (vap, nk) in enumerate(vchunks):
            vi = nxt("vt", 4)
            src_ps = trps[0:nk, ci * 512: ci * 512 + 128]
            dstv = vt_[vi][0:nk, :]
            if use_act:
                S.op("act", lambda e, src_ps=src_ps, dstv=dstv: e.activation(out=dstv, in_=src_ps, func=AF.Copy),
                     reads=[r_tr0], writes=[r_vtile[vi]])
            else:
                S.op("dve", lambda e, src_ps=src_ps, dstv=dstv: e.tensor_copy(out=dstv, in_=src_ps),
                     reads=[r_tr0], writes=[r_vtile[vi]])
            vts.append((vi, nk))

        def fpv(e):
            for hh in range(2):
                pr = slice(hh * 64, hh * 64 + 64)
                for ci, (vi, nk) in enumerate(vts):
                    e.matmul(pvps[pr, 0:nq], lhsT=vt_[vi][0:nk, pr], rhs=ptv[0:nk, hh, ci, :],
                             start=(ci == 0), stop=(ci == 1))
                for ci, (vi, nk) in enumerate(vts):
                    ins = e.matmul(pvps[pr, 128:128 + nq], lhsT=ones_list[ci][0:nk, :], rhs=ptv[0:nk, hh, ci, :],
                                   start=(ci == 0), stop=(ci == 1))
            return ins
        S.op("pe", fpv, reads=[r_pt[ei], r_misc] + [r_vtile[vi] for vi, _ in vts], writes=[r_pv])
        pvv = pvps[:, 0:256].rearrange("p (a q) -> p a q", a=2)[:, :, 0:nq]
        if first:
            S.op("act", lambda e: e.activation(out=as_view, in_=pvv, func=AF.Copy), reads=[r_pv], writes=[r_as[as_i]])
        else:
            S.op("dve", lambda e: e.tensor_tensor(out=as_view, in0=pvv, in1=as_view, op=ALU.add),
                 reads=[r_pv], writes=[r_as[as_i]])

    def attention_tile(ti):
        for c in range(4):
            ai = nxt("as", 1)
            asb = AS[ai]
            for blk in range(4):
                q = qT_ap(0, c)[:, blk * 128:(blk + 1) * 128]
                kc = KT[0][:, c, 128 + blk * 128: 256 + blk * 128]
                kp = KT[0][:, c, blk * 128: 128 + blk * 128]
                vc = VT[0][:, c, 128 + blk * 128: 256 + blk * 128]
                vp = VT[0][:, c, blk * 128: 128 + blk * 128]
                halo = (ti == 0 and blk == 0)
                attn_block(0, c, q, [(kc, 128), (kp, 128)], [(vc, 128), (vp, 128)], 128, ai,
                           asb[:, :, blk * 128:(blk + 1) * 128], True,
                           [vones[:, 4, :], vones[:, 0, :] if halo else vones[:, 4, :]])
            for r in range(4):
                q = qT_ap(1, c)[:, r * 128:(r + 1) * 128]
                kv = KT[1][:, c, :].rearrange("p (r l) -> p r l", r=4)
                vv = VT[1][:, c, :].rearrange("p (r l) -> p r l", r=4)
                halo = (ti == 0)
                attn_block(1, c, q, [(kv[:, r, 128:256], 128), (kv[:, r, 0:128], 128)],
                           [(vv[:, r, 128:256], 128), (vv[:, r, 0:128], 128)], 128, ai,
                           asb[:].rearrange("p a (l r) -> p a l r", r=4)[:, :, :, r], False,
                           [vones[:, 4, :], vones[:, 0, :] if halo else vones[:, 4, :]])
            for r in range(16):
                q = qT_ap(2, c)[:, r * 32:(r + 1) * 32]
                kv = KT[2][:, c, :].rearrange("p (r l) -> p r l", r=16)
                vv = VT[2][:, c, :].rearrange("p (r l) -> p r l", r=16)
                attn_block(2, c, q, [(kv[:, r, 128:160], 32), (kv[:, r, 0:128], 128)],
                           [(vv[:, r, 128:160], 32), (vv[:, r, 0:128], 128)], 32, ai,
                           asb[:].rearrange("p a (l r) -> p a l r", r=16)[:, :, :, r], False,
                           [vones[:, 4, :], vones[:, min(ti, 4), :]])
            S.op("dve", lambda e, asb=asb: e.reciprocal(out=asb[:, 1, :], in_=asb[:, 1, :]), reads=[r_as[ai]], writes=[r_as[ai]])
            S.op("dve", lambda e, asb=asb, c=c: e.tensor_tensor(out=oT[:, c, :], in0=asb[:, 0, :], in1=asb[:, 1, :], op=ALU.mult),
                 reads=[r_as[ai]], writes=[r_o[c]])

    def sample_attention():
        N = NS
        for c in range(4):
            inherit(r_qbd, [r_kt[1][c], r_vt[1][c]])
        inherit(r_samp, [r_as[0]])
        S.op("pool", lambda e: e.memset(QbdA[:, 0:4096], 0.0), writes=[r_qbd])
        S.op("pool", lambda e: e.memset(QbdB[:, 0:2048], 0.0), writes=[r_qbd])
        for g in range(3):
            for c in range(4):
                for hh in range(2):
                    pr = slice(hh * 64, hh * 64 + 64)
                    S.op("pool", lambda e, g=g, c=c, hh=hh, pr=pr: e.tensor_copy(out=Qbd_ap(g)[pr, c, :, 2 * c + hh],
                                                                                 in_=qT_ap(g, c)[pr, 0:N]),
                         reads=[r_q[g][c]], writes=[r_qbd])
        S.op("pool", lambda e: e.tensor_copy(out=Es[:, 0:32].rearrange("p (t h) -> p t h", h=8),
                                             in_=Eall[:, 0:8, 128:132].rearrange("p h t -> p t h")),
             reads=[r_E], writes=[r_samp])
        for g in (1, 2):
            S.op("pool", lambda e, g=g: e.tensor_copy(out=Es[:, 32 * g:32 * g + 32].rearrange("p (t h) -> p t h", h=8),
                                                      in_=Eall[:, 8 * g:8 * g + 8, 128:129].rearrange("p h t -> p t h").to_broadcast([128, 4, 8])),
                 reads=[r_E], writes=[r_samp])
        S.op("pool", lambda e: e.memset(accn[:], 0.0), writes=[r_samp])
        S.op("pool", lambda e: e.memset(sn[:], 0.0), writes=[r_samp])
        for g in range(3):
            for c in range(4):
                qv = qT_ap(g, c)[:, 0:N].rearrange("p (b t) -> p b t", t=4)
                kvw = KTn[:, g, c, :].rearrange("p (b t) -> p b t", t=4)
                vvw = VTn[:, g, c, :].rearrange("p (b t) -> p b t", t=4)
                for dl in range(4 if g == 0 else 1):
                    n = 4 - dl
                    pv3 = prodb[:, 0:SB_ * n].rearrange("p (b t) -> p b t", t=n)
                    S.op("dve", lambda e, qv=qv, kvw=kvw, dl=dl, n=n, pv3=pv3: e.tensor_tensor(
                        out=pv3, in0=qv[:, :, dl:4], in1=kvw[:, :, 0:n], op=ALU.mult),
                        reads=[r_q[g][c], r_kt[g][c]], writes=[r_samp])
                    S.op("pe", lambda e, n=n: e.matmul(nsps[:, 0:SB_ * n], lhsT=blk64[:], rhs=prodb[:, 0:SB_ * n], start=True, stop=True),
                         reads=[r_samp, r_misc], writes=[r_ns])
                    S.op("act", lambda e, n=n: e.activation(out=pn[:, 0:SB_ * n], in_=nsps[:, 0:SB_ * n], func=AF.Exp, scale=0.125),
                         reads=[r_ns], writes=[r_samp])
                    pn3 = pn[:, 0:SB_ * n].rearrange("p (b t) -> p b t", t=n)
                    S.op("dve", lambda e, g=g, c=c, dl=dl, pn3=pn3: e.tensor_scalar(
                        out=pn3, in0=pn3, scalar1=ebn[:, g * 4 + c, dl:dl + 1], scalar2=None, op0=ALU.mult),
                        reads=[r_samp], writes=[r_samp])
                    snv = sn[:, c, :].rearrange("p (b t) -> p b t", t=4)[:, :, dl:4]
                    acv = accn[:, c, :].rearrange("p (b t) -> p b t", t=4)[:, :, dl:4]
                    S.op("dve", lambda e, snv=snv, pn3=pn3: e.tensor_tensor(out=snv, in0=snv, in1=pn3, op=ALU.add),
                         reads=[r_samp], writes=[r_samp])
                    S.op("dve", lambda e, pn3=pn3, vvw=vvw, n=n: e.tensor_tensor(out=pn3, in0=pn3, in1=vvw[:, :, 0:n], op=ALU.mult),
                         reads=[r_samp, r_vt[g][c]], writes=[r_samp])
                    S.op("dve", lambda e, acv=acv, pn3=pn3: e.tensor_tensor(out=acv, in0=acv, in1=pn3, op=ALU.add),
                         reads=[r_samp], writes=[r_samp])
        K2f = KT[2][:].rearrange("p c w -> p (c w)")
        V2f = VT[2][:].rearrange("p c w -> p (c w)")
        kts = [K2f[:, i * 4608:(i + 1) * 4608].rearrange("p (c k j) -> p c k j", c=4, k=9) for i in range(2)]
        vts = [V2f[:, i * 4608:(i + 1) * 4608].rearrange("p (k f) -> p k f", k=9) for i in range(2)]
        r_kts = [R("kts0"), R("kts1")]
        r_vts = [R("vts0"), R("vts1")]
        for i in range(2):
            for c in range(4):
                inherit(r_kts[i], [r_kt[2][c]])
                inherit(r_vts[i], [r_vt[2][c]])
        for b in range(SB_):
            i = b % 2
            for g in range(3):
                nt = 1 if g == 0 else 4
                k0 = 0 if g == 0 else (1 if g == 1 else 5)
                S.op("pool", lambda e, g=g, b=b, i=i, k0=k0, nt=nt: e.dma_start(
                    out=kts[i][:, :, k0:k0 + nt, :], in_=kcT[g][b].rearrange("(c p) t j -> p c t j", p=128)),
                    writes=[r_kts[i]], dma=True)
                d = DIL[g]
                S.op("pool", lambda e, g=g, b=b, i=i, k0=k0, nt=nt, d=d: e.dma_start(
                    out=vts[i][:, k0:k0 + nt, :], in_=cv[g][b].rearrange("(j t) f -> j t f", t=d)[:, 0:nt, :]),
                    writes=[r_vts[i]], dma=True)
            st = nxt("st", 2)

            def fs(e, b=b, i=i, st=st):
                for kap in range(9):
                    g = 0 if kap == 0 else (1 if kap < 5 else 2)
                    t = 0 if kap == 0 else (kap - 1) % 4
                    if g == 0:
                        col, ncol = 0, 32
                    else:
                        col, ncol = 32 * g + 8 * t, 8
                    for c in range(4):
                        if g == 0:
                            rhs = Qbd_ap(0)[:, c, 4 * b:4 * b + 4, :]
                        else:
                            rhs = Qbd_ap(g)[:, c, 4 * b + t, :]
                        ins = e.matmul(stps[st][:, col:col + ncol], lhsT=kts[i][:, c, kap, :], rhs=rhs,
                                       start=(c == 0), stop=(c == 3))
                return ins
            S.op("pe", fs, reads=[r_kts[i], r_qbd], writes=[r_st[st]])
            ei = nxt("es", 2)
            S.op("act", lambda e, st=st, ei=ei: e.activation(out=es_[ei][:, 0:96], in_=stps[st][:, 0:96], func=AF.Exp, scale=0.125),
                 reads=[r_st[st]], writes=[r_es[ei]])
            S.op("dve", lambda e, ei=ei: e.tensor_tensor(out=pt_[ei][:, 0:96], in0=es_[ei][:, 0:96], in1=Es[:], op=ALU.mult),
                 reads=[r_es[ei], r_samp], writes=[r_pt[ei]])

            def fp(e, i=i, ei=ei):
                for c in range(4):
                    for kap in range(9):
                        g = 0 if kap == 0 else (1 if kap < 5 else 2)
                        t = 0 if kap == 0 else (kap - 1) % 4
                        col, ncol = (0, 32) if g == 0 else (32 * g + 8 * t, 8)
                        e.matmul(pvps[:, c * 96 + col: c * 96 + col + ncol], lhsT=vts[i][:, kap, c * 128:(c + 1) * 128],
                                 rhs=pt_[ei][:, col:col + ncol], start=True, stop=True)
                return e.matmul(pvps[:, 384:480], lhsT=onesb[:], rhs=pt_[ei][:, 0:96], start=True, stop=True)
            S.op("pe", fp, reads=[r_vts[i], r_pt[ei], r_misc], writes=[r_pv])
            for c in range(4):
                for hh in range(2):
                    pr = slice(hh * 64, hh * 64 + 64)
                    h = 2 * c + hh
                    srcA = pvps[pr, c * 96 + h: c * 96 + h + 96].rearrange("p (g t h) -> p t g h", g=3, t=4)[:, :, :, 0]
                    srcS = pvps[pr, 384 + h: 384 + h + 96].rearrange("p (g t h) -> p t g h", g=3, t=4)[:, :, :, 0]
                    S.op("dve", lambda e, pr=pr, c=c, b=b, srcA=srcA: e.tensor_reduce(
                        out=accb[pr, c, 4 * b:4 * b + 4], in_=srcA, axis=mybir.AxisListType.X, op=ALU.add),
                        reads=[r_pv], writes=[r_samp])
                    S.op("dve", lambda e, pr=pr, c=c, b=b, srcS=srcS: e.tensor_reduce(
                        out=sbb[pr, c, 4 * b:4 * b + 4], in_=srcS, axis=mybir.AxisListType.X, op=ALU.add),
                        reads=[r_pv], writes=[r_samp])
        S.op("dve", lambda e: e.tensor_tensor(out=accb[:], in0=accb[:], in1=accn[:], op=ALU.add), reads=[r_samp], writes=[r_samp])
        S.op("dve", lambda e: e.tensor_tensor(out=sbb[:], in0=sbb[:], in1=sn[:], op=ALU.add), reads=[r_samp], writes=[r_samp])
        S.op("dve", lambda e: e.reciprocal(out=sbb[:], in_=sbb[:]), reads=[r_samp], writes=[r_samp])
        for c in range(4):
            S.op("dve", lambda e, c=c: e.tensor_tensor(out=oT[:, c, 0:N], in0=accb[:, c, :], in1=sbb[:, c, :], op=ALU.mult),
                 reads=[r_samp], writes=[r_o[c]])
        for g in range(3):
            for which, buf in ((0, KTn), (1, VTn)):
                i = nxt("tf", 2)
                for c in range(4):
                    ti_ = nxt("tr", 2)
                    S.op("pe", lambda e, buf=buf, g=g, c=c, ti_=ti_: e.transpose(trps[0:N, ti_ * 512: ti_ * 512 + 128], buf[:, g, c, :], identb[:]),
                         reads=[r_kt[g][c], r_vt[g][c], r_misc], writes=[r_tr[ti_]])
                    S.op("act", lambda e, i=i, c=c, ti_=ti_: e.activation(
                        out=tmpf[i][0:N, c * 128:(c + 1) * 128], in_=trps[0:N, ti_ * 512: ti_ * 512 + 128], func=AF.Copy),
                        reads=[r_tr[ti_]], writes=[r_tmpf[i]])
                dst = (sk if which == 0 else sv)[g]
                for b in range(SB_):
                    out_ops.append(S.op("sp", lambda e, dst=dst, g=g, i=i, b=b: e.dma_start(
                        out=dst[b, WIN[g] - 4:WIN[g], :], in_=tmpf[i][4 * b:4 * b + 4, :]), reads=[r_tmpf[i]], dma=True))

    def full_tile(ti, sample):
        N = NS if sample else T
        mi = 0
        if sample:
            load_x(xsT, N)
            dump(xTs[:, 0, 0:N], [r_x[0]], "x0")
            norm_stage(N, None, None, True, A1, 0)
            dump(rstd[:, 0:N], [r_rstd], "rstd1")
            dump(xnT[:, 0, 0:N], [r_xn[0]], "xn0", True)
            dump(xnT[:, 7, 0:N], [r_xn[7]], "xn7", True)
            dump(A1[:, 0, :], [r_mod], "A1_0")
            dump(modT[:, 0, :], [r_mod], "shift1_0")
        else:
            load_x(xT[:, NH + ti * T: NH + (ti + 1) * T], N)
            norm_stage(N, A1, modT[:, 0:8, :], False)
        for half in range(2):
            wv, wr = wload("in", 15 + half)
            for j in range(4):
                cc = half * 4 + j
                b = mm_group(wv, j, 8, lambda c: xnT[:, c, 0:N], r_xn, wr, N)
                S.op("act", lambda e, b=b, cc=cc: e.activation(out=mixT[:, cc, 0:N], in_=mmps[b][:, 0:N], func=AF.Sigmoid),
                     reads=[r_mm[b]], writes=[r_mix[cc]])
        for j in range(4):
            inherit(r_h[j], [r_hid[20 + 2 * j], r_hid[21 + 2 * j]])
        for cc in range(8):
            inherit(r_by[cc], [r_hid[12 + cc]])
        conv_branch(N, sample)
        for half in range(2):
            wv, wr = wload("co", half)
            for j in range(4):
                cc = half * 4 + j
                b = mm_group(wv, j, 8, lambda c: by_ap(c)[:, 0:N], r_by, wr, N)
                S.op("dve", lambda e, b=b, cc=cc: e.tensor_tensor(out=mixT[:, cc, 0:N], in0=mmps[b][:, 0:N], in1=mixT[:, cc, 0:N], op=ALU.mult),
                     reads=[r_mm[b]], writes=[r_mix[cc]])
        for g in range(3):
            for c in range(4):
                inherit(r_q[g][c], [r_hid[g * 4 + c]] + ([r_sg[g * 4 + c]] if g * 4 + c < 8 else []))
        for g in range(3):
            wv, wr = wload("in", 6 + g)
            qk_block(wv, wr, g, 0, N, sample)
            wv, wr = wload("in", 9 + g)
            qk_block(wv, wr, g, 1, N, sample)
            wv, wr = wload("in", 12 + g)
            v_block(wv, wr, g, N, sample, False)
        if sample:
            dump(mixT[:, 0, 0:N], [r_mix[0]], "mix0_after_conv", True)
            dump(qT_ap(0, 0)[:, 0:N], [r_q[0][0]], "q00", True)
            dump(KTn[:, 0, 0, :], [r_kt[0][0]], "k00", True)
            dump(VTn[:, 0, 0, :], [r_vt[0][0]], "v00", True)
            dump(qT_ap(2, 1)[:, 0:N], [r_q[2][1]], "q21", True)
            dump(KTn[:, 2, 1, :], [r_kt[2][1]], "k21", True)
            dump(VTn[:, 2, 1, :], [r_vt[2][1]], "v21", True)
            sample_attention()
            dump(accn[:, 0, :], [r_samp], "accn0")
            dump(sn[:, 0, :], [r_samp], "sn0")
            dump(accb[:, 0, :], [r_samp], "accb0(final)")
            dump(sbb[:, 0, :], [r_samp], "rsbb0(final recip)")
            dump(oT[:, 0, 0:N], [r_o[0]], "o0", True)
            dump(oT[:, 3, 0:N], [r_o[3]], "o3", True)
        else:
            attention_tile(ti)
            shift_hist()
        for k in range(8):
            inherit(r_sg[k], [r_q[k // 4][k % 4]])
        for half in range(2):
            wv, wr = wload("in", 17 + half)
            for j in range(4):
                cc = half * 4 + j
                b = mm_group(wv, j, 8, lambda c: xnT[:, c, 0:N], r_xn, wr, N)
                S.op("act", lambda e, b=b, cc=cc: e.activation(out=sgatt_ap(cc)[:, 0:N], in_=mmps[b][:, 0:N], func=AF.Sigmoid),
                     reads=[r_mm[b]], writes=[r_sg[cc]])
        wv, wr = wload("ao", 0)
        for jj in range(8):
            b = mm_group(wv, jj, 4, lambda c: oT[:, c, 0:N], r_o, wr, N)
            i = nxt("tf", 2)
            S.op("dve", lambda e, b=b, jj=jj, i=i: e.tensor_tensor(out=tmpf[i][:, 0:N], in0=mmps[b][:, 0:N], in1=sgatt_ap(jj)[:, 0:N], op=ALU.mult),
                 reads=[r_mm[b], r_sg[jj]], writes=[r_tmpf[i]])
            S.op("pool", lambda e, jj=jj, i=i: e.tensor_tensor(out=mixT[:, jj, 0:N], in0=mixT[:, jj, 0:N], in1=tmpf[i][:, 0:N], op=ALU.add),
                 reads=[r_tmpf[i]], writes=[r_mix[jj]])
        for half in range(2):
            wv, wr = wload("o", half)
            for j in range(4):
                jj = half * 4 + j
                b = mm_group(wv, j, 8, lambda c: mixT[:, c, 0:N], r_mix, wr, N)
                if not sample:
                    S.op("dve", lambda e, b=b, jj=jj: e.scalar_tensor_tensor(out=xTs[:, jj, 0:N], in0=mmps[b][:, 0:N], scalar=modT[:, 16 + jj, 0:1],
                                                                            in1=xTs[:, jj, 0:N], op0=ALU.mult, op1=ALU.add),
                         reads=[r_mm[b], r_mod], writes=[r_x[jj]])
                else:
                    i = nxt("tf", 2)
                    S.op("dve", lambda e, b=b, jj=jj, i=i: e.tensor_tensor(out=v3(tmpf[i][:, 0:N]), in0=v3(mmps[b][:, 0:N]), in1=mb(modT, 16 + jj), op=ALU.mult),
                         reads=[r_mm[b], r_mod], writes=[r_tmpf[i]])
                    S.op("dve", lambda e, jj=jj, i=i: e.tensor_tensor(out=xTs[:, jj, 0:N], in0=xTs[:, jj, 0:N], in1=tmpf[i][:, 0:N], op=ALU.add),
                         reads=[r_tmpf[i]], writes=[r_x[jj]])
        if sample:
            dump(mixT[:, 0, 0:N], [r_mix[0]], "mix0_final", True)
            dump(xTs[:, 0, 0:N], [r_x[0]], "x1_0")
        if sample:
            norm_stage(N, None, None, True, A2, 24)
        else:
            norm_stage(N, A2, modT[:, 24:32, :], False)
        for k in range(32):
            inherit(r_hid[k], hid_alias(k))
        for blk in range(8):
            wv, wr = wload("mi", blk)
            for j in range(4):
                k = blk * 4 + j
                b = mm_group(wv, j, 8, lambda c: xnT[:, c, 0:N], r_xn, wr, N)
                i = nxt("tf", 2)
                S.op("act", lambda e, b=b, i=i: e.activation(out=tmpf[i][:, 0:N], in_=mmps[b][:, 0:N], func=AF.Relu),
                     reads=[r_mm[b]], writes=[r_tmpf[i]])
                S.op("pool", lambda e, k=k, i=i: e.tensor_tensor(out=hid_ap(k)[:, 0:N], in0=tmpf[i][:, 0:N], in1=tmpf[i][:, 0:N], op=ALU.mult),
                     reads=[r_tmpf[i]], writes=[r_hid[k]])
        for jj in range(8):
            wv, wr = wload("mo", jj)
            b = mm_group(wv, 0, 32, lambda c: hid_ap(c)[:, 0:N], r_hid, wr, N)
            if not sample:
                S.op("dve", lambda e, b=b, jj=jj: e.scalar_tensor_tensor(out=xTs[:, jj, 0:N], in0=mmps[b][:, 0:N], scalar=modT[:, 40 + jj, 0:1],
                                                                        in1=xTs[:, jj, 0:N], op0=ALU.mult, op1=ALU.add),
                     reads=[r_mm[b], r_mod], writes=[r_x[jj]])
                dst = yT[jj * 128:(jj + 1) * 128, ti * T:(ti + 1) * T]
            else:
                i = nxt("tf", 2)
                S.op("dve", lambda e, b=b, jj=jj, i=i: e.tensor_tensor(out=v3(tmpf[i][:, 0:N]), in0=v3(mmps[b][:, 0:N]), in1=mb(modT, 40 + jj), op=ALU.mult),
                     reads=[r_mm[b], r_mod], writes=[r_tmpf[i]])
                S.op("dve", lambda e, jj=jj, i=i: e.tensor_tensor(out=xTs[:, jj, 0:N], in0=xTs[:, jj, 0:N], in1=tmpf[i][:, 0:N], op=ALU.add),
                     reads=[r_tmpf[i]], writes=[r_x[jj]])
                dst = ysT[jj * 128:(jj + 1) * 128, :]
            out_ops.append(S.op("act", lambda e, jj=jj, dst=dst: e.dma_start(out=dst, in_=xTs[:, jj, 0:N]), reads=[r_x[jj]], dma=True))

    for hi in range(NHT if (STAGE >= 1 and not NOPROMPT) else 0):
        load_x(xT[:, hi * T:(hi + 1) * T], T)
        norm_stage(T, A1, modT[:, 0:8, :], False)
        groups = (0, 1, 2) if hi == NHT - 1 else (2,)
        for g in groups:
            wv, wr = wload("in", 9 + g)
            qk_block(wv, wr, g, 1, T, False)
            wv, wr = wload("in", 12 + g)
            v_block(wv, wr, g, T, False, True)
        if hi == NHT - 1:
            conv_branch(T, False, halo_only=True)
        shift_hist()
    for ti in range(0 if NOPROMPT else (NT if STAGE >= 3 else (1 if STAGE == 2 else 0))):
        full_tile(ti, False)
    out_ops.append(S.op("sp", lambda e: e.dma_start(out=pconvT[:, :, :], in_=ucarry[:]), reads=[r_uc], dma=True))
    for g in range(3):
        d = DIL[g]
        for buf, rr, dst in ((KT, r_kt, pkT[g]), (VT, r_vt, pvT[g])):
            for c in range(4):
                srcv = buf[g][:, c, :].rearrange("p (r l) -> p r l", r=d)[:, :, 0:128] if g > 0 else buf[g][:, c, 0:128].unsqueeze(1)
                out_ops.append(S.op("pool", lambda e, srcv=srcv, dst=dst, c=c: e.dma_start(out=dst[c * 128:(c + 1) * 128, :, :], in_=srcv),
                                    reads=[rr[g][c]], dma=True))
    if STAGE >= 4:
        S.strict = True
        full_tile(0, True)
        S.strict = False
    if STAGE >= 5:
        issue_copies(len(copy_jobs))
    S.op("sp", None, reads=[], writes=[])
    fin = S.q["sp"][-1]
    for o in out_ops:
        if o not in fin.deps:
            fin.deps.append(o)
            o.flag = True
    fin.fn = lambda e: e.nop()

    with nc.Block() as block:
        S.emit(nc, es, block)
    es.close()
    return nc


def _consts():
    onehot = np.zeros((32, 387), np.float32)
    for g in range(3):
        k = np.arange(129)
        bk = t5_bucket(k * DIL[g])
        onehot[bk, 129 * g + k] = 1.0
    sel = np.zeros((24, 12, 128), np.float32)
    for g in range(3):
        for c in range(4):
            for p in range(128):
                sel[8 * g + 2 * c + p // 64, g * 4 + c, p] = 1.0
    return onehot, sel


_NC_CACHE = {}


def kernel(x_prompt, x_sample, c_prompt, c_sample, state_conv, cache_k1, cache_v1, cache_k2, cache_v2,
           cache_k3, cache_v3, rel_bias, norm1_g, norm2_g, w_ada, b_ada, w_in, conv_w, q_norm_g, k_norm_g,
           w_conv_out, w_attn_out, w_o, w_mlp_in, w_mlp_out):
    f32 = np.float32
    A = lambda a: np.ascontiguousarray(np.asarray(a, dtype=f32))
    x_prompt, x_sample = A(x_prompt), A(x_sample)
    c_prompt, c_sample, state_conv = A(c_prompt), A(c_sample), A(state_conv)
    caches_k = [A(cache_k1)[0], A(cache_k2)[0], A(cache_k3)[0]]
    caches_v = [A(cache_v1)[0], A(cache_v2)[0], A(cache_v3)[0]]
    if "nc" not in _NC_CACHE:
        _NC_CACHE["nc"] = build_program()
    nc = _NC_CACHE["nc"]
    onehot, sel = _consts()

    def fm(v, n):
        return np.ascontiguousarray(A(v).reshape(n, 128).T)

    shared = {
        "b_adaT": fm(b_ada[0], 48), "n1g": fm(norm1_g[0], 8), "n2g": fm(norm2_g[0], 8),
        "convw": np.ascontiguousarray(A(conv_w[0]).reshape(3, 8, 128).transpose(2, 1, 0)),
        "qkg": np.ascontiguousarray(np.stack([np.tile(A(q_norm_g[0]), 2), np.tile(A(k_norm_g[0]), 2)], axis=1)),
        "onehot": onehot, "rel_bias": A(rel_bias), "ident": np.eye(128, dtype=f32), "sel": sel,
        "w_ada": A(w_ada[0]), "w_in": A(w_in[0]), "w_conv_out": A(w_conv_out[0]), "w_attn_out": A(w_attn_out[0]),
        "w_o": A(w_o[0]), "w_mlp_in": A(w_mlp_in[0]), "w_mlp_out": A(w_mlp_out[0]),
    }
    in_maps = []
    for core in range(NCORE):
        b, half = divmod(core, 2)
        m = dict(shared)
        xt = np.zeros((D, NH + NTOK), f32)
        xt[:, NH:] = x_prompt[b, half * NTOK:(half + 1) * NTOK].T
        if half == 1:
            xt[:, :NH] = x_prompt[b, NTOK - NH:NTOK].T
        m["xT"] = xt
        sl = slice(core * SB_, (core + 1) * SB_)
        m["xsT"] = np.ascontiguousarray(x_sample[sl].reshape(NS, D).T)
        crow = np.concatenate([c_prompt[b:b + 1], c_sample[sl]], axis=0)
        m["cT"] = np.ascontiguousarray(crow.reshape(17, 8, 128).transpose(2, 1, 0))
        m["stT"] = np.ascontiguousarray(state_conv[0, sl].reshape(SB_, 2, 8, 128).transpose(3, 2, 0, 1))
        m["flagv"] = np.full((128, 1), float(half), f32)
        vo = np.ones((128, 5, 64), f32)
        for ti in range(4):
            vo[:128 - 32 * ti, ti, :] = float(half)
        m["vones"] = vo
        for g in range(3):
            m["ck%d" % g] = np.ascontiguousarray(caches_k[g][sl].reshape(SB_, WIN[g], 512))
            m["cv%d" % g] = np.ascontiguousarray(caches_v[g][sl].reshape(SB_, WIN[g], 512))
            kk = m["ck%d" % g]
            d = DIL[g]
            nt = 1 if g == 0 else 4
            kr = kk.reshape(SB_, 128, d, 512)[:, :, 0:nt, :]
            m["kc%dT" % g] = np.ascontiguousarray(kr.transpose(0, 3, 2, 1))
        in_maps.append(m)

    res = run_bass_kernel_spmd(nc, in_maps, core_ids=list(range(NCORE)))
    R_ = res.results
    yp = np.empty((4, 8192, D), f32)
    ys = np.empty((128, 4, D), f32)
    p_conv = np.empty((1, 4, 2, D), f32)
    pk = [np.empty((1, 4, WIN[g], 8, 64), f32) for g in range(3)]
    pv = [np.empty((1, 4, WIN[g], 8, 64), f32) for g in range(3)]
    s_conv = np.empty((1, 128, 2, D), f32)
    skk = [np.empty((1, 128, WIN[g], 8, 64), f32) for g in range(3)]
    svv = [np.empty((1, 128, WIN[g], 8, 64), f32) for g in range(3)]
    for core in range(NCORE):
        b, half = divmod(core, 2)
        r = R_[core]
        yp[b, half * NTOK:(half + 1) * NTOK] = r["yT"].T
        sl = slice(core * SB_, (core + 1) * SB_)
        ys[sl] = r["ysT"].T.reshape(SB_, 4, D)
        s_conv[0, sl] = r["sconvT"].transpose(2, 3, 1, 0).reshape(SB_, 2, D)
        for g in range(3):
            skk[g][0, sl] = r["sk%d" % g].reshape(SB_, WIN[g], 8, 64)
            svv[g][0, sl] = r["sv%d" % g].reshape(SB_, WIN[g], 8, 64)
        if half == 1:
            p_conv[0, b] = r["pconvT"].transpose(2, 1, 0).reshape(2, D)
            for g in range(3):
                pk[g][0, b] = r["pk%dT" % g].transpose(2, 1, 0).reshape(WIN[g], 8, 64)
                pv[g][0, b] = r["pv%dT" % g].transpose(2, 1, 0).reshape(WIN[g], 8, 64)
    return (yp, ys, p_conv, pk[0], pv[0], pk[1], pv[1], pk[2], pv[2],
            s_conv, skk[0], svv[0], skk[1], svv[1], skk[2], svv[2])
```

```python
import math
from contextlib import ExitStack
import numpy as np
import concourse.bass as bass
import concourse.mybir as mybir
from concourse.ap import AP
from concourse.bass_utils import run_bass_kernel_spmd

F32, BF16 = mybir.dt.float32, mybir.dt.bfloat16
AF = mybir.ActivationFunctionType
ALU = mybir.AluOpType

NCORE = 8
D = 1024
T = 512
NH = 2048
NTOK = 4096
NT = NTOK // T
NHT = NH // T
SB_ = 16
NS = 64
DPROJ = 9728
EPS = 1e-6
WIN = (128, 512, 2048)
DIL = (1, 4, 16)


class Res:
    __slots__ = ("name", "w", "r", "const")

    def __init__(self, name, const=False):
        self.name, self.w, self.r, self.const = name, None, [], const


class Op:
    __slots__ = ("eng", "fn", "deps", "flag", "rank", "dma", "sem", "semval")


ENGS = ["pe", "act", "dve", "pool", "sp"]


class Sched:
    def __init__(self):
        self.q = {e: [] for e in ENGS}
        self.strict = False

    def op(self, eng, fn, reads=(), writes=(), dma=False):
        o = Op()
        o.eng, o.fn, o.dma, o.flag, o.deps, o.rank, o.sem, o.semval = eng, fn, dma, False, [], 0, None, 0
        dl = []
        for R in reads:
            if R.w is not None:
                dl.append((R.w, True))
        for R in writes:
            if R.w is not None:
                dl.append((R.w, False))
            dl.extend((x, False) for x in R.r)
        for d, raw in dl:
            if d is o:
                continue
            if d.eng == eng and not d.dma and not dma:
                if not (self.strict and raw and eng != "pe"):
                    continue
            if d not in o.deps:
                o.deps.append(d)
                d.flag = True
        for R in reads:
            if not R.const:
                R.r.append(o)
        for R in writes:
            R.w = o
            R.r = []
        self.q[eng].append(o)
        return o

    def emit(self, nc, es, block):
        NDS = 12
        csem = {e: es.enter_context(nc.semaphore("c_" + e)) for e in ENGS}
        dsem = {e: [es.enter_context(nc.semaphore("d_%s%d" % (e, i))) for i in range(NDS)] for e in ENGS}
        for e in ENGS:
            k = 0
            kd = 0
            for o in self.q[e]:
                if o.dma:
                    o.flag = True
                    o.sem = dsem[e][kd % NDS]
                    o.semval = 16 * (kd // NDS + 1)
                    kd += 1
                elif o.flag:
                    k += 1
                    o.rank = k

        def run(e, eng):
            waited = {}
            for o in self.q[e]:
                for d in o.deps:
                    if d.dma:
                        key, sem, val = id(d.sem), d.sem, d.semval
                    else:
                        key, sem, val = d.eng, csem[d.eng], d.rank
                    if waited.get(key, 0) >= val:
                        continue
                    waited[key] = val
                    eng.wait_ge(sem, val)
                if o.fn is None:
                    continue
                ins = o.fn(eng)
                if o.flag:
                    if o.dma:
                        ins.then_inc(o.sem, 16)
                    else:
                        ins.then_inc(csem[e], 1)

        block.tensor(lambda eng: run("pe", eng))
        block.scalar(lambda eng: run("act", eng))
        block.vector(lambda eng: run("dve", eng))
        block.gpsimd(lambda eng: run("pool", eng))
        block.sync(lambda eng: run("sp", eng))


def t5_bucket(dist):
    dist = np.asarray(dist, np.int64)
    max_exact = 16
    ratio = np.maximum(dist, 1).astype(np.float32) / np.float32(max_exact)
    large = max_exact + (np.log(ratio) / np.float32(math.log(2048 / max_exact)) * np.float32(32 - max_exact)).astype(np.int32)
    large = np.minimum(large, 31)
    return np.where(dist < max_exact, dist, large)


def build_program():
    import os
    STAGE = int(os.environ.get("KSTAGE", "9"))
    NOPROMPT = int(os.environ.get("KNOPROMPT", "0"))
    out_ops = []
    nc = bass.Bass("TRN2", target_bir_lowering=False)
    S = Sched()
    es = ExitStack()

    def din(name, shape):
        return nc.dram_tensor(name, list(shape), F32, kind="ExternalInput").ap()

    def dout(name, shape):
        return nc.dram_tensor(name, list(shape), F32, kind="ExternalOutput").ap()

    def dtmp(name, shape, dt):
        return nc.dram_tensor(name, list(shape), dt, kind="Internal").ap()

    def sb(name, shape, dt):
        return es.enter_context(nc.sbuf_tensor(name, list(shape), dt))

    def psum(name, shape, dt=F32):
        return es.enter_context(nc.psum_tensor(name, list(shape), dt))

    xT = din("xT", [D, NH + NTOK])
    xsT = din("xsT", [D, NS])
    cT = din("cT", [128, 8, 17])
    stT = din("stT", [128, 8, SB_, 2])
    b_adaT = din("b_adaT", [128, 48])
    n1g_d = din("n1g", [128, 8])
    n2g_d = din("n2g", [128, 8])
    convw_d = din("convw", [128, 8, 3])
    qkg_d = din("qkg", [128, 2])
    flag_d = din("flagv", [128, 1])
    vones_d = din("vones", [128, 5, 64])
    onehot_d = din("onehot", [32, 387])
    relb_d = din("rel_bias", [32, 24])
    ident_d = din("ident", [128, 128])
    sel_d = din("sel", [24, 12, 128])
    w_ada = din("w_ada", [D, 6144])
    w_in = din("w_in", [D, DPROJ])
    w_co = din("w_conv_out", [D, D])
    w_ao = din("w_attn_out", [512, D])
    w_o = din("w_o", [D, D])
    w_mi = din("w_mlp_in", [D, 4096])
    w_mo = din("w_mlp_out", [4096, D])
    ck = [din("ck%d" % g, [SB_, WIN[g], 512]) for g in range(3)]
    cv = [din("cv%d" % g, [SB_, WIN[g], 512]) for g in range(3)]
    kcT = [din("kc0T", [SB_, 512, 1, 128]), din("kc1T", [SB_, 512, 4, 128]), din("kc2T", [SB_, 512, 4, 128])]

    yT = dout("yT", [D, NTOK])
    ysT = dout("ysT", [D, NS])
    pconvT = dout("pconvT", [128, 8, 2])
    sconvT = dout("sconvT", [128, 8, SB_, 2])
    pkT = [dout("pk%dT" % g, [512, DIL[g], 128]) for g in range(3)]
    pvT = [dout("pv%dT" % g, [512, DIL[g], 128]) for g in range(3)]
    sk = [dout("sk%d" % g, [SB_, WIN[g], 512]) for g in range(3)]
    sv = [dout("sv%d" % g, [SB_, WIN[g], 512]) for g in range(3)]

    DBG = int(os.environ.get("KDBG", "0"))
    dbg = dout("dbg", [128, 32, 64]) if DBG else None
    dbg_n = [0]

    def dump(ap, res, name, cast=False):
        if not DBG:
            return
        k = dbg_n[0]
        dbg_n[0] += 1
        n = ap.shape[1]
        print("DBGSLOT", k, name, n)
        out_ops.append(S.op("pool" if cast else "sp", lambda e: e.dma_start(out=dbg[0:ap.shape[0], k, 0:n], in_=ap), reads=list(res), dma=True))

    wb_in = dtmp("wb_in", [D, DPROJ], BF16)
    wb_co = dtmp("wb_co", [D, D], BF16)
    wb_ao = dtmp("wb_ao", [512, D], BF16)
    wb_o = dtmp("wb_o", [D, D], BF16)
    wb_mi = dtmp("wb_mi", [D, 4096], BF16)
    wb_mo = dtmp("wb_mo", [4096, D], BF16)
    ebz = dtmp("ebz", [24, 128 * 512], F32)

    identb = sb("identb", [128, 128], BF16)
    onesb = sb("onesb", [128, 128], BF16)
    blk64 = sb("blk64", [128, 128], BF16)
    vones = sb("vones_s", [128, 5, 64], BF16)
    flagv = sb("flag_s", [128, 1], F32)
    epsv = sb("epsv", [128, 2], F32)
    Eall = sb("Eall", [128, 24, 256], BF16)
    modT = sb("modT", [128, 48, 17], F32)
    A1 = sb("A1", [128, 8, 17], F32)
    A2 = sb("A2", [128, 8, 17], F32)
    n1g = sb("n1g_s", [128, 8], F32)
    n2g = sb("n2g_s", [128, 8], F32)
    convw = sb("convw_s", [128, 8, 3], F32)
    qkg = sb("qkg_s", [128, 2], F32)
    badat = sb("badat", [128, 48], F32)
    ebn = sb("ebn", [128, 12, 4], F32)
    ucarry = sb("ucarry", [128, 8, 2], F32)
    KTW = (640, 1024, 2560)
    KT = [sb("KT%d" % g, [128, 4, KTW[g]], BF16) for g in range(3)]
    VT = [sb("VT%d" % g, [128, 4, KTW[g]], BF16) for g in range(3)]
    xTs = sb("xTs", [128, 8, T], F32)
    xnT = sb("xnT", [128, 8, T], BF16)
    sqb = [sb("sqb%d" % i, [128, T], BF16) for i in range(2)]
    rstd = sb("rstd", [128, T], F32)
    tmpf = [sb("tmpf%d" % i, [128, T], F32) for i in range(2)]
    A1a = sb("arenaA1", [128, 12 * T], BF16)
    A2a = sb("arenaA2", [128, 8 * T], BF16)
    A3a = sb("arenaA3", [128, 4 * T], F32)
    A4a = sb("arenaA4", [128, 4 * T], BF16)
    ue = [sb("ue0", [128, T + 2], F32)]
    mixT = sb("mixT", [128, 8, T], BF16)
    es_ = [sb("es%d" % i, [128, 512], F32) for i in range(2)]
    pt_ = [sb("pt%d" % i, [128, 512], BF16) for i in range(2)]
    vt_ = [sb("vt%d" % i, [128, 128], BF16) for i in range(4)]
    AS = [sb("AS0", [128, 2, T], F32)]
    oT = A2a[:, 0:4 * T].rearrange("p (c t) -> p c t", c=4)
    NWB = 3
    wring = [sb("wring%d" % i, [128, 4096], BF16) for i in range(NWB)]
    A1f = A1a[:].bitcast(F32)
    A2f = A2a[:].bitcast(F32)
    selS = A1f[0:24, 0:1536].rearrange("p (a b) -> p a b", a=12)
    oneh = A1f[0:32, 1536:1923]
    ebs = A1f[0:24, 1924:2311]
    zt = A1f[0:24, 2312:2824]
    cTs = A2f[:, 0:136].rearrange("p (a b) -> p a b", a=8)
    relb = A2f[0:32, 136:160]
    csil = A2a[:, 400:536].rearrange("p (a b) -> p a b", a=8)
    ues = sb("ues", [128, SB_, 6], F32)
    Es = sb("Es", [128, 96], BF16)
    KTn = sb("KTn", [128, 3, 4, NS], BF16)
    VTn = sb("VTn", [128, 3, 4, NS], BF16)
    pn = sb("pn", [128, NS], F32)
    prodb = sb("prodb", [128, NS], BF16)
    ASf = AS[0][:].rearrange("p a t -> p (a t)")
    accn = ASf[:, 0:256].rearrange("p (c n) -> p c n", c=4)
    sn = ASf[:, 256:512].rearrange("p (c n) -> p c n", c=4)
    accb = ASf[:, 512:768].rearrange("p (c n) -> p c n", c=4)
    sbb = ASf[:, 768:1024].rearrange("p (c n) -> p c n", c=4)
    QbdA = KT[1][:].rearrange("p c w -> p (c w)")
    QbdB = VT[1][:].rearrange("p c w -> p (c w)")

    def Qbd_ap(g):
        base = QbdA[:, g * 2048:(g + 1) * 2048] if g < 2 else QbdB[:, 0:2048]
        return base.rearrange("p (c n h) -> p c n h", c=4, h=8)

    big = psum("bigps", [128, 8, 512])
    mmps = [big[:, i, :] for i in range(3)]
    nsps = big[:, 3, :]
    stp = big[:, 4:6, :]
    stps = [stp[:, 0, :], stp[:, 0, :]]
    pvps = big[:, 6, :]
    trps = big[:, 7, :].bitcast(BF16)
    st_sets = [big[:, 4:6, :], big[:, 0:2, :]]
    pv_sets = [big[:, 6, :], big[:, 2, :]]
    tr_sets = [big[:, 7, :].bitcast(BF16), big[:, 3, :].bitcast(BF16)]

    def R(n, const=False):
        return Res(n, const)

    r_const = R("const", True)
    r_mm = [R("mm%d" % i) for i in range(3)]
    r_ns = R("ns")
    r_st0 = R("st")
    r_st = [r_st0, r_st0]
    r_pv = R("pv")
    r_tr0 = R("tr")
    r_tr = [r_tr0, r_tr0]
    r_w = [R("w%d" % i) for i in range(NWB)]
    r_x = [R("x%d" % c) for c in range(8)]
    r_xn = [R("xn%d" % c) for c in range(8)]
    r_sq = [R("sq0"), R("sq1")]
    r_rstd = R("rstd")
    r_tmpf = [R("tf0"), R("tf1")]
    r_q = [[R("q%d_%d" % (g, c)) for c in range(4)] for g in range(3)]
    r_kt = [[R("kt%d_%d" % (g, c)) for c in range(4)] for g in range(3)]
    r_vt = [[R("vt%d_%d" % (g, c)) for c in range(4)] for g in range(3)]
    r_sg = [R("sg%d" % c) for c in range(8)]
    r_by = [R("by%d" % c) for c in range(8)]
    r_h = [R("h%d" % c) for c in range(4)]
    r_hid = [R("hid%d" % c) for c in range(32)]
    r_ue = [R("ue0")]
    r_mix = [R("mix%d" % c) for c in range(8)]
    r_es = [R("es0"), R("es1")]
    r_pt = [R("pt0"), R("pt1")]
    r_vtile = [R("vtile%d" % i) for i in range(4)]
    r_as = [R("as0")]
    r_o = [R("o%d" % c) for c in range(4)]
    r_uc = R("ucarry")
    r_mod = R("mod")
    r_E = R("E")
    r_misc = R("misc")
    r_wb = {k: R("wb_" + k) for k in ("in", "co", "ao", "o", "mi", "mo")}
    r_ebz = R("ebz")
    r_samp = R("samp")
    r_qbd = R("qbd")
    r_init = R("init")
    cnt = {"mm": 0, "st": 0, "w": 0, "sq": 0, "tf": 0, "es": 0, "vt": 0, "tr": 0, "as": 0, "ue": 0, "bal": 0}

    def nxt(key, n):
        v = cnt[key] % n
        cnt[key] += 1
        return v

    def qT_ap(g, c):
        return A1a[:, (g * 4 + c) * T:(g * 4 + c + 1) * T]

    def sgatt_ap(c):
        return A1a[:, c * T:(c + 1) * T]

    def by_ap(c):
        return A2a[:, c * T:(c + 1) * T]

    def h_ap(c):
        return A3a[:, c * T:(c + 1) * T]

    def hid_ap(k):
        if k < 12:
            return A1a[:, k * T:(k + 1) * T]
        if k < 20:
            return A2a[:, (k - 12) * T:(k - 11) * T]
        if k < 28:
            kk = k - 20
            return A3a[:, (kk // 2) * T:(kk // 2 + 1) * T].bitcast(BF16)[:, (kk % 2) * T:(kk % 2 + 1) * T]
        return A4a[:, (k - 28) * T:(k - 27) * T]

    def hid_alias(k):
        if k < 12:
            g, c = divmod(k, 4)
            return [r_q[g][c]] + ([r_sg[k]] if k < 8 else [])
        if k < 20:
            return [r_by[k - 12]] + ([r_o[k - 12]] if k < 16 else [])
        if k < 28:
            return [r_h[(k - 20) // 2]]
        return []

    def inherit(newR, oldRs):
        for o in oldRs:
            if o.w is not None:
                newR.r.append(o.w)
            newR.r.extend(o.r)

    def cast_flat(dst, src, nelem, rres, tag):
        rows = nelem // 2048
        d2 = dst.rearrange("a b -> (a b)").rearrange("(r c) -> r c", c=2048)
        s2 = src.rearrange("a b -> (a b)").rearrange("(r c) -> r c", c=2048)
        step = 512
        for r0 in range(0, rows, step):
            r1 = min(rows, r0 + step)
            S.op("pool", lambda e, r0=r0, r1=r1: e.dma_start(out=d2[r0:r1, :], in_=s2[r0:r1, :]),
                 writes=[rres], dma=True)

    def ld(eng, dst, src, wres, rres=()):
        return S.op(eng, lambda e: e.dma_start(out=dst, in_=src), reads=list(rres), writes=list(wres), dma=True)

    ld("sp", n1g[:], n1g_d[:, :], [r_misc])
    ld("sp", n2g[:], n2g_d[:, :], [r_misc])
    ld("sp", convw[:], convw_d[:, :, :], [r_misc])
    ld("sp", qkg[:], qkg_d[:, :], [r_misc])
    ld("sp", flagv[:], flag_d[:, :], [r_misc])
    ld("sp", badat[:], b_adaT[:, :], [r_misc])
    ld("pool", identb[:], ident_d[:, :], [r_misc])
    ld("pool", vones[:], vones_d[:, :, :], [r_misc])
    S.op("pool", lambda e: e.memset(onesb[:], 1.0), writes=[r_misc])
    S.op("pool", lambda e: e.memset(epsv[:, 0:1], float(D * EPS)), writes=[r_misc])
    S.op("pool", lambda e: e.memset(epsv[:, 1:2], float(64 * EPS)), writes=[r_misc])
    S.op("pool", lambda e: e.memset(blk64[:], 0.0), writes=[r_misc])
    S.op("pool", lambda e: e.memset(blk64[0:64, 0:64], 1.0), writes=[r_misc])
    S.op("pool", lambda e: e.memset(blk64[64:128, 64:128], 1.0), writes=[r_misc])
    S.op("pool", lambda e: e.memset(zt[:], 0.0), writes=[r_misc])
    S.op("pool", lambda e: e.memset(ucarry[:], 0.0), writes=[r_uc])
    S.op("dve", lambda e: e.tensor_scalar(out=qkg[:], in0=qkg[:], scalar1=8.0, scalar2=None, op0=ALU.mult),
         reads=[r_misc], writes=[r_misc])

    cast_flat(wb_in, w_in, D * DPROJ, r_wb["in"], "in")

    ld("sp", relb, relb_d[:, :], [r_misc])
    ld("sp", oneh[:], onehot_d[:, :], [r_misc])
    ld("sp", selS[:], sel_d[:, :, :], [r_misc])
    S.op("pe", lambda e: e.matmul(mmps[0][0:24, 0:387], lhsT=relb, rhs=oneh[:], start=True, stop=True),
         reads=[r_misc], writes=[r_mm[0]])
    S.op("act", lambda e: e.activation(out=ebs[:], in_=mmps[0][0:24, 0:387], func=AF.Exp),
         reads=[r_mm[0]], writes=[r_E])
    for g in range(3):
        S.op("sp", lambda e, g=g: e.dma_start(out=zt[8 * g:8 * g + 8, 128:257],
                                             in_=ebs[8 * g:8 * g + 8, 129 * g:129 * g + 129]),
             reads=[r_E, r_misc], writes=[r_init], dma=True)
    S.op("sp", lambda e: e.dma_start(out=ebz[:, :].rearrange("a (j c) -> a j c", c=512),
                                     in_=zt.unsqueeze(1).to_broadcast([24, 128, 512])),
         reads=[r_init], writes=[r_ebz], dma=True)
    for g in range(3):
        src = AP(ebz.tensor, 8 * g * 65536 + 128, [[511, 128], [65536, 8], [1, 256]])
        S.op("pool", lambda e, g=g, src=src: e.dma_start(out=Eall[:, 8 * g:8 * g + 8, :], in_=src),
             reads=[r_ebz], writes=[r_E], dma=True)
    for g in range(3):
        for c in range(4):
            S.op("pe", lambda e, g=g, c=c: e.matmul(nsps[:, (g * 4 + c) * 4:(g * 4 + c) * 4 + 4],
                                                     lhsT=selS[:, g * 4 + c, :], rhs=ebs[:, 129 * g:129 * g + 4],
                                                     start=True, stop=True),
                 reads=[r_E, r_misc], writes=[r_ns])
    S.op("act", lambda e: e.activation(out=ebn[:].rearrange("p a b -> p (a b)"), in_=nsps[:, 0:48], func=AF.Copy),
         reads=[r_ns], writes=[r_samp])

    ld("sp", cTs[:], cT[:, :, :], [r_misc])
    S.op("act", lambda e: e.activation(out=csil[:], in_=cTs[:], func=AF.Silu), reads=[r_misc], writes=[r_mod])
    for blk in range(12):
        slot = nxt("w", NWB)
        wt = wring[slot][:].rearrange("p (c f) -> p c f", c=8)
        src = w_ada[:, blk * 512:(blk + 1) * 512].rearrange("(c p) f -> p c f", p=128)
        S.op("pool", lambda e, wt=wt, src=src: e.dma_start(out=wt, in_=src), writes=[r_w[slot]], dma=True)
        for j in range(4):
            k = blk * 4 + j
            b = nxt("mm", 3)

            def f(e, wt=wt, j=j, b=b):
                for c in range(8):
                    ins = e.matmul(mmps[b][:, 0:17], lhsT=wt[:, c, j * 128:(j + 1) * 128], rhs=csil[:, c, :],
                                   start=(c == 0), stop=(c == 7))
                return ins
            S.op("pe", f, reads=[r_w[slot], r_mod], writes=[r_mm[b]])
            S.op("act", lambda e, k=k, b=b: e.activation(out=modT[:, k, :], in_=mmps[b][:, 0:17], func=AF.Identity,
                                                        bias=badat[:, k:k + 1], scale=1.0),
                 reads=[r_mm[b], r_misc], writes=[r_mod])
    for (Ax, gx, off) in ((A1, n1g, 8), (A2, n2g, 32)):
        S.op("dve", lambda e, Ax=Ax, off=off: e.tensor_scalar(out=Ax[:], in0=modT[:, off:off + 8, :], scalar1=1.0,
                                                             scalar2=32.0, op0=ALU.add, op1=ALU.mult),
             reads=[r_mod], writes=[r_mod])
        S.op("dve", lambda e, Ax=Ax, gx=gx: e.tensor_tensor(out=Ax[:], in0=Ax[:],
                                                           in1=gx[:].unsqueeze(2).to_broadcast([128, 8, 17]),
                                                           op=ALU.mult),
             reads=[r_mod, r_misc], writes=[r_mod])
    cast_flat(wb_co, w_co, D * D, r_wb["co"], "co")
    cast_flat(wb_ao, w_ao, 512 * D, r_wb["ao"], "ao")
    cast_flat(wb_o, w_o, D * D, r_wb["o"], "o")
    cast_flat(wb_mi, w_mi, D * 4096, r_wb["mi"], "mi")
    cast_flat(wb_mo, w_mo, 4096 * D, r_wb["mo"], "mo")

    copy_jobs = []
    for g in (2, 1, 0):
        for b in range(SB_):
            for (dst, src) in ((sk[g], ck[g]), (sv[g], cv[g])):
                copy_jobs.append((dst[b, 0:WIN[g] - 4, :], src[b, 4:WIN[g], :]))

    def issue_copies(n):
        for _ in range(n):
            if copy_jobs:
                d_, s_ = copy_jobs.pop(0)
                out_ops.append(S.op("sp", lambda e, d_=d_, s_=s_: e.dma_start(out=d_, in_=s_), dma=True))

    def wload(kind, idx):
        slot = nxt("w", NWB)
        if kind == "in":
            src = wb_in[:, idx * 512:(idx + 1) * 512].rearrange("(c p) f -> p c f", p=128)
            view = wring[slot][:].rearrange("p (c f) -> p c f", c=8)
        elif kind in ("co", "o"):
            wsrc = wb_co if kind == "co" else wb_o
            src = wsrc[:, idx * 512:(idx + 1) * 512].rearrange("(c p) f -> p c f", p=128)
            view = wring[slot][:].rearrange("p (c f) -> p c f", c=8)
        elif kind == "ao":
            src = wb_ao[:, :].rearrange("(c p) f -> p c f", p=128)
            view = wring[slot][:].rearrange("p (c f) -> p c f", c=4)
        elif kind == "mi":
            src = wb_mi[:, idx * 512:(idx + 1) * 512].rearrange("(c p) f -> p c f", p=128)
            view = wring[slot][:].rearrange("p (c f) -> p c f", c=8)
        else:
            src = wb_mo[:, idx * 128:(idx + 1) * 128].rearrange("(c p) f -> p c f", p=128)
            view = wring[slot][:].rearrange("p (c f) -> p c f", c=32)
        S.op("sp", lambda e, view=view, src=src: e.dma_start(out=view, in_=src),
             reads=[r_wb[kind]], writes=[r_w[slot]], dma=True)
        if STAGE >= 5:
            issue_copies(1 if cnt["w"] % 4 == 0 else 0)
        return view, r_w[slot]

    def mm_group(wview, j, nk, rhs_fn, rhs_res, wres, N):
        b = nxt("mm", 3)

        def f(e):
            for c in range(nk):
                ins = e.matmul(mmps[b][:, 0:N], lhsT=wview[:, c, j * 128:(j + 1) * 128], rhs=rhs_fn(c),
                               start=(c == 0), stop=(c == nk - 1))
            return ins
        S.op("pe", f, reads=[wres] + list(rhs_res), writes=[r_mm[b]])
        return b

    def v3(ap):
        return ap.rearrange("p (b t) -> p b t", t=4)

    def mb(tile, k):
        return tile[:, k, 1:17].unsqueeze(2).to_broadcast([128, SB_, 4])

    def norm_stage(N, A_sc, shift_sc, sample, Atok=None, shtok=None):
        for c in range(8):
            i = nxt("sq", 2)
            S.op("act", lambda e, c=c, i=i: e.activation(out=sqb[i][:, 0:N], in_=xTs[:, c, 0:N], func=AF.Square),
                 reads=[r_x[c]], writes=[r_sq[i]])
            S.op("pe", lambda e, c=c, i=i: e.matmul(nsps[:, 0:N], lhsT=onesb[:], rhs=sqb[i][:, 0:N],
                                                     start=(c == 0), stop=(c == 7)),
                 reads=[r_sq[i], r_misc], writes=[r_ns] if c == 0 else [r_ns])
        S.op("act", lambda e: e.activation(out=rstd[:, 0:N], in_=nsps[:, 0:N], func=AF.Ln, bias=epsv[:, 0:1], scale=1.0),
             reads=[r_ns, r_misc], writes=[r_rstd])
        S.op("act", lambda e: e.activation(out=rstd[:, 0:N], in_=rstd[:, 0:N], func=AF.Exp, scale=-0.5), reads=[r_rstd], writes=[r_rstd])
        for c in range(8):
            i = nxt("tf", 2)
            if not sample:
                S.op("dve", lambda e, c=c, i=i: e.scalar_tensor_tensor(out=tmpf[i][:, 0:N], in0=xTs[:, c, 0:N],
                                                                       scalar=A_sc[:, c, 0:1], in1=rstd[:, 0:N],
                                                                       op0=ALU.mult, op1=ALU.mult),
                     reads=[r_x[c], r_rstd, r_mod], writes=[r_tmpf[i]])
                S.op("act", lambda e, c=c, i=i: e.activation(out=xnT[:, c, 0:N], in_=tmpf[i][:, 0:N], func=AF.Identity,
                                                             bias=shift_sc[:, c, 0:1], scale=1.0),
                     reads=[r_tmpf[i], r_mod], writes=[r_xn[c]])
            else:
                S.op("dve", lambda e, c=c, i=i: e.tensor_tensor(out=tmpf[i][:, 0:N], in0=xTs[:, c, 0:N],
                                                                in1=rstd[:, 0:N], op=ALU.mult),
                     reads=[r_x[c], r_rstd], writes=[r_tmpf[i]])
                S.op("dve", lambda e, c=c, i=i: e.tensor_tensor(out=v3(tmpf[i][:, 0:N]), in0=v3(tmpf[i][:, 0:N]),
                                                                in1=mb(Atok, c), op=ALU.mult),
                     reads=[r_mod], writes=[r_tmpf[i]])
                S.op("dve", lambda e, c=c, i=i: e.tensor_tensor(out=v3(xnT[:, c, 0:N]), in0=v3(tmpf[i][:, 0:N]),
                                                                in1=mb(modT, shtok + c), op=ALU.add),
                     reads=[r_tmpf[i], r_mod], writes=[r_xn[c]])

    def kv_dst(buf, g, c, N, sample):
        if sample:
            return (KTn if buf is KT else VTn)[:, g, c, :], None
        if g == 0:
            return buf[0][:, c, 128:128 + T], None
        d = DIL[g]
        L = T // d
        hist = 128
        v = buf[g][:, c, :].rearrange("p (r l) -> p r l", r=d)[:, :, hist:hist + L]
        return v.rearrange("p r l -> p l r"), (L, d)

    def src_view(ap2d, perm):
        if perm is None:
            return ap2d
        L, d = perm
        return ap2d.rearrange("p (l r) -> p l r", r=d)

    def qk_block(wv, wres, g, which, N, sample, qdst=None):
        for j in range(4):
            b = mm_group(wv, j, 8, lambda c: xnT[:, c, 0:N], r_xn, wres, N)
            i = nxt("sq", 2)
            S.op("act", lambda e, b=b, i=i: e.activation(out=sqb[i][:, 0:N], in_=mmps[b][:, 0:N], func=AF.Square),
                 reads=[r_mm[b]], writes=[r_sq[i]])
            S.op("pe", lambda e, i=i: e.matmul(nsps[:, 0:N], lhsT=blk64[:], rhs=sqb[i][:, 0:N], start=True, stop=True),
                 reads=[r_sq[i], r_misc], writes=[r_ns])
            S.op("act", lambda e: e.activation(out=rstd[:, 0:N], in_=nsps[:, 0:N], func=AF.Ln, bias=epsv[:, 1:2], scale=1.0),
                 reads=[r_ns, r_misc], writes=[r_rstd])
            S.op("act", lambda e: e.activation(out=rstd[:, 0:N], in_=rstd[:, 0:N], func=AF.Exp, scale=-0.5), reads=[r_rstd], writes=[r_rstd])
            if which == 0:
                if sample:
                    dst, perm = qT_ap(g, j)[:, 0:N], None
                else:
                    if g == 0:
                        dst, perm = qT_ap(g, j), None
                    else:
                        d = DIL[g]
                        dst = qT_ap(g, j).rearrange("p (r l) -> p l r", r=d)
                        perm = (T // d, d)
                wr = [r_q[g][j]]
            else:
                dst, perm = kv_dst(KT, g, j, N, sample)
                wr = [r_kt[g][j]]
            S.op("dve", lambda e, b=b, dst=dst, perm=perm, which=which: e.scalar_tensor_tensor(
                out=dst, in0=src_view(mmps[b][:, 0:N], perm), scalar=qkg[:, which:which + 1],
                in1=src_view(rstd[:, 0:N], perm), op0=ALU.mult, op1=ALU.mult),
                reads=[r_mm[b], r_rstd, r_misc], writes=wr)

    def v_block(wv, wres, g, N, sample, halo):
        for j in range(4):
            b = mm_group(wv, j, 8, lambda c: xnT[:, c, 0:N], r_xn, wres, N)
            dst, perm = kv_dst(VT, g, j, N, sample)
            if halo:
                S.op("act", lambda e, b=b, dst=dst, perm=perm: e.activation(out=dst, in_=src_view(mmps[b][:, 0:N], perm),
                                                                            func=AF.Identity, scale=flagv[:, 0:1]),
                     reads=[r_mm[b], r_misc], writes=[r_vt[g][j]])
            else:
                S.op("act", lambda e, b=b, dst=dst, perm=perm: e.activation(out=dst, in_=src_view(mmps[b][:, 0:N], perm),
                                                                            func=AF.Copy),
                     reads=[r_mm[b]], writes=[r_vt[g][j]])

    def shift_hist():
        for g in range(3):
            for buf, rr in ((KT, r_kt), (VT, r_vt)):
                for c in range(4):
                    if g == 0:
                        S.op("pool", lambda e, buf=buf, c=c: e.tensor_copy(out=buf[0][:, c, 0:128], in_=buf[0][:, c, T:T + 128]),
                             reads=[rr[0][c]], writes=[rr[0][c]])
                    else:
                        d = DIL[g]
                        L = T // d
                        v = buf[g][:, c, :].rearrange("p (r l) -> p r l", r=d)
                        S.op("pool", lambda e, v=v, L=L: e.tensor_copy(out=v[:, :, 0:128], in_=v[:, :, L:L + 128]),
                             reads=[rr[g][c]], writes=[rr[g][c]])

    def load_x(src_ap, N):
        for c in range(8):
            S.op("act", lambda e, c=c: e.dma_start(out=xTs[:, c, 0:N], in_=src_ap[c * 128:(c + 1) * 128, :]),
                 writes=[r_x[c]], dma=True)

    def conv_branch(N, sample, halo_only=False):
        for half in range(2):
            wv_h, wr_h = wload("in", 0 + half)
            hb = []
            for j in range(4):
                if halo_only:
                    b = mm_group(wv_h, j, 8, lambda c: xnT[:, c, T - 2:T], r_xn, wr_h, 2)
                    S.op("act", lambda e, b=b, j=j: e.activation(out=h_ap(j)[:, 0:2], in_=mmps[b][:, 0:2], func=AF.Copy),
                         reads=[r_mm[b]], writes=[r_h[j]])
                else:
                    b = mm_group(wv_h, j, 8, lambda c: xnT[:, c, 0:N], r_xn, wr_h, N)
                    S.op("act", lambda e, b=b, j=j: e.activation(out=h_ap(j)[:, 0:N], in_=mmps[b][:, 0:N], func=AF.Copy),
                         reads=[r_mm[b]], writes=[r_h[j]])
            wv_c, wr_c = wload("in", 4 + half)
            for j in range(4):
                cc = half * 4 + j
                if halo_only:
                    b = mm_group(wv_c, j, 8, lambda c: xnT[:, c, T - 2:T], r_xn, wr_c, 2)
                    S.op("dve", lambda e, b=b, j=j, cc=cc: e.scalar_tensor_tensor(
                        out=ucarry[:, cc, :], in0=mmps[b][:, 0:2], scalar=flagv[:, 0:1], in1=h_ap(j)[:, 0:2],
                        op0=ALU.mult, op1=ALU.mult), reads=[r_mm[b], r_h[j], r_misc], writes=[r_uc])
                    continue
                b = mm_group(wv_c, j, 8, lambda c: xnT[:, c, 0:N], r_xn, wr_c, N)
                if not sample:
                    u = nxt("ue", 1)
                    S.op("pool", lambda e, u=u, cc=cc: e.tensor_copy(out=ue[u][:, 0:2], in_=ucarry[:, cc, :]),
                         reads=[r_uc], writes=[r_ue[u]])
                    S.op("dve", lambda e, b=b, j=j, u=u: e.tensor_tensor(out=ue[u][:, 2:2 + N], in0=mmps[b][:, 0:N],
                                                                         in1=h_ap(j)[:, 0:N], op=ALU.mult),
                         reads=[r_mm[b], r_h[j]], writes=[r_ue[u]])
                    S.op("pool", lambda e, u=u, cc=cc: e.tensor_copy(out=ucarry[:, cc, :], in_=ue[u][:, N:N + 2]),
                         reads=[r_ue[u]], writes=[r_uc])
                    y = h_ap(j)[:, 0:N]
                    S.op("dve", lambda e, u=u, cc=cc, y=y: e.tensor_scalar(out=y, in0=ue[u][:, 0:N], scalar1=convw[:, cc, 0:1],
                                                                            scalar2=None, op0=ALU.mult),
                         reads=[r_ue[u], r_misc], writes=[r_h[j]])
                    for tap in (1, 2):
                        S.op("dve", lambda e, u=u, cc=cc, y=y, tap=tap: e.scalar_tensor_tensor(
                            out=y, in0=ue[u][:, tap:tap + N], scalar=convw[:, cc, tap:tap + 1], in1=y,
                            op0=ALU.mult, op1=ALU.add), reads=[r_ue[u], r_misc], writes=[r_h[j]])
                else:
                    S.op("sp", lambda e, cc=cc: e.dma_start(out=ues[:, :, 0:2], in_=stT[:, cc, :, :]),
                         writes=[r_ue[0]], dma=True)
                    S.op("dve", lambda e, b=b, j=j: e.tensor_tensor(
                        out=ues[:, :, 2:6], in0=mmps[b][:, 0:N].rearrange("p (b t) -> p b t", t=4),
                        in1=h_ap(j)[:, 0:N].rearrange("p (b t) -> p b t", t=4), op=ALU.mult),
                        reads=[r_mm[b], r_h[j]], writes=[r_ue[0]])
                    out_ops.append(S.op("sp", lambda e, cc=cc: e.dma_start(out=sconvT[:, cc, :, :], in_=ues[:, :, 4:6]),
                                        reads=[r_ue[0]], dma=True))
                    y = h_ap(j)[:, 0:N].rearrange("p (b t) -> p b t", t=4)
                    S.op("pool", lambda e, cc=cc, y=y: e.tensor_scalar(out=y, in0=ues[:, :, 0:4], scalar1=convw[:, cc, 0:1],
                                                                       scalar2=None, op0=ALU.mult),
                         reads=[r_ue[0], r_misc], writes=[r_h[j]])
                    for tap in (1, 2):
                        S.op("dve", lambda e, cc=cc, y=y, tap=tap: e.scalar_tensor_tensor(
                            out=y, in0=ues[:, :, tap:tap + 4], scalar=convw[:, cc, tap:tap + 1], in1=y,
                            op0=ALU.mult, op1=ALU.add), reads=[r_ue[0], r_misc], writes=[r_h[j]])
            if halo_only:
                continue
            wv_b, wr_b = wload("in", 2 + half)
            for j in range(4):
                cc = half * 4 + j
                b = mm_group(wv_b, j, 8, lambda c: xnT[:, c, 0:N], r_xn, wr_b, N)
                S.op("dve", lambda e, b=b, j=j, cc=cc: e.tensor_tensor(out=by_ap(cc)[:, 0:N], in0=mmps[b][:, 0:N],
                                                                      in1=h_ap(j)[:, 0:N], op=ALU.mult),
                     reads=[r_mm[b], r_h[j]], writes=[r_by[cc]])

    def evac_vt(src_ps, dst, rsrc, rdst, npart):
        if nxt("bal", 2) == 0:
            S.op("act", lambda e: e.activation(out=dst, in_=src_ps, func=AF.Copy), reads=[rsrc], writes=[rdst])
        else:
            S.op("dve", lambda e: e.tensor_copy(out=dst, in_=src_ps), reads=[rsrc], writes=[rdst])

    def attn_stages(idx, g, c, qap, kchunks, vchunks, nq, as_i, as_view, first, ones_list):
        sset = idx % 2
        stt = st_sets[sset]
        pvt = pv_sets[sset]
        trt = tr_sets[sset]
        rs_st = [r_st0] if sset == 0 else [r_mm[0], r_mm[1]]
        rs_pv = [r_pv] if sset == 0 else [r_mm[2]]
        rs_tr = [r_tr0] if sset == 0 else [r_ns]
        W = 2 * nq
        ei = sset
        nk0 = kchunks[0][1]
        stv = stt[:, :, 0:W].rearrange("p h (k q) -> p h k q", k=2)
        esv = es_[ei][:, 0:2 * W].rearrange("p (h k q) -> p h k q", h=2, k=2)
        ptv = pt_[ei][:, 0:2 * W].rearrange("p (h k q) -> p h k q", h=2, k=2)
        gh = 8 * g + 2 * c
        vts = [(2 * sset + ci, nk) for ci, (vap, nk) in enumerate(vchunks)]

        def stageA():
            def fs(e):
                for hh in range(2):
                    pr = slice(hh * 64, hh * 64 + 64)
                    for ci, (kap, nk) in enumerate(kchunks):
                        ins = e.matmul(stt[0:nk, hh, ci * nq: ci * nq + nq], lhsT=kap[pr, :], rhs=qap[pr, :],
                                       start=True, stop=True)
                return ins
            S.op("pe", fs, reads=[r_q[g][c], r_kt[g][c]], writes=rs_st)
            for ci, (vap, nk) in enumerate(vchunks):
                S.op("pe", lambda e, vap=vap, nk=nk, ci=ci: e.transpose(trt[0:nk, ci * 512: ci * 512 + 128], vap, identb[:]),
                     reads=[r_vt[g][c], r_misc], writes=rs_tr)

        def stageB():
            if nk0 == 128:
                S.op("act", lambda e: e.activation(out=es_[ei][:, 0:2 * W].rearrange("p (h w) -> p h w", h=2), in_=stt[:, :, 0:W],
                                                   func=AF.Exp, scale=0.125),
                     reads=rs_st, writes=[r_es[ei]])
                Ev = Eall[:, gh:gh + 2, :].rearrange("p h (k q) -> p h k q", k=2)[:, :, :, 0:nq]
                S.op("dve", lambda e: e.tensor_tensor(out=ptv, in0=esv, in1=Ev, op=ALU.mult),
                     reads=[r_es[ei], r_E], writes=[r_pt[ei]])
            else:
                S.op("act", lambda e: e.activation(out=esv[:, :, 1, :], in_=stv[:, :, 1, :], func=AF.Exp, scale=0.125),
                     reads=rs_st, writes=[r_es[ei]])
                S.op("act", lambda e: e.activation(out=esv[0:nk0, :, 0, :], in_=stv[0:nk0, :, 0, :], func=AF.Exp, scale=0.125),
                     reads=rs_st, writes=[r_es[ei]])
                S.op("dve", lambda e: e.tensor_tensor(out=ptv[:, :, 1, :], in0=esv[:, :, 1, :],
                                                      in1=Eall[:, gh:gh + 2, 128:128 + nq], op=ALU.mult),
                     reads=[r_es[ei], r_E], writes=[r_pt[ei]])
                S.op("dve", lambda e: e.tensor_tensor(out=ptv[0:nk0, :, 0, :], in0=esv[0:nk0, :, 0, :],
                                                      in1=Eall[0:nk0, gh:gh + 2, 0:nq], op=ALU.mult),
                     reads=[r_es[ei], r_E], writes=[r_pt[ei]])
            use_act = (idx % 2 == 0)
            for ci, (vap, nk) in enumerate(vchunks):
                vi = vts[ci][0]
                src_ps = trt[0:nk, ci * 512: ci * 512 + 128]
                dstv = vt_[vi][0:nk, :]
                if use_act:
                    S.op("act", lambda e, src_ps=src_ps, dstv=dstv: e.activation(out=dstv, in_=src_ps, func=AF.Copy),
                         reads=rs_tr, writes=[r_vtile[vi]])
                else:
                    S.op("dve", lambda e, src_ps=src_ps, dstv=dstv: e.tensor_copy(out=dstv, in_=src_ps),
                         reads=rs_tr, writes=[r_vtile[vi]])

        def stageC():
            def fpv(e):
                for hh in range(2):
                    pr = slice(hh * 64, hh * 64 + 64)
                    for ci, (vi, nk) in enumerate(vts):
                        e.matmul(pvt[pr, 0:nq], lhsT=vt_[vi][0:nk, pr], rhs=ptv[0:nk, hh, ci, :],
                                 start=(ci == 0), stop=(ci == 1))
                    for ci, (vi, nk) in enumerate(vts):
                        ins = e.matmul(pvt[pr, 128:128 + nq], lhsT=ones_list[ci][0:nk, :], rhs=ptv[0:nk, hh, ci, :],
                                       start=(ci == 0), stop=(ci == 1))
                return ins
            S.op("pe", fpv, reads=[r_pt[ei], r_misc] + [r_vtile[vi] for vi, _ in vts], writes=rs_pv)

        def stageD():
            pvv = pvt[:, 0:256].rearrange("p (a q) -> p a q", a=2)[:, :, 0:nq]
            if first:
                S.op("act", lambda e: e.activation(out=as_view, in_=pvv, func=AF.Copy), reads=rs_pv, writes=[r_as[as_i]])
            else:
                S.op("dve", lambda e: e.tensor_tensor(out=as_view, in0=pvv, in1=as_view, op=ALU.add),
                     reads=rs_pv, writes=[r_as[as_i]])
        return stageA, stageB, stageC, stageD

    def attention_tile(ti):
        blocks = []
        for c in range(4):
            ai = 0
            asb = AS[ai]
            for blk in range(4):
                q = qT_ap(0, c)[:, blk * 128:(blk + 1) * 128]
                kc = KT[0][:, c, 128 + blk * 128: 256 + blk * 128]
                kp = KT[0][:, c, blk * 128: 128 + blk * 128]
                vc = VT[0][:, c, 128 + blk * 128: 256 + blk * 128]
                vp = VT[0][:, c, blk * 128: 128 + blk * 128]
                halo = (ti == 0 and blk == 0)
                blocks.append((0, c, q, [(kc, 128), (kp, 128)], [(vc, 128), (vp, 128)], 128, ai,
                               asb[:, :, blk * 128:(blk + 1) * 128], True,
                               [vones[:, 4, :], vones[:, 0, :] if halo else vones[:, 4, :]]))
            for r in range(4):
                q = qT_ap(1, c)[:, r * 128:(r + 1) * 128]
                kv = KT[1][:, c, :].rearrange("p (r l) -> p r l", r=4)
                vv = VT[1][:, c, :].rearrange("p (r l) -> p r l", r=4)
                halo = (ti == 0)
                blocks.append((1, c, q, [(kv[:, r, 128:256], 128), (kv[:, r, 0:128], 128)],
                               [(vv[:, r, 128:256], 128), (vv[:, r, 0:128], 128)], 128, ai,
                               asb[:].rearrange("p a (l r) -> p a l r", r=4)[:, :, :, r], False,
                               [vones[:, 4, :], vones[:, 0, :] if halo else vones[:, 4, :]]))
            for r in range(16):
                q = qT_ap(2, c)[:, r * 32:(r + 1) * 32]
                kv = KT[2][:, c, :].rearrange("p (r l) -> p r l", r=16)
                vv = VT[2][:, c, :].rearrange("p (r l) -> p r l", r=16)
                blocks.append((2, c, q, [(kv[:, r, 128:160], 32), (kv[:, r, 0:128], 128)],
                               [(vv[:, r, 128:160], 32), (vv[:, r, 0:128], 128)], 32, ai,
                               asb[:].rearrange("p a (l r) -> p a l r", r=16)[:, :, :, r], False,
                               [vones[:, 4, :], vones[:, min(ti, 4), :]]))
        stages = [attn_stages(i, *blk) for i, blk in enumerate(blocks)]
        nb = len(stages)
        stages[0][0]()
        for i in range(nb):
            if i + 1 < nb:
                stages[i + 1][0]()
            stages[i][1]()
            stages[i][2]()
            stages[i][3]()
            if i % 24 == 23:
                c = i // 24
                asb = AS[0]
                S.op("dve", lambda e, asb=asb: e.reciprocal(out=asb[:, 1, :], in_=asb[:, 1, :]), reads=[r_as[0]], writes=[r_as[0]])
                S.op("dve", lambda e, asb=asb, c=c: e.tensor_tensor(out=oT[:, c, :], in0=asb[:, 0, :], in1=asb[:, 1, :], op=ALU.mult),
                     reads=[r_as[0]], writes=[r_o[c]])

    def sample_attention():
        N = NS
        for c in range(4):
            inherit(r_qbd, [r_kt[1][c], r_vt[1][c]])
        inherit(r_samp, [r_as[0]])
        S.op("pool", lambda e: e.memset(QbdA[:, 0:4096], 0.0), writes=[r_qbd])
        S.op("pool", lambda e: e.memset(QbdB[:, 0:2048], 0.0), writes=[r_qbd])
        for g in range(3):
            for c in range(4):
                for hh in range(2):
                    pr = slice(hh * 64, hh * 64 + 64)
                    S.op("pool", lambda e, g=g, c=c, hh=hh, pr=pr: e.tensor_copy(out=Qbd_ap(g)[pr, c, :, 2 * c + hh],
                                                                                 in_=qT_ap(g, c)[pr, 0:N]),
                         reads=[r_q[g][c]], writes=[r_qbd])
        S.op("pool", lambda e: e.tensor_copy(out=Es[:, 0:32].rearrange("p (t h) -> p t h", h=8),
                                             in_=Eall[:, 0:8, 128:132].rearrange("p h t -> p t h")),
             reads=[r_E], writes=[r_samp])
        for g in (1, 2):
            S.op("pool", lambda e, g=g: e.tensor_copy(out=Es[:, 32 * g:32 * g + 32].rearrange("p (t h) -> p t h", h=8),
                                                      in_=Eall[:, 8 * g:8 * g + 8, 128:129].rearrange("p h t -> p t h").to_broadcast([128, 4, 8])),
                 reads=[r_E], writes=[r_samp])
        S.op("pool", lambda e: e.memset(accn[:], 0.0), writes=[r_samp])
        S.op("pool", lambda e: e.memset(sn[:], 0.0), writes=[r_samp])
        for g in range(3):
            for c in range(4):
                qv = qT_ap(g, c)[:, 0:N].rearrange("p (b t) -> p b t", t=4)
                kvw = KTn[:, g, c, :].rearrange("p (b t) -> p b t", t=4)
                vvw = VTn[:, g, c, :].rearrange("p (b t) -> p b t", t=4)
                for dl in range(4 if g == 0 else 1):
                    n = 4 - dl
                    pv3 = prodb[:, 0:SB_ * n].rearrange("p (b t) -> p b t", t=n)
                    S.op("dve", lambda e, qv=qv, kvw=kvw, dl=dl, n=n, pv3=pv3: e.tensor_tensor(
                        out=pv3, in0=qv[:, :, dl:4], in1=kvw[:, :, 0:n], op=ALU.mult),
                        reads=[r_q[g][c], r_kt[g][c]], writes=[r_samp])
                    S.op("pe", lambda e, n=n: e.matmul(nsps[:, 0:SB_ * n], lhsT=blk64[:], rhs=prodb[:, 0:SB_ * n], start=True, stop=True),
                         reads=[r_samp, r_misc], writes=[r_ns])
                    S.op("act", lambda e, n=n: e.activation(out=pn[:, 0:SB_ * n], in_=nsps[:, 0:SB_ * n], func=AF.Exp, scale=0.125),
                         reads=[r_ns], writes=[r_samp])
                    pn3 = pn[:, 0:SB_ * n].rearrange("p (b t) -> p b t", t=n)
                    S.op("dve", lambda e, g=g, c=c, dl=dl, pn3=pn3: e.tensor_scalar(
                        out=pn3, in0=pn3, scalar1=ebn[:, g * 4 + c, dl:dl + 1], scalar2=None, op0=ALU.mult),
                        reads=[r_samp], writes=[r_samp])
                    snv = sn[:, c, :].rearrange("p (b t) -> p b t", t=4)[:, :, dl:4]
                    acv = accn[:, c, :].rearrange("p (b t) -> p b t", t=4)[:, :, dl:4]
                    S.op("dve", lambda e, snv=snv, pn3=pn3: e.tensor_tensor(out=snv, in0=snv, in1=pn3, op=ALU.add),
                         reads=[r_samp], writes=[r_samp])
                    S.op("dve", lambda e, pn3=pn3, vvw=vvw, n=n: e.tensor_tensor(out=pn3, in0=pn3, in1=vvw[:, :, 0:n], op=ALU.mult),
                         reads=[r_samp, r_vt[g][c]], writes=[r_samp])
                    S.op("dve", lambda e, acv=acv, pn3=pn3: e.tensor_tensor(out=acv, in0=acv, in1=pn3, op=ALU.add),
                         reads=[r_samp], writes=[r_samp])
        K2f = KT[2][:].rearrange("p c w -> p (c w)")
        V2f = VT[2][:].rearrange("p c w -> p (c w)")
        kts = [K2f[:, i * 4608:(i + 1) * 4608].rearrange("p (c k j) -> p c k j", c=4, k=9) for i in range(2)]
        vts = [V2f[:, i * 4608:(i + 1) * 4608].rearrange("p (k f) -> p k f", k=9) for i in range(2)]
        r_kts = [R("kts0"), R("kts1")]
        r_vts = [R("vts0"), R("vts1")]
        for i in range(2):
            for c in range(4):
                inherit(r_kts[i], [r_kt[2][c]])
                inherit(r_vts[i], [r_vt[2][c]])
        for b in range(SB_):
            i = b % 2
            for g in range(3):
                nt = 1 if g == 0 else 4
                k0 = 0 if g == 0 else (1 if g == 1 else 5)
                S.op("pool", lambda e, g=g, b=b, i=i, k0=k0, nt=nt: e.dma_start(
                    out=kts[i][:, :, k0:k0 + nt, :], in_=kcT[g][b].rearrange("(c p) t j -> p c t j", p=128)),
                    writes=[r_kts[i]], dma=True)
                d = DIL[g]
                S.op("pool", lambda e, g=g, b=b, i=i, k0=k0, nt=nt, d=d: e.dma_start(
                    out=vts[i][:, k0:k0 + nt, :], in_=cv[g][b].rearrange("(j t) f -> j t f", t=d)[:, 0:nt, :]),
                    writes=[r_vts[i]], dma=True)
            st = nxt("st", 2)

            def fs(e, b=b, i=i, st=st):
                for kap in range(9):
                    g = 0 if kap == 0 else (1 if kap < 5 else 2)
                    t = 0 if kap == 0 else (kap - 1) % 4
                    if g == 0:
                        col, ncol = 0, 32
                    else:
                        col, ncol = 32 * g + 8 * t, 8
                    for c in range(4):
                        if g == 0:
                            rhs = Qbd_ap(0)[:, c, 4 * b:4 * b + 4, :]
                        else:
                            rhs = Qbd_ap(g)[:, c, 4 * b + t, :]
                        ins = e.matmul(stps[st][:, col:col + ncol], lhsT=kts[i][:, c, kap, :], rhs=rhs,
                                       start=(c == 0), stop=(c == 3))
                return ins
            S.op("pe", fs, reads=[r_kts[i], r_qbd], writes=[r_st[st]])
            ei = nxt("es", 2)
            S.op("act", lambda e, st=st, ei=ei: e.activation(out=es_[ei][:, 0:96], in_=stps[st][:, 0:96], func=AF.Exp, scale=0.125),
                 reads=[r_st[st]], writes=[r_es[ei]])
            S.op("dve", lambda e, ei=ei: e.tensor_tensor(out=pt_[ei][:, 0:96], in0=es_[ei][:, 0:96], in1=Es[:], op=ALU.mult),
                 reads=[r_es[ei], r_samp], writes=[r_pt[ei]])

            def fp(e, i=i, ei=ei):
                for c in range(4):
                    for kap in range(9):
                        g = 0 if kap == 0 else (1 if kap < 5 else 2)
                        t = 0 if kap == 0 else (kap - 1) % 4
                        col, ncol = (0, 32) if g == 0 else (32 * g + 8 * t, 8)
                        e.matmul(pvps[:, c * 96 + col: c * 96 + col + ncol], lhsT=vts[i][:, kap, c * 128:(c + 1) * 128],
                                 rhs=pt_[ei][:, col:col + ncol], start=True, stop=True)
                return e.matmul(pvps[:, 384:480], lhsT=onesb[:], rhs=pt_[ei][:, 0:96], start=True, stop=True)
            S.op("pe", fp, reads=[r_vts[i], r_pt[ei], r_misc], writes=[r_pv])
            for c in range(4):
                for hh in range(2):
                    pr = slice(hh * 64, hh * 64 + 64)
                    h = 2 * c + hh
                    srcA = pvps[pr, c * 96 + h: c * 96 + h + 96].rearrange("p (g t h) -> p t g h", g=3, t=4)[:, :, :, 0]
                    srcS = pvps[pr, 384 + h: 384 + h + 96].rearrange("p (g t h) -> p t g h", g=3, t=4)[:, :, :, 0]
                    S.op("dve", lambda e, pr=pr, c=c, b=b, srcA=srcA: e.tensor_reduce(
                        out=accb[pr, c, 4 * b:4 * b + 4], in_=srcA, axis=mybir.AxisListType.X, op=ALU.add),
                        reads=[r_pv], writes=[r_samp])
                    S.op("dve", lambda e, pr=pr, c=c, b=b, srcS=srcS: e.tensor_reduce(
                        out=sbb[pr, c, 4 * b:4 * b + 4], in_=srcS, axis=mybir.AxisListType.X, op=ALU.add),
                        reads=[r_pv], writes=[r_samp])
        S.op("dve", lambda e: e.tensor_tensor(out=accb[:], in0=accb[:], in1=accn[:], op=ALU.add), reads=[r_samp], writes=[r_samp])
        S.op("dve", lambda e: e.tensor_tensor(out=sbb[:], in0=sbb[:], in1=sn[:], op=ALU.add), reads=[r_samp], writes=[r_samp])
        S.op("dve", lambda e: e.reciprocal(out=sbb[:], in_=sbb[:]), reads=[r_samp], writes=[r_samp])
        for c in range(4):
            S.op("dve", lambda e, c=c: e.tensor_tensor(out=oT[:, c, 0:N], in0=accb[:, c, :], in1=sbb[:, c, :], op=ALU.mult),
                 reads=[r_samp], writes=[r_o[c]])
        for g in range(3):
            for which, buf in ((0, KTn), (1, VTn)):
                i = nxt("tf", 2)
                for c in range(4):
                    ti_ = nxt("tr", 2)
                    S.op("pe", lambda e, buf=buf, g=g, c=c, ti_=ti_: e.transpose(trps[0:N, ti_ * 512: ti_ * 512 + 128], buf[:, g, c, :], identb[:]),
                         reads=[r_kt[g][c], r_vt[g][c], r_misc], writes=[r_tr[ti_]])
                    S.op("act", lambda e, i=i, c=c, ti_=ti_: e.activation(
                        out=tmpf[i][0:N, c * 128:(c + 1) * 128], in_=trps[0:N, ti_ * 512: ti_ * 512 + 128], func=AF.Copy),
                        reads=[r_tr[ti_]], writes=[r_tmpf[i]])
                dst = (sk if which == 0 else sv)[g]
                for b in range(SB_):
                    out_ops.append(S.op("sp", lambda e, dst=dst, g=g, i=i, b=b: e.dma_start(
                        out=dst[b, WIN[g] - 4:WIN[g], :], in_=tmpf[i][4 * b:4 * b + 4, :]), reads=[r_tmpf[i]], dma=True))

    def full_tile(ti, sample):
        N = NS if sample else T
        mi = 0
        if sample:
            load_x(xsT, N)
            dump(xTs[:, 0, 0:N], [r_x[0]], "x0")
            norm_stage(N, None, None, True, A1, 0)
            dump(rstd[:, 0:N], [r_rstd], "rstd1")
            dump(xnT[:, 0, 0:N], [r_xn[0]], "xn0", True)
            dump(xnT[:, 7, 0:N], [r_xn[7]], "xn7", True)
            dump(A1[:, 0, :], [r_mod], "A1_0")
            dump(modT[:, 0, :], [r_mod], "shift1_0")
        else:
            load_x(xT[:, NH + ti * T: NH + (ti + 1) * T], N)
            norm_stage(N, A1, modT[:, 0:8, :], False)
        for half in range(2):
            wv, wr = wload("in", 15 + half)
            for j in range(4):
                cc = half * 4 + j
                b = mm_group(wv, j, 8, lambda c: xnT[:, c, 0:N], r_xn, wr, N)
                S.op("act", lambda e, b=b, cc=cc: e.activation(out=mixT[:, cc, 0:N], in_=mmps[b][:, 0:N], func=AF.Sigmoid),
                     reads=[r_mm[b]], writes=[r_mix[cc]])
        for j in range(4):
            inherit(r_h[j], [r_hid[20 + 2 * j], r_hid[21 + 2 * j]])
        for cc in range(8):
            inherit(r_by[cc], [r_hid[12 + cc]])
        conv_branch(N, sample)
        for half in range(2):
            wv, wr = wload("co", half)
            for j in range(4):
                cc = half * 4 + j
                b = mm_group(wv, j, 8, lambda c: by_ap(c)[:, 0:N], r_by, wr, N)
                S.op("dve", lambda e, b=b, cc=cc: e.tensor_tensor(out=mixT[:, cc, 0:N], in0=mmps[b][:, 0:N], in1=mixT[:, cc, 0:N], op=ALU.mult),
                     reads=[r_mm[b]], writes=[r_mix[cc]])
        for g in range(3):
            for c in range(4):
                inherit(r_q[g][c], [r_hid[g * 4 + c]] + ([r_sg[g * 4 + c]] if g * 4 + c < 8 else []))
        for g in range(3):
            wv, wr = wload("in", 6 + g)
            qk_block(wv, wr, g, 0, N, sample)
            wv, wr = wload("in", 9 + g)
            qk_block(wv, wr, g, 1, N, sample)
            wv, wr = wload("in", 12 + g)
            v_block(wv, wr, g, N, sample, False)
        if sample:
            dump(mixT[:, 0, 0:N], [r_mix[0]], "mix0_after_conv", True)
            dump(qT_ap(0, 0)[:, 0:N], [r_q[0][0]], "q00", True)
            dump(KTn[:, 0, 0, :], [r_kt[0][0]], "k00", True)
            dump(VTn[:, 0, 0, :], [r_vt[0][0]], "v00", True)
            dump(qT_ap(2, 1)[:, 0:N], [r_q[2][1]], "q21", True)
            dump(KTn[:, 2, 1, :], [r_kt[2][1]], "k21", True)
            dump(VTn[:, 2, 1, :], [r_vt[2][1]], "v21", True)
            sample_attention()
            dump(accn[:, 0, :], [r_samp], "accn0")
            dump(sn[:, 0, :], [r_samp], "sn0")
            dump(accb[:, 0, :], [r_samp], "accb0(final)")
            dump(sbb[:, 0, :], [r_samp], "rsbb0(final recip)")
            dump(oT[:, 0, 0:N], [r_o[0]], "o0", True)
            dump(oT[:, 3, 0:N], [r_o[3]], "o3", True)
        else:
            attention_tile(ti)
            shift_hist()
        for k in range(8):
            inherit(r_sg[k], [r_q[k // 4][k % 4]])
        for half in range(2):
            wv, wr = wload("in", 17 + half)
            for j in range(4):
                cc = half * 4 + j
                b = mm_group(wv, j, 8, lambda c: xnT[:, c, 0:N], r_xn, wr, N)
                S.op("act", lambda e, b=b, cc=cc: e.activation(out=sgatt_ap(cc)[:, 0:N], in_=mmps[b][:, 0:N], func=AF.Sigmoid),
                     reads=[r_mm[b]], writes=[r_sg[cc]])
        wv, wr = wload("ao", 0)
        for jj in range(8):
            b = mm_group(wv, jj, 4, lambda c: oT[:, c, 0:N], r_o, wr, N)
            i = nxt("tf", 2)
            S.op("dve", lambda e, b=b, jj=jj, i=i: e.tensor_tensor(out=tmpf[i][:, 0:N], in0=mmps[b][:, 0:N], in1=sgatt_ap(jj)[:, 0:N], op=ALU.mult),
                 reads=[r_mm[b], r_sg[jj]], writes=[r_tmpf[i]])
            S.op("pool", lambda e, jj=jj, i=i: e.tensor_tensor(out=mixT[:, jj, 0:N], in0=mixT[:, jj, 0:N], in1=tmpf[i][:, 0:N], op=ALU.add),
                 reads=[r_tmpf[i]], writes=[r_mix[jj]])
        for half in range(2):
            wv, wr = wload("o", half)
            for j in range(4):
                jj = half * 4 + j
                b = mm_group(wv, j, 8, lambda c: mixT[:, c, 0:N], r_mix, wr, N)
                if not sample:
                    S.op("dve", lambda e, b=b, jj=jj: e.scalar_tensor_tensor(out=xTs[:, jj, 0:N], in0=mmps[b][:, 0:N], scalar=modT[:, 16 + jj, 0:1],
                                                                            in1=xTs[:, jj, 0:N], op0=ALU.mult, op1=ALU.add),
                         reads=[r_mm[b], r_mod], writes=[r_x[jj]])
                else:
                    i = nxt("tf", 2)
                    S.op("dve", lambda e, b=b, jj=jj, i=i: e.tensor_tensor(out=v3(tmpf[i][:, 0:N]), in0=v3(mmps[b][:, 0:N]), in1=mb(modT, 16 + jj), op=ALU.mult),
                         reads=[r_mm[b], r_mod], writes=[r_tmpf[i]])
                    S.op("dve", lambda e, jj=jj, i=i: e.tensor_tensor(out=xTs[:, jj, 0:N], in0=xTs[:, jj, 0:N], in1=tmpf[i][:, 0:N], op=ALU.add),
                         reads=[r_tmpf[i]], writes=[r_x[jj]])
        if sample:
            dump(mixT[:, 0, 0:N], [r_mix[0]], "mix0_final", True)
            dump(xTs[:, 0, 0:N], [r_x[0]], "x1_0")
        if sample:
            norm_stage(N, None, None, True, A2, 24)
        else:
            norm_stage(N, A2, modT[:, 24:32, :], False)
        for k in range(32):
            inherit(r_hid[k], hid_alias(k))
        for blk in range(8):
            wv, wr = wload("mi", blk)
            for j in range(4):
                k = blk * 4 + j
                b = mm_group(wv, j, 8, lambda c: xnT[:, c, 0:N], r_xn, wr, N)
                i = nxt("tf", 2)
                S.op("act", lambda e, b=b, i=i: e.activation(out=tmpf[i][:, 0:N], in_=mmps[b][:, 0:N], func=AF.Relu),
                     reads=[r_mm[b]], writes=[r_tmpf[i]])
                S.op("pool", lambda e, k=k, i=i: e.tensor_tensor(out=hid_ap(k)[:, 0:N], in0=tmpf[i][:, 0:N], in1=tmpf[i][:, 0:N], op=ALU.mult),
                     reads=[r_tmpf[i]], writes=[r_hid[k]])
        for jj in range(8):
            wv, wr = wload("mo", jj)
            b = mm_group(wv, 0, 32, lambda c: hid_ap(c)[:, 0:N], r_hid, wr, N)
            if not sample:
                S.op("dve", lambda e, b=b, jj=jj: e.scalar_tensor_tensor(out=xTs[:, jj, 0:N], in0=mmps[b][:, 0:N], scalar=modT[:, 40 + jj, 0:1],
                                                                        in1=xTs[:, jj, 0:N], op0=ALU.mult, op1=ALU.add),
                     reads=[r_mm[b], r_mod], writes=[r_x[jj]])
                dst = yT[jj * 128:(jj + 1) * 128, ti * T:(ti + 1) * T]
            else:
                i = nxt("tf", 2)
                S.op("dve", lambda e, b=b, jj=jj, i=i: e.tensor_tensor(out=v3(tmpf[i][:, 0:N]), in0=v3(mmps[b][:, 0:N]), in1=mb(modT, 40 + jj), op=ALU.mult),
                     reads=[r_mm[b], r_mod], writes=[r_tmpf[i]])
                S.op("dve", lambda e, jj=jj, i=i: e.tensor_tensor(out=xTs[:, jj, 0:N], in0=xTs[:, jj, 0:N], in1=tmpf[i][:, 0:N], op=ALU.add),
                     reads=[r_tmpf[i]], writes=[r_x[jj]])
                dst = ysT[jj * 128:(jj + 1) * 128, :]
            out_ops.append(S.op("act", lambda e, jj=jj, dst=dst: e.dma_start(out=dst, in_=xTs[:, jj, 0:N]), reads=[r_x[jj]], dma=True))

    for hi in range(NHT if (STAGE >= 1 and not NOPROMPT) else 0):
        load_x(xT[:, hi * T:(hi + 1) * T], T)
        norm_stage(T, A1, modT[:, 0:8, :], False)
        groups = (0, 1, 2) if hi == NHT - 1 else (2,)
        for g in groups:
            wv, wr = wload("in", 9 + g)
            qk_block(wv, wr, g, 1, T, False)
            wv, wr = wload("in", 12 + g)
            v_block(wv, wr, g, T, False, True)
        if hi == NHT - 1:
            conv_branch(T, False, halo_only=True)
        shift_hist()
    for ti in range(0 if NOPROMPT else (NT if STAGE >= 3 else (1 if STAGE == 2 else 0))):
        full_tile(ti, False)
    out_ops.append(S.op("sp", lambda e: e.dma_start(out=pconvT[:, :, :], in_=ucarry[:]), reads=[r_uc], dma=True))
    for g in range(3):
        d = DIL[g]
        for buf, rr, dst in ((KT, r_kt, pkT[g]), (VT, r_vt, pvT[g])):
            for c in range(4):
                srcv = buf[g][:, c, :].rearrange("p (r l) -> p r l", r=d)[:, :, 0:128] if g > 0 else buf[g][:, c, 0:128].unsqueeze(1)
                out_ops.append(S.op("pool", lambda e, srcv=srcv, dst=dst, c=c: e.dma_start(out=dst[c * 128:(c + 1) * 128, :, :], in_=srcv),
                                    reads=[rr[g][c]], dma=True))
    if STAGE >= 4:
        S.strict = True
        full_tile(0, True)
        S.strict = False
    if STAGE >= 5:
        issue_copies(len(copy_jobs))
    S.op("sp", None, reads=[], writes=[])
    fin = S.q["sp"][-1]
    for o in out_ops:
        if o not in fin.deps:
            fin.deps.append(o)
            o.flag = True
    fin.fn = lambda e: e.nop()

    with nc.Block() as block:
        S.emit(nc, es, block)
    es.close()
    return nc


def _consts():
    onehot = np.zeros((32, 387), np.float32)
    for g in range(3):
        k = np.arange(129)
        bk = t5_bucket(k * DIL[g])
        onehot[bk, 129 * g + k] = 1.0
    sel = np.zeros((24, 12, 128), np.float32)
    for g in range(3):
        for c in range(4):
            for p in range(128):
                sel[8 * g + 2 * c + p // 64, g * 4 + c, p] = 1.0
    return onehot, sel


_NC_CACHE = {}


def kernel(x_prompt, x_sample, c_prompt, c_sample, state_conv, cache_k1, cache_v1, cache_k2, cache_v2,
           cache_k3, cache_v3, rel_bias, norm1_g, norm2_g, w_ada, b_ada, w_in, conv_w, q_norm_g, k_norm_g,
           w_conv_out, w_attn_out, w_o, w_mlp_in, w_mlp_out):
    f32 = np.float32
    A = lambda a: np.ascontiguousarray(np.asarray(a, dtype=f32))
    x_prompt, x_sample = A(x_prompt), A(x_sample)
    c_prompt, c_sample, state_conv = A(c_prompt), A(c_sample), A(state_conv)
    caches_k = [A(cache_k1)[0], A(cache_k2)[0], A(cache_k3)[0]]
    caches_v = [A(cache_v1)[0], A(cache_v2)[0], A(cache_v3)[0]]
    if "nc" not in _NC_CACHE:
        _NC_CACHE["nc"] = build_program()
    nc = _NC_CACHE["nc"]
    onehot, sel = _consts()

    def fm(v, n):
        return np.ascontiguousarray(A(v).reshape(n, 128).T)

    shared = {
        "b_adaT": fm(b_ada[0], 48), "n1g": fm(norm1_g[0], 8), "n2g": fm(norm2_g[0], 8),
        "convw": np.ascontiguousarray(A(conv_w[0]).reshape(3, 8, 128).transpose(2, 1, 0)),
        "qkg": np.ascontiguousarray(np.stack([np.tile(A(q_norm_g[0]), 2), np.tile(A(k_norm_g[0]), 2)], axis=1)),
        "onehot": onehot, "rel_bias": A(rel_bias), "ident": np.eye(128, dtype=f32), "sel": sel,
        "w_ada": A(w_ada[0]), "w_in": A(w_in[0]), "w_conv_out": A(w_conv_out[0]), "w_attn_out": A(w_attn_out[0]),
        "w_o": A(w_o[0]), "w_mlp_in": A(w_mlp_in[0]), "w_mlp_out": A(w_mlp_out[0]),
    }
    in_maps = []
    for core in range(NCORE):
        b, half = divmod(core, 2)
        m = dict(shared)
        xt = np.zeros((D, NH + NTOK), f32)
        xt[:, NH:] = x_prompt[b, half * NTOK:(half + 1) * NTOK].T
        if half == 1:
            xt[:, :NH] = x_prompt[b, NTOK - NH:NTOK].T
        m["xT"] = xt
        sl = slice(core * SB_, (core + 1) * SB_)
        m["xsT"] = np.ascontiguousarray(x_sample[sl].reshape(NS, D).T)
        crow = np.concatenate([c_prompt[b:b + 1], c_sample[sl]], axis=0)
        m["cT"] = np.ascontiguousarray(crow.reshape(17, 8, 128).transpose(2, 1, 0))
        m["stT"] = np.ascontiguousarray(state_conv[0, sl].reshape(SB_, 2, 8, 128).transpose(3, 2, 0, 1))
        m["flagv"] = np.full((128, 1), float(half), f32)
        vo = np.ones((128, 5, 64), f32)
        for ti in range(4):
            vo[:128 - 32 * ti, ti, :] = float(half)
        m["vones"] = vo
        for g in range(3):
            m["ck%d" % g] = np.ascontiguousarray(caches_k[g][sl].reshape(SB_, WIN[g], 512))
            m["cv%d" % g] = np.ascontiguousarray(caches_v[g][sl].reshape(SB_, WIN[g], 512))
            kk = m["ck%d" % g]
            d = DIL[g]
            nt = 1 if g == 0 else 4
            kr = kk.reshape(SB_, 128, d, 512)[:, :, 0:nt, :]
            m["kc%dT" % g] = np.ascontiguousarray(kr.transpose(0, 3, 2, 1))
        in_maps.append(m)

    res = run_bass_kernel_spmd(nc, in_maps, core_ids=list(range(NCORE)))
    R_ = res.results
    yp = np.empty((4, 8192, D), f32)
    ys = np.empty((128, 4, D), f32)
    p_conv = np.empty((1, 4, 2, D), f32)
    pk = [np.empty((1, 4, WIN[g], 8, 64), f32) for g in range(3)]
    pv = [np.empty((1, 4, WIN[g], 8, 64), f32) for g in range(3)]
    s_conv = np.empty((1, 128, 2, D), f32)
    skk = [np.empty((1, 128, WIN[g], 8, 64), f32) for g in range(3)]
    svv = [np.empty((1, 128, WIN[g], 8, 64), f32) for g in range(3)]
    for core in range(NCORE):
        b, half = divmod(core, 2)
        r = R_[core]
        yp[b, half * NTOK:(half + 1) * NTOK] = r["yT"].T
        sl = slice(core * SB_, (core + 1) * SB_)
        ys[sl] = r["ysT"].T.reshape(SB_, 4, D)
        s_conv[0, sl] = r["sconvT"].transpose(2, 3, 1, 0).reshape(SB_, 2, D)
        for g in range(3):
            skk[g][0, sl] = r["sk%d" % g].reshape(SB_, WIN[g], 8, 64)
            svv[g][0, sl] = r["sv%d" % g].reshape(SB_, WIN[g], 8, 64)
        if half == 1:
            p_conv[0, b] = r["pconvT"].transpose(2, 1, 0).reshape(2, D)
            for g in range(3):
                pk[g][0, b] = r["pk%dT" % g].transpose(2, 1, 0).reshape(WIN[g], 8, 64)
                pv[g][0, b] = r["pv%dT" % g].transpose(2, 1, 0).reshape(WIN[g], 8, 64)
    return (yp, ys, p_conv, pk[0], pv[0], pk[1], pv[1], pk[2], pv[2],
            s_conv, skk[0], svv[0], skk[1], svv[1], skk[2], svv[2])
```

```python
import math
from contextlib import ExitStack
import numpy as np
import concourse.bass as bass
import concourse.mybir as mybir
from concourse.ap import AP
from concourse.bass_utils import run_bass_kernel_spmd

F32, BF16 = mybir.dt.float32, mybir.dt.bfloat16
AF = mybir.ActivationFunctionType
ALU = mybir.AluOpType

NCORE = 8
D = 1024
T = 512
NH = 2048
NTOK = 4096
NT = NTOK // T
NHT = NH // T
SB_ = 16
NS = 64
DPROJ = 9728
EPS = 1e-6
WIN = (128, 512, 2048)
DIL = (1, 4, 16)


class Res:
    __slots__ = ("name", "w", "r", "const")

    def __init__(self, name, const=False):
        self.name, self.w, self.r, self.const = name, None, [], const


class Op:
    __slots__ = ("eng", "fn", "deps", "flag", "rank", "dma", "sem", "semval")


ENGS = ["pe", "act", "dve", "pool", "sp"]


class Sched:
    def __init__(self):
        self.q = {e: [] for e in ENGS}
        self.strict = False

    def op(self, eng, fn, reads=(), writes=(), dma=False):
        o = Op()
        o.eng, o.fn, o.dma, o.flag, o.deps, o.rank, o.sem, o.semval = eng, fn, dma, False, [], 0, None, 0
        dl = []
        for R in reads:
            if R.w is not None:
                dl.append((R.w, True))
        for R in writes:
            if R.w is not None:
                dl.append((R.w, False))
            dl.extend((x, False) for x in R.r)
        for d, raw in dl:
            if d is o:
                continue
            if d.eng == eng and not d.dma and not dma:
                if not (self.strict and raw and eng != "pe"):
                    continue
            if d not in o.deps:
                o.deps.append(d)
                d.flag = True
        for R in reads:
            if not R.const:
                R.r.append(o)
        for R in writes:
            R.w = o
            R.r = []
        self.q[eng].append(o)
        return o

    def emit(self, nc, es, block):
        NDS = 12
        csem = {e: es.enter_context(nc.semaphore("c_" + e)) for e in ENGS}
        dsem = {e: [es.enter_context(nc.semaphore("d_%s%d" % (e, i))) for i in range(NDS)] for e in ENGS}
        for e in ENGS:
            k = 0
            kd = 0
            for o in self.q[e]:
                if o.dma:
                    o.flag = True
                    o.sem = dsem[e][kd % NDS]
                    o.semval = 16 * (kd // NDS + 1)
                    kd += 1
                elif o.flag:
                    k += 1
                    o.rank = k

        def run(e, eng):
            waited = {}
            for o in self.q[e]:
                for d in o.deps:
                    if d.dma:
                        key, sem, val = id(d.sem), d.sem, d.semval
                    else:
                        key, sem, val = d.eng, csem[d.eng], d.rank
                    if waited.get(key, 0) >= val:
                        continue
                    waited[key] = val
                    eng.wait_ge(sem, val)
                if o.fn is None:
                    continue
                ins = o.fn(eng)
                if o.flag:
                    if o.dma:
                        ins.then_inc(o.sem, 16)
                    else:
                        ins.then_inc(csem[e], 1)

        block.tensor(lambda eng: run("pe", eng))
        block.scalar(lambda eng: run("act", eng))
        block.vector(lambda eng: run("dve", eng))
        block.gpsimd(lambda eng: run("pool", eng))
        block.sync(lambda eng: run("sp", eng))


def t5_bucket(dist):
    dist = np.asarray(dist, np.int64)
    max_exact = 16
    ratio = np.maximum(dist, 1).astype(np.float32) / np.float32(max_exact)
    large = max_exact + (np.log(ratio) / np.float32(math.log(2048 / max_exact)) * np.float32(32 - max_exact)).astype(np.int32)
    large = np.minimum(large, 31)
    return np.where(dist < max_exact, dist, large)


def build_program():
    import os
    STAGE = int(os.environ.get("KSTAGE", "9"))
    NOPROMPT = int(os.environ.get("KNOPROMPT", "0"))
    out_ops = []
    nc = bass.Bass("TRN2", target_bir_lowering=False)
    S = Sched()
    es = ExitStack()

    def din(name, shape):
        return nc.dram_tensor(name, list(shape), F32, kind="ExternalInput").ap()

    def dout(name, shape):
        return nc.dram_tensor(name, list(shape), F32, kind="ExternalOutput").ap()

    def dtmp(name, shape, dt):
        return nc.dram_tensor(name, list(shape), dt, kind="Internal").ap()

    def sb(name, shape, dt):
        return es.enter_context(nc.sbuf_tensor(name, list(shape), dt))

    def psum(name, shape, dt=F32):
        return es.enter_context(nc.psum_tensor(name, list(shape), dt))

    xT = din("xT", [D, NH + NTOK])
    xsT = din("xsT", [D, NS])
    cT = din("cT", [128, 8, 17])
    stT = din("stT", [128, 8, SB_, 2])
    b_adaT = din("b_adaT", [128, 48])
    n1g_d = din("n1g", [128, 8])
    n2g_d = din("n2g", [128, 8])
    convw_d = din("convw", [128, 8, 3])
    qkg_d = din("qkg", [128, 2])
    flag_d = din("flagv", [128, 1])
    vones_d = din("vones", [128, 5, 64])
    onehot_d = din("onehot", [32, 387])
    relb_d = din("rel_bias", [32, 24])
    ident_d = din("ident", [128, 128])
    sel_d = din("sel", [24, 12, 128])
    w_ada = din("w_ada", [D, 6144])
    w_in = din("w_in", [D, DPROJ])
    w_co = din("w_conv_out", [D, D])
    w_ao = din("w_attn_out", [512, D])
    w_o = din("w_o", [D, D])
    w_mi = din("w_mlp_in", [D, 4096])
    w_mo = din("w_mlp_out", [4096, D])
    ck = [din("ck%d" % g, [SB_, WIN[g], 512]) for g in range(3)]
    cv = [din("cv%d" % g, [SB_, WIN[g], 512]) for g in range(3)]
    kcT = [din("kc0T", [SB_, 128, 4, 1, 128]), din("kc1T", [SB_, 128, 4, 4, 128]), din("kc2T", [SB_, 128, 4, 4, 128])]

    yT = dout("yT", [D, NTOK])
    ysT = dout("ysT", [D, NS])
    pconvT = dout("pconvT", [128, 8, 2])
    sconvT = dout("sconvT", [128, 8, SB_, 2])
    pkT = [dout("pk%dT" % g, [512, DIL[g], 128]) for g in range(3)]
    pvT = [dout("pv%dT" % g, [512, DIL[g], 128]) for g in range(3)]
    sk = [dout("sk%d" % g, [SB_, WIN[g], 512]) for g in range(3)]
    sv = [dout("sv%d" % g, [SB_, WIN[g], 512]) for g in range(3)]

    DBG = int(os.environ.get("KDBG", "0"))
    dbg = dout("dbg", [128, 32, 64]) if DBG else None
    dbg_n = [0]

    def dump(ap, res, name, cast=False):
        if not DBG:
            return
        k = dbg_n[0]
        dbg_n[0] += 1
        n = ap.shape[1]
        print("DBGSLOT", k, name, n)
        out_ops.append(S.op("pool" if cast else "sp", lambda e: e.dma_start(out=dbg[0:ap.shape[0], k, 0:n], in_=ap), reads=list(res), dma=True))

    wb_in = dtmp("wb_in", [19, 128, 8 * 512], BF16)
    wb_co = dtmp("wb_co", [2, 128, 8 * 512], BF16)
    wb_ao = dtmp("wb_ao", [1, 128, 4 * 1024], BF16)
    wb_o = dtmp("wb_o", [2, 128, 8 * 512], BF16)
    wb_mi = dtmp("wb_mi", [8, 128, 8 * 512], BF16)
    wb_mo = dtmp("wb_mo", [8, 128, 32 * 128], BF16)
    ebz = dtmp("ebz", [24, 128 * 512], F32)

    identb = sb("identb", [128, 128], BF16)
    onesb = sb("onesb", [128, 128], BF16)
    blk64 = sb("blk64", [128, 128], BF16)
    vones = sb("vones_s", [128, 5, 64], BF16)
    flagv = sb("flag_s", [128, 1], F32)
    epsv = sb("epsv", [128, 2], F32)
    Eall = sb("Eall", [128, 24, 256], BF16)
    modT = sb("modT", [128, 48, 17], F32)
    A1 = sb("A1", [128, 8, 17], F32)
    A2 = sb("A2", [128, 8, 17], F32)
    n1g = sb("n1g_s", [128, 8], F32)
    n2g = sb("n2g_s", [128, 8], F32)
    convw = sb("convw_s", [128, 8, 3], F32)
    qkg = sb("qkg_s", [128, 2], F32)
    badat = sb("badat", [128, 48], F32)
    ebn = sb("ebn", [128, 12, 4], F32)
    ucarry = sb("ucarry", [128, 8, 2], F32)
    KTW = (640, 1024, 2560)
    KT = [sb("KT%d" % g, [128, 4, KTW[g]], BF16) for g in range(3)]
    VT = [sb("VT%d" % g, [128, 4, KTW[g]], BF16) for g in range(3)]
    xTs = sb("xTs", [128, 8, T], F32)
    xnT = sb("xnT", [128, 8, T], BF16)
    sqb = [sb("sqb%d" % i, [128, T], BF16) for i in range(2)]
    rstd = sb("rstd", [128, T], F32)
    tmpf = [sb("tmpf%d" % i, [128, T], F32) for i in range(2)]
    A1a = sb("arenaA1", [128, 12 * T], BF16)
    A2a = sb("arenaA2", [128, 8 * T], BF16)
    A3a = sb("arenaA3", [128, 4 * T], F32)
    A4a = sb("arenaA4", [128, 4 * T], BF16)
    ue = [sb("ue0", [128, T + 2], F32)]
    mixT = sb("mixT", [128, 8, T], BF16)
    es_ = [sb("es%d" % i, [128, 512], F32) for i in range(2)]
    pt_ = [sb("pt%d" % i, [128, 512], BF16) for i in range(2)]
    vt_ = [sb("vt%d" % i, [128, 128], BF16) for i in range(4)]
    AS = [sb("AS0", [128, 2, T], F32)]
    oT = A2a[:, 0:4 * T].rearrange("p (c t) -> p c t", c=4)
    NWB = 3
    wring = [sb("wring%d" % i, [128, 4096], BF16) for i in range(NWB)]
    A1f = A1a[:].bitcast(F32)
    A2f = A2a[:].bitcast(F32)
    selS = A1f[0:24, 0:1536].rearrange("p (a b) -> p a b", a=12)
    oneh = A1f[0:32, 1536:1923]
    ebs = A1f[0:24, 1924:2311]
    zt = A1f[0:24, 2312:2824]
    cTs = A2f[:, 0:136].rearrange("p (a b) -> p a b", a=8)
    relb = A2f[0:32, 136:160]
    csil = A2a[:, 400:536].rearrange("p (a b) -> p a b", a=8)
    ues = sb("ues", [128, SB_, 6], F32)
    Es = sb("Es", [128, 96], BF16)
    KTn = sb("KTn", [128, 3, 4, NS], BF16)
    VTn = sb("VTn", [128, 3, 4, NS], BF16)
    pn = sb("pn", [128, NS], F32)
    prodb = sb("prodb", [128, NS], BF16)
    ASf = AS[0][:].rearrange("p a t -> p (a t)")
    accn = ASf[:, 0:256].rearrange("p (c n) -> p c n", c=4)
    sn = ASf[:, 256:512].rearrange("p (c n) -> p c n", c=4)
    accb = ASf[:, 512:768].rearrange("p (c n) -> p c n", c=4)
    sbb = ASf[:, 768:1024].rearrange("p (c n) -> p c n", c=4)
    QbdA = KT[1][:].rearrange("p c w -> p (c w)")
    QbdB = VT[1][:].rearrange("p c w -> p (c w)")

    def Qbd_ap(g):
        base = QbdA[:, g * 2048:(g + 1) * 2048] if g < 2 else QbdB[:, 0:2048]
        return base.rearrange("p (c n h) -> p c n h", c=4, h=8)

    big = psum("bigps", [128, 8, 512])
    mmps = [big[:, i, :] for i in range(3)]
    nsps = big[:, 3, :]
    stp = big[:, 4:6, :]
    stps = [stp[:, 0, :], stp[:, 0, :]]
    pvps = big[:, 6, :]
    trps = big[:, 7, :].bitcast(BF16)
    st_sets = [big[:, 4:6, :], big[:, 0:2, :]]
    pv_sets = [big[:, 6, :], big[:, 2, :]]
    tr_sets = [big[:, 7, :].bitcast(BF16), big[:, 3, :].bitcast(BF16)]

    def R(n, const=False):
        return Res(n, const)

    r_const = R("const", True)
    r_mm = [R("mm%d" % i) for i in range(3)]
    r_ns = R("ns")
    r_st0 = R("st")
    r_st = [r_st0, r_st0]
    r_pv = R("pv")
    r_tr0 = R("tr")
    r_tr = [r_tr0, r_tr0]
    r_w = [R("w%d" % i) for i in range(NWB)]
    r_x = [R("x%d" % c) for c in range(8)]
    r_xn = [R("xn%d" % c) for c in range(8)]
    r_sq = [R("sq0"), R("sq1")]
    r_rstd = R("rstd")
    r_tmpf = [R("tf0"), R("tf1")]
    r_q = [[R("q%d_%d" % (g, c)) for c in range(4)] for g in range(3)]
    r_kt = [[R("kt%d_%d" % (g, c)) for c in range(4)] for g in range(3)]
    r_vt = [[R("vt%d_%d" % (g, c)) for c in range(4)] for g in range(3)]
    r_sg = [R("sg%d" % c) for c in range(8)]
    r_by = [R("by%d" % c) for c in range(8)]
    r_h = [R("h%d" % c) for c in range(4)]
    r_hid = [R("hid%d" % c) for c in range(32)]
    r_ue = [R("ue0")]
    r_mix = [R("mix%d" % c) for c in range(8)]
    r_es = [R("es0"), R("es1")]
    r_pt = [R("pt0"), R("pt1")]
    r_vtile = [R("vtile%d" % i) for i in range(4)]
    r_as = [R("as0")]
    r_o = [R("o%d" % c) for c in range(4)]
    r_uc = R("ucarry")
    r_mod = R("mod")
    r_E = R("E")
    r_misc = R("misc")
    r_wbt = {}
    r_ebz = R("ebz")
    r_samp = R("samp")
    r_qbd = R("qbd")
    r_init = R("init")
    cnt = {"mm": 0, "st": 0, "w": 0, "sq": 0, "tf": 0, "es": 0, "vt": 0, "tr": 0, "as": 0, "ue": 0, "bal": 0}

    def nxt(key, n):
        v = cnt[key] % n
        cnt[key] += 1
        return v

    def qT_ap(g, c):
        return A1a[:, (g * 4 + c) * T:(g * 4 + c + 1) * T]

    def sgatt_ap(c):
        return A1a[:, c * T:(c + 1) * T]

    def by_ap(c):
        return A2a[:, c * T:(c + 1) * T]

    def h_ap(c):
        return A3a[:, c * T:(c + 1) * T]

    def hid_ap(k):
        if k < 12:
            return A1a[:, k * T:(k + 1) * T]
        if k < 20:
            return A2a[:, (k - 12) * T:(k - 11) * T]
        if k < 28:
            kk = k - 20
            return A3a[:, (kk // 2) * T:(kk // 2 + 1) * T].bitcast(BF16)[:, (kk % 2) * T:(kk % 2 + 1) * T]
        return A4a[:, (k - 28) * T:(k - 27) * T]

    def hid_alias(k):
        if k < 12:
            g, c = divmod(k, 4)
            return [r_q[g][c]] + ([r_sg[k]] if k < 8 else [])
        if k < 20:
            return [r_by[k - 12]] + ([r_o[k - 12]] if k < 16 else [])
        if k < 28:
            return [r_h[(k - 20) // 2]]
        return []

    def inherit(newR, oldRs):
        for o in oldRs:
            if o.w is not None:
                newR.r.append(o.w)
            newR.r.extend(o.r)

    wsrc = {"in": (w_in, wb_in, 8, 512), "co": (w_co, wb_co, 8, 512), "ao": (w_ao, wb_ao, 4, 1024), "o": (w_o, wb_o, 8, 512),
            "mi": (w_mi, wb_mi, 8, 512), "mo": (w_mo, wb_mo, 32, 128)}

    def cast_tile(kind, idx):
        src_t, dst_t, nc_, nf = wsrc[kind]
        src = src_t[:, idx * nf:(idx + 1) * nf].rearrange("(c p) f -> p c f", p=128)
        dst = dst_t[idx].rearrange("p (c f) -> p c f", c=nc_)
        rr = Res("wb_%s%d" % (kind, idx))
        r_wbt[(kind, idx)] = rr
        S.op("pool", lambda e: e.dma_start(out=dst, in_=src), writes=[rr], dma=True)

    def ld(eng, dst, src, wres, rres=()):
        return S.op(eng, lambda e: e.dma_start(out=dst, in_=src), reads=list(rres), writes=list(wres), dma=True)

    ld("sp", n1g[:], n1g_d[:, :], [r_misc])
    ld("sp", n2g[:], n2g_d[:, :], [r_misc])
    ld("sp", convw[:], convw_d[:, :, :], [r_misc])
    ld("sp", qkg[:], qkg_d[:, :], [r_misc])
    ld("sp", flagv[:], flag_d[:, :], [r_misc])
    ld("sp", badat[:], b_adaT[:, :], [r_misc])
    ld("pool", identb[:], ident_d[:, :], [r_misc])
    ld("pool", vones[:], vones_d[:, :, :], [r_misc])
    S.op("pool", lambda e: e.memset(onesb[:], 1.0), writes=[r_misc])
    S.op("pool", lambda e: e.memset(epsv[:, 0:1], float(D * EPS)), writes=[r_misc])
    S.op("pool", lambda e: e.memset(epsv[:, 1:2], float(64 * EPS)), writes=[r_misc])
    S.op("pool", lambda e: e.memset(blk64[:], 0.0), writes=[r_misc])
    S.op("pool", lambda e: e.memset(blk64[0:64, 0:64], 1.0), writes=[r_misc])
    S.op("pool", lambda e: e.memset(blk64[64:128, 64:128], 1.0), writes=[r_misc])
    S.op("pool", lambda e: e.memset(zt[:], 0.0), writes=[r_misc])
    S.op("pool", lambda e: e.memset(ucarry[:], 0.0), writes=[r_uc])
    S.op("dve", lambda e: e.tensor_scalar(out=qkg[:], in0=qkg[:], scalar1=8.0, scalar2=None, op0=ALU.mult),
         reads=[r_misc], writes=[r_misc])

    for blk in (11, 14, 10, 13, 9, 12, 0, 4, 1, 5):
        cast_tile("in", blk)

    ld("sp", relb, relb_d[:, :], [r_misc])
    ld("sp", oneh[:], onehot_d[:, :], [r_misc])
    ld("sp", selS[:], sel_d[:, :, :], [r_misc])
    S.op("pe", lambda e: e.matmul(mmps[0][0:24, 0:387], lhsT=relb, rhs=oneh[:], start=True, stop=True),
         reads=[r_misc], writes=[r_mm[0]])
    S.op("act", lambda e: e.activation(out=ebs[:], in_=mmps[0][0:24, 0:387], func=AF.Exp),
         reads=[r_mm[0]], writes=[r_E])
    for g in range(3):
        S.op("sp", lambda e, g=g: e.dma_start(out=zt[8 * g:8 * g + 8, 128:257],
                                             in_=ebs[8 * g:8 * g + 8, 129 * g:129 * g + 129]),
             reads=[r_E, r_misc], writes=[r_init], dma=True)
    S.op("sp", lambda e: e.dma_start(out=ebz[:, :].rearrange("a (j c) -> a j c", c=512),
                                     in_=zt.unsqueeze(1).to_broadcast([24, 128, 512])),
         reads=[r_init], writes=[r_ebz], dma=True)
    for g in range(3):
        src = AP(ebz.tensor, 8 * g * 65536 + 128, [[511, 128], [65536, 8], [1, 256]])
        S.op("pool", lambda e, g=g, src=src: e.dma_start(out=Eall[:, 8 * g:8 * g + 8, :], in_=src),
             reads=[r_ebz], writes=[r_E], dma=True)
    for g in range(3):
        for c in range(4):
            S.op("pe", lambda e, g=g, c=c: e.matmul(nsps[:, (g * 4 + c) * 4:(g * 4 + c) * 4 + 4],
                                                     lhsT=selS[:, g * 4 + c, :], rhs=ebs[:, 129 * g:129 * g + 4],
                                                     start=True, stop=True),
                 reads=[r_E, r_misc], writes=[r_ns])
    S.op("act", lambda e: e.activation(out=ebn[:].rearrange("p a b -> p (a b)"), in_=nsps[:, 0:48], func=AF.Copy),
         reads=[r_ns], writes=[r_samp])

    ld("sp", cTs[:], cT[:, :, :], [r_misc])
    S.op("act", lambda e: e.activation(out=csil[:], in_=cTs[:], func=AF.Silu), reads=[r_misc], writes=[r_mod])
    for blk in range(12):
        slot = nxt("w", NWB)
        wt = wring[slot][:].rearrange("p (c f) -> p c f", c=8)
        src = w_ada[:, blk * 512:(blk + 1) * 512].rearrange("(c p) f -> p c f", p=128)
        S.op("pool", lambda e, wt=wt, src=src: e.dma_start(out=wt, in_=src), writes=[r_w[slot]], dma=True)
        for j in range(4):
            k = blk * 4 + j
            b = nxt("mm", 3)

            def f(e, wt=wt, j=j, b=b):
                for c in range(8):
                    ins = e.matmul(mmps[b][:, 0:17], lhsT=wt[:, c, j * 128:(j + 1) * 128], rhs=csil[:, c, :],
                                   start=(c == 0), stop=(c == 7))
                return ins
            S.op("pe", f, reads=[r_w[slot], r_mod], writes=[r_mm[b]])
            S.op("act", lambda e, k=k, b=b: e.activation(out=modT[:, k, :], in_=mmps[b][:, 0:17], func=AF.Identity,
                                                        bias=badat[:, k:k + 1], scale=1.0),
                 reads=[r_mm[b], r_misc], writes=[r_mod])
    for (Ax, gx, off) in ((A1, n1g, 8), (A2, n2g, 32)):
        S.op("dve", lambda e, Ax=Ax, off=off: e.tensor_scalar(out=Ax[:], in0=modT[:, off:off + 8, :], scalar1=1.0,
                                                             scalar2=32.0, op0=ALU.add, op1=ALU.mult),
             reads=[r_mod], writes=[r_mod])
        S.op("dve", lambda e, Ax=Ax, gx=gx: e.tensor_tensor(out=Ax[:], in0=Ax[:],
                                                           in1=gx[:].unsqueeze(2).to_broadcast([128, 8, 17]),
                                                           op=ALU.mult),
             reads=[r_mod, r_misc], writes=[r_mod])
    for blk in (15, 16, 2, 3):
        cast_tile("in", blk)
    for i in range(2):
        cast_tile("co", i)
    for blk in (6, 7, 8, 17, 18):
        cast_tile("in", blk)
    cast_tile("ao", 0)
    for i in range(2):
        cast_tile("o", i)
    for i in range(8):
        cast_tile("mi", i)
    for i in range(8):
        cast_tile("mo", i)

    copy_jobs = []
    for g in (2, 1, 0):
        for b in range(SB_):
            for (dst, src) in ((sk[g], ck[g]), (sv[g], cv[g])):
                copy_jobs.append((dst[b, 0:WIN[g] - 4, :], src[b, 4:WIN[g], :]))

    def issue_copies(n):
        for _ in range(n):
            if copy_jobs:
                d_, s_ = copy_jobs.pop(0)
                out_ops.append(S.op("sp", lambda e, d_=d_, s_=s_: e.dma_start(out=d_, in_=s_), dma=True))

    def wload(kind, idx):
        slot = nxt("w", NWB)
        src_t, dst_t, nc_, nf = wsrc[kind]
        src = dst_t[idx]
        view = wring[slot][:].rearrange("p (c f) -> p c f", c=nc_)
        flat = wring[slot][:]
        S.op("sp", lambda e, flat=flat, src=src: e.dma_start(out=flat, in_=src),
             reads=[r_wbt[(kind, idx)]], writes=[r_w[slot]], dma=True)
        if STAGE >= 5:
            issue_copies(1 if cnt["w"] % 4 == 0 else 0)
        return view, r_w[slot]

    def mm_group(wview, j, nk, rhs_fn, rhs_res, wres, N):
        b = nxt("mm", 3)

        def f(e):
            for c in range(nk):
                ins = e.matmul(mmps[b][:, 0:N], lhsT=wview[:, c, j * 128:(j + 1) * 128], rhs=rhs_fn(c),
                               start=(c == 0), stop=(c == nk - 1))
            return ins
        S.op("pe", f, reads=[wres] + list(rhs_res), writes=[r_mm[b]])
        return b

    def v3(ap):
        return ap.rearrange("p (b t) -> p b t", t=4)

    def mb(tile, k):
        return tile[:, k, 1:17].unsqueeze(2).to_broadcast([128, SB_, 4])

    def norm_stage(N, A_sc, shift_sc, sample, Atok=None, shtok=None):
        for c in range(8):
            i = nxt("sq", 2)
            S.op("act", lambda e, c=c, i=i: e.activation(out=sqb[i][:, 0:N], in_=xTs[:, c, 0:N], func=AF.Square),
                 reads=[r_x[c]], writes=[r_sq[i]])
            S.op("pe", lambda e, c=c, i=i: e.matmul(nsps[:, 0:N], lhsT=onesb[:], rhs=sqb[i][:, 0:N],
                                                     start=(c == 0), stop=(c == 7)),
                 reads=[r_sq[i], r_misc], writes=[r_ns] if c == 0 else [r_ns])
        S.op("act", lambda e: e.activation(out=rstd[:, 0:N], in_=nsps[:, 0:N], func=AF.Ln, bias=epsv[:, 0:1], scale=1.0),
             reads=[r_ns, r_misc], writes=[r_rstd])
        S.op("act", lambda e: e.activation(out=rstd[:, 0:N], in_=rstd[:, 0:N], func=AF.Exp, scale=-0.5), reads=[r_rstd], writes=[r_rstd])
        for c in range(8):
            i = nxt("tf", 2)
            if not sample:
                S.op("dve", lambda e, c=c, i=i: e.scalar_tensor_tensor(out=tmpf[i][:, 0:N], in0=xTs[:, c, 0:N],
                                                                       scalar=A_sc[:, c, 0:1], in1=rstd[:, 0:N],
                                                                       op0=ALU.mult, op1=ALU.mult),
                     reads=[r_x[c], r_rstd, r_mod], writes=[r_tmpf[i]])
                S.op("act", lambda e, c=c, i=i: e.activation(out=xnT[:, c, 0:N], in_=tmpf[i][:, 0:N], func=AF.Identity,
                                                             bias=shift_sc[:, c, 0:1], scale=1.0),
                     reads=[r_tmpf[i], r_mod], writes=[r_xn[c]])
            else:
                S.op("dve", lambda e, c=c, i=i: e.tensor_tensor(out=tmpf[i][:, 0:N], in0=xTs[:, c, 0:N],
                                                                in1=rstd[:, 0:N], op=ALU.mult),
                     reads=[r_x[c], r_rstd], writes=[r_tmpf[i]])
                S.op("dve", lambda e, c=c, i=i: e.tensor_tensor(out=v3(tmpf[i][:, 0:N]), in0=v3(tmpf[i][:, 0:N]),
                                                                in1=mb(Atok, c), op=ALU.mult),
                     reads=[r_mod], writes=[r_tmpf[i]])
                S.op("dve", lambda e, c=c, i=i: e.tensor_tensor(out=v3(xnT[:, c, 0:N]), in0=v3(tmpf[i][:, 0:N]),
                                                                in1=mb(modT, shtok + c), op=ALU.add),
                     reads=[r_tmpf[i], r_mod], writes=[r_xn[c]])

    def kv_dst(buf, g, c, N, sample):
        if sample:
            return (KTn if buf is KT else VTn)[:, g, c, :], None
        if g == 0:
            return buf[0][:, c, 128:128 + T], None
        d = DIL[g]
        L = T // d
        hist = 128
        v = buf[g][:, c, :].rearrange("p (r l) -> p r l", r=d)[:, :, hist:hist + L]
        return v.rearrange("p r l -> p l r"), (L, d)

    def src_view(ap2d, perm):
        if perm is None:
            return ap2d
        L, d = perm
        return ap2d.rearrange("p (l r) -> p l r", r=d)

    pending = []

    def flush_pending():
        while pending:
            pending.pop(0)()

    def qk_block(wv, wres, g, which, N, sample, qdst=None):
        for j in range(4):
            b = mm_group(wv, j, 8, lambda c: xnT[:, c, 0:N], r_xn, wres, N)
            flush_pending()
            i = nxt("sq", 2)
            S.op("act", lambda e, b=b, i=i: e.activation(out=sqb[i][:, 0:N], in_=mmps[b][:, 0:N], func=AF.Square),
                 reads=[r_mm[b]], writes=[r_sq[i]])
            if which == 0:
                if sample:
                    dst, perm = qT_ap(g, j)[:, 0:N], None
                else:
                    if g == 0:
                        dst, perm = qT_ap(g, j), None
                    else:
                        d = DIL[g]
                        dst = qT_ap(g, j).rearrange("p (r l) -> p l r", r=d)
                        perm = (T // d, d)
                wr = [r_q[g][j]]
            else:
                dst, perm = kv_dst(KT, g, j, N, sample)
                wr = [r_kt[g][j]]

            def fin(b=b, i=i, dst=dst, perm=perm, wr=wr):
                S.op("pe", lambda e: e.matmul(nsps[:, 0:N], lhsT=blk64[:], rhs=sqb[i][:, 0:N], start=True, stop=True),
                     reads=[r_sq[i], r_misc], writes=[r_ns])
                S.op("act", lambda e: e.activation(out=rstd[:, 0:N], in_=nsps[:, 0:N], func=AF.Ln, bias=epsv[:, 1:2], scale=1.0),
                     reads=[r_ns, r_misc], writes=[r_rstd])
                S.op("act", lambda e: e.activation(out=rstd[:, 0:N], in_=rstd[:, 0:N], func=AF.Exp, scale=-0.5),
                     reads=[r_rstd], writes=[r_rstd])
                S.op("dve", lambda e: e.scalar_tensor_tensor(
                    out=dst, in0=src_view(mmps[b][:, 0:N], perm), scalar=qkg[:, which:which + 1],
                    in1=src_view(rstd[:, 0:N], perm), op0=ALU.mult, op1=ALU.mult),
                    reads=[r_mm[b], r_rstd, r_misc], writes=wr)
            pending.append(fin)

    def v_block(wv, wres, g, N, sample, halo):
        for j in range(4):
            b = mm_group(wv, j, 8, lambda c: xnT[:, c, 0:N], r_xn, wres, N)
            flush_pending()
            dst, perm = kv_dst(VT, g, j, N, sample)
            if halo:
                S.op("act", lambda e, b=b, dst=dst, perm=perm: e.activation(out=dst, in_=src_view(mmps[b][:, 0:N], perm),
                                                                            func=AF.Identity, scale=flagv[:, 0:1]),
                     reads=[r_mm[b], r_misc], writes=[r_vt[g][j]])
            else:
                S.op("act", lambda e, b=b, dst=dst, perm=perm: e.activation(out=dst, in_=src_view(mmps[b][:, 0:N], perm),
                                                                            func=AF.Copy),
                     reads=[r_mm[b]], writes=[r_vt[g][j]])

    def shift_hist():
        for g in range(3):
            for buf, rr in ((KT, r_kt), (VT, r_vt)):
                for c in range(4):
                    if g == 0:
                        S.op("pool", lambda e, buf=buf, c=c: e.tensor_copy(out=buf[0][:, c, 0:128], in_=buf[0][:, c, T:T + 128]),
                             reads=[rr[0][c]], writes=[rr[0][c]])
                    else:
                        d = DIL[g]
                        L = T // d
                        v = buf[g][:, c, :].rearrange("p (r l) -> p r l", r=d)
                        S.op("pool", lambda e, v=v, L=L: e.tensor_copy(out=v[:, :, 0:128], in_=v[:, :, L:L + 128]),
                             reads=[rr[g][c]], writes=[rr[g][c]])

    def load_x(src_ap, N):
        for c in range(8):
            S.op("act", lambda e, c=c: e.dma_start(out=xTs[:, c, 0:N], in_=src_ap[c * 128:(c + 1) * 128, :]),
                 writes=[r_x[c]], dma=True)

    def conv_branch(N, sample, halo_only=False):
        for half in range(2):
            wv_h, wr_h = wload("in", 0 + half)
            hb = []
            for j in range(4):
                if halo_only:
                    b = mm_group(wv_h, j, 8, lambda c: xnT[:, c, T - 2:T], r_xn, wr_h, 2)
                    S.op("act", lambda e, b=b, j=j: e.activation(out=h_ap(j)[:, 0:2], in_=mmps[b][:, 0:2], func=AF.Copy),
                         reads=[r_mm[b]], writes=[r_h[j]])
                else:
                    b = mm_group(wv_h, j, 8, lambda c: xnT[:, c, 0:N], r_xn, wr_h, N)
                    S.op("act", lambda e, b=b, j=j: e.activation(out=h_ap(j)[:, 0:N], in_=mmps[b][:, 0:N], func=AF.Copy),
                         reads=[r_mm[b]], writes=[r_h[j]])
            wv_c, wr_c = wload("in", 4 + half)
            for j in range(4):
                cc = half * 4 + j
                if halo_only:
                    b = mm_group(wv_c, j, 8, lambda c: xnT[:, c, T - 2:T], r_xn, wr_c, 2)
                    S.op("dve", lambda e, b=b, j=j, cc=cc: e.scalar_tensor_tensor(
                        out=ucarry[:, cc, :], in0=mmps[b][:, 0:2], scalar=flagv[:, 0:1], in1=h_ap(j)[:, 0:2],
                        op0=ALU.mult, op1=ALU.mult), reads=[r_mm[b], r_h[j], r_misc], writes=[r_uc])
                    continue
                b = mm_group(wv_c, j, 8, lambda c: xnT[:, c, 0:N], r_xn, wr_c, N)
                if not sample:
                    u = nxt("ue", 1)
                    S.op("pool", lambda e, u=u, cc=cc: e.tensor_copy(out=ue[u][:, 0:2], in_=ucarry[:, cc, :]),
                         reads=[r_uc], writes=[r_ue[u]])
                    S.op("dve", lambda e, b=b, j=j, u=u: e.tensor_tensor(out=ue[u][:, 2:2 + N], in0=mmps[b][:, 0:N],
                                                                         in1=h_ap(j)[:, 0:N], op=ALU.mult),
                         reads=[r_mm[b], r_h[j]], writes=[r_ue[u]])
                    S.op("pool", lambda e, u=u, cc=cc: e.tensor_copy(out=ucarry[:, cc, :], in_=ue[u][:, N:N + 2]),
                         reads=[r_ue[u]], writes=[r_uc])
                    y = h_ap(j)[:, 0:N]
                    S.op("dve", lambda e, u=u, cc=cc, y=y: e.tensor_scalar(out=y, in0=ue[u][:, 0:N], scalar1=convw[:, cc, 0:1],
                                                                            scalar2=None, op0=ALU.mult),
                         reads=[r_ue[u], r_misc], writes=[r_h[j]])
                    for tap in (1, 2):
                        S.op("dve", lambda e, u=u, cc=cc, y=y, tap=tap: e.scalar_tensor_tensor(
                            out=y, in0=ue[u][:, tap:tap + N], scalar=convw[:, cc, tap:tap + 1], in1=y,
                            op0=ALU.mult, op1=ALU.add), reads=[r_ue[u], r_misc], writes=[r_h[j]])
                else:
                    S.op("sp", lambda e, cc=cc: e.dma_start(out=ues[:, :, 0:2], in_=stT[:, cc, :, :]),
                         writes=[r_ue[0]], dma=True)
                    S.op("dve", lambda e, b=b, j=j: e.tensor_tensor(
                        out=ues[:, :, 2:6], in0=mmps[b][:, 0:N].rearrange("p (b t) -> p b t", t=4),
                        in1=h_ap(j)[:, 0:N].rearrange("p (b t) -> p b t", t=4), op=ALU.mult),
                        reads=[r_mm[b], r_h[j]], writes=[r_ue[0]])
                    out_ops.append(S.op("sp", lambda e, cc=cc: e.dma_start(out=sconvT[:, cc, :, :], in_=ues[:, :, 4:6]),
                                        reads=[r_ue[0]], dma=True))
                    y = h_ap(j)[:, 0:N].rearrange("p (b t) -> p b t", t=4)
                    S.op("pool", lambda e, cc=cc, y=y: e.tensor_scalar(out=y, in0=ues[:, :, 0:4], scalar1=convw[:, cc, 0:1],
                                                                       scalar2=None, op0=ALU.mult),
                         reads=[r_ue[0], r_misc], writes=[r_h[j]])
                    for tap in (1, 2):
                        S.op("dve", lambda e, cc=cc, y=y, tap=tap: e.scalar_tensor_tensor(
                            out=y, in0=ues[:, :, tap:tap + 4], scalar=convw[:, cc, tap:tap + 1], in1=y,
                            op0=ALU.mult, op1=ALU.add), reads=[r_ue[0], r_misc], writes=[r_h[j]])
            if halo_only:
                continue
            wv_b, wr_b = wload("in", 2 + half)
            for j in range(4):
                cc = half * 4 + j
                b = mm_group(wv_b, j, 8, lambda c: xnT[:, c, 0:N], r_xn, wr_b, N)
                S.op("dve", lambda e, b=b, j=j, cc=cc: e.tensor_tensor(out=by_ap(cc)[:, 0:N], in0=mmps[b][:, 0:N],
                                                                      in1=h_ap(j)[:, 0:N], op=ALU.mult),
                     reads=[r_mm[b], r_h[j]], writes=[r_by[cc]])

    def evac_vt(src_ps, dst, rsrc, rdst, npart):
        if nxt("bal", 2) == 0:
            S.op("act", lambda e: e.activation(out=dst, in_=src_ps, func=AF.Copy), reads=[rsrc], writes=[rdst])
        else:
            S.op("dve", lambda e: e.tensor_copy(out=dst, in_=src_ps), reads=[rsrc], writes=[rdst])

    def attn_stages(idx, g, c, qap, kchunks, vchunks, nq, as_i, as_view, first, ones_list):
        sset = idx % 2
        stt = st_sets[sset]
        pvt = pv_sets[sset]
        trt = tr_sets[sset]
        rs_st = [r_st0] if sset == 0 else [r_mm[0], r_mm[1]]
        rs_pv = [r_pv] if sset == 0 else [r_mm[2]]
        rs_tr = [r_tr0] if sset == 0 else [r_ns]
        W = 2 * nq
        ei = sset
        nk0 = kchunks[0][1]
        stv = stt[:, :, 0:W].rearrange("p h (k q) -> p h k q", k=2)
        esv = es_[ei][:, 0:2 * W].rearrange("p (h k q) -> p h k q", h=2, k=2)
        ptv = pt_[ei][:, 0:2 * W].rearrange("p (h k q) -> p h k q", h=2, k=2)
        gh = 8 * g + 2 * c
        vts = [(2 * sset + ci, nk) for ci, (vap, nk) in enumerate(vchunks)]

        def stageA():
            def fs(e):
                for hh in range(2):
                    pr = slice(hh * 64, hh * 64 + 64)
                    for ci, (kap, nk) in enumerate(kchunks):
                        ins = e.matmul(stt[0:nk, hh, ci * nq: ci * nq + nq], lhsT=kap[pr, :], rhs=qap[pr, :],
                                       start=True, stop=True)
                return ins
            S.op("pe", fs, reads=[r_q[g][c], r_kt[g][c]], writes=rs_st)
            for ci, (vap, nk) in enumerate(vchunks):
                S.op("pe", lambda e, vap=vap, nk=nk, ci=ci: e.transpose(trt[0:nk, ci * 512: ci * 512 + 128], vap, identb[:]),
                     reads=[r_vt[g][c], r_misc], writes=rs_tr)

        def stageB1():
            use_act = (idx % 2 == 0)
            if nk0 == 128:
                S.op("act", lambda e: e.activation(out=es_[ei][:, 0:2 * W].rearrange("p (h w) -> p h w", h=2), in_=stt[:, :, 0:W],
                                                   func=AF.Exp, scale=0.125),
                     reads=rs_st, writes=[r_es[ei]])
            else:
                S.op("act", lambda e: e.activation(out=esv[:, :, 1, :], in_=stv[:, :, 1, :], func=AF.Exp, scale=0.125),
                     reads=rs_st, writes=[r_es[ei]])
                S.op("act", lambda e: e.activation(out=esv[0:nk0, :, 0, :], in_=stv[0:nk0, :, 0, :], func=AF.Exp, scale=0.125),
                     reads=rs_st, writes=[r_es[ei]])
            for ci, (vap, nk) in enumerate(vchunks):
                vi = vts[ci][0]
                src_ps = trt[0:nk, ci * 512: ci * 512 + 128]
                dstv = vt_[vi][0:nk, :]
                if use_act:
                    S.op("act", lambda e, src_ps=src_ps, dstv=dstv: e.activation(out=dstv, in_=src_ps, func=AF.Copy),
                         reads=rs_tr, writes=[r_vtile[vi]])
                else:
                    S.op("dve", lambda e, src_ps=src_ps, dstv=dstv: e.tensor_copy(out=dstv, in_=src_ps),
                         reads=rs_tr, writes=[r_vtile[vi]])

        def stageB():
            if nk0 == 128:
                Ev = Eall[:, gh:gh + 2, :].rearrange("p h (k q) -> p h k q", k=2)[:, :, :, 0:nq]
                S.op("dve", lambda e: e.tensor_tensor(out=ptv, in0=esv, in1=Ev, op=ALU.mult),
                     reads=[r_es[ei], r_E], writes=[r_pt[ei]])
            else:
                S.op("dve", lambda e: e.tensor_tensor(out=ptv[:, :, 1, :], in0=esv[:, :, 1, :],
                                                      in1=Eall[:, gh:gh + 2, 128:128 + nq], op=ALU.mult),
                     reads=[r_es[ei], r_E], writes=[r_pt[ei]])
                S.op("dve", lambda e: e.tensor_tensor(out=ptv[0:nk0, :, 0, :], in0=esv[0:nk0, :, 0, :],
                                                      in1=Eall[0:nk0, gh:gh + 2, 0:nq], op=ALU.mult),
                     reads=[r_es[ei], r_E], writes=[r_pt[ei]])

        def stageC():
            def fpv(e):
                for hh in range(2):
                    pr = slice(hh * 64, hh * 64 + 64)
                    for ci, (vi, nk) in enumerate(vts):
                        e.matmul(pvt[pr, 0:nq], lhsT=vt_[vi][0:nk, pr], rhs=ptv[0:nk, hh, ci, :],
                                 start=(ci == 0), stop=(ci == 1))
                    for ci, (vi, nk) in enumerate(vts):
                        ins = e.matmul(pvt[pr, 128:128 + nq], lhsT=ones_list[ci][0:nk, :], rhs=ptv[0:nk, hh, ci, :],
                                       start=(ci == 0), stop=(ci == 1))
                return ins
            S.op("pe", fpv, reads=[r_pt[ei], r_misc] + [r_vtile[vi] for vi, _ in vts], writes=rs_pv)

        def stageD():
            pvv = pvt[:, 0:256].rearrange("p (a q) -> p a q", a=2)[:, :, 0:nq]
            if first:
                S.op("act", lambda e: e.activation(out=as_view, in_=pvv, func=AF.Copy), reads=rs_pv, writes=[r_as[as_i]])
            else:
                S.op("dve", lambda e: e.tensor_tensor(out=as_view, in0=pvv, in1=as_view, op=ALU.add),
                     reads=rs_pv, writes=[r_as[as_i]])
        return stageA, stageB, stageC, stageD, stageB1

    def attention_tile(ti):
        blocks = []
        for c in range(4):
            ai = 0
            asb = AS[ai]
            for blk in range(4):
                q = qT_ap(0, c)[:, blk * 128:(blk + 1) * 128]
                kc = KT[0][:, c, 128 + blk * 128: 256 + blk * 128]
                kp = KT[0][:, c, blk * 128: 128 + blk * 128]
                vc = VT[0][:, c, 128 + blk * 128: 256 + blk * 128]
                vp = VT[0][:, c, blk * 128: 128 + blk * 128]
                halo = (ti == 0 and blk == 0)
                blocks.append((0, c, q, [(kc, 128), (kp, 128)], [(vc, 128), (vp, 128)], 128, ai,
                               asb[:, :, blk * 128:(blk + 1) * 128], True,
                               [vones[:, 4, :], vones[:, 0, :] if halo else vones[:, 4, :]]))
            for r in range(4):
                q = qT_ap(1, c)[:, r * 128:(r + 1) * 128]
                kv = KT[1][:, c, :].rearrange("p (r l) -> p r l", r=4)
                vv = VT[1][:, c, :].rearrange("p (r l) -> p r l", r=4)
                halo = (ti == 0)
                blocks.append((1, c, q, [(kv[:, r, 128:256], 128), (kv[:, r, 0:128], 128)],
                               [(vv[:, r, 128:256], 128), (vv[:, r, 0:128], 128)], 128, ai,
                               asb[:].rearrange("p a (l r) -> p a l r", r=4)[:, :, :, r], False,
                               [vones[:, 4, :], vones[:, 0, :] if halo else vones[:, 4, :]]))
            for r in range(16):
                q = qT_ap(2, c)[:, r * 32:(r + 1) * 32]
                kv = KT[2][:, c, :].rearrange("p (r l) -> p r l", r=16)
                vv = VT[2][:, c, :].rearrange("p (r l) -> p r l", r=16)
                blocks.append((2, c, q, [(kv[:, r, 128:160], 32), (kv[:, r, 0:128], 128)],
                               [(vv[:, r, 128:160], 32), (vv[:, r, 0:128], 128)], 32, ai,
                               asb[:].rearrange("p a (l r) -> p a l r", r=16)[:, :, :, r], False,
                               [vones[:, 4, :], vones[:, min(ti, 4), :]]))
        stages = [attn_stages(i, *blk) for i, blk in enumerate(blocks)]
        nb = len(stages)
        stages[0][0]()
        stages[1][0]()
        for i in range(nb):
            stages[i][4]()
            if i + 2 < nb:
                stages[i + 2][0]()
            stages[i][1]()
            stages[i][2]()
            stages[i][3]()
            if i % 24 == 23:
                c = i // 24
                asb = AS[0]
                S.op("dve", lambda e, asb=asb: e.reciprocal(out=asb[:, 1, :], in_=asb[:, 1, :]), reads=[r_as[0]], writes=[r_as[0]])
                S.op("dve", lambda e, asb=asb, c=c: e.tensor_tensor(out=oT[:, c, :], in0=asb[:, 0, :], in1=asb[:, 1, :], op=ALU.mult),
                     reads=[r_as[0]], writes=[r_o[c]])

    def sample_attention():
        N = NS
        for c in range(4):
            inherit(r_qbd, [r_kt[1][c], r_vt[1][c]])
        inherit(r_samp, [r_as[0]])
        S.op("pool", lambda e: e.memset(QbdA[:, 0:4096], 0.0), writes=[r_qbd])
        S.op("pool", lambda e: e.memset(QbdB[:, 0:2048], 0.0), writes=[r_qbd])
        for g in range(3):
            for c in range(4):
                for hh in range(2):
                    pr = slice(hh * 64, hh * 64 + 64)
                    S.op("pool", lambda e, g=g, c=c, hh=hh, pr=pr: e.tensor_copy(out=Qbd_ap(g)[pr, c, :, 2 * c + hh],
                                                                                 in_=qT_ap(g, c)[pr, 0:N]),
                         reads=[r_q[g][c]], writes=[r_qbd])
        S.op("pool", lambda e: e.tensor_copy(out=Es[:, 0:32].rearrange("p (t h) -> p t h", h=8),
                                             in_=Eall[:, 0:8, 128:132].rearrange("p h t -> p t h")),
             reads=[r_E], writes=[r_samp])
        for g in (1, 2):
            S.op("pool", lambda e, g=g: e.tensor_copy(out=Es[:, 32 * g:32 * g + 32].rearrange("p (t h) -> p t h", h=8),
                                                      in_=Eall[:, 8 * g:8 * g + 8, 128:129].rearrange("p h t -> p t h").to_broadcast([128, 4, 8])),
                 reads=[r_E], writes=[r_samp])
        S.op("pool", lambda e: e.memset(accn[:], 0.0), writes=[r_samp])
        S.op("pool", lambda e: e.memset(sn[:], 0.0), writes=[r_samp])
        for g in range(3):
            for c in range(4):
                qv = qT_ap(g, c)[:, 0:N].rearrange("p (b t) -> p b t", t=4)
                kvw = KTn[:, g, c, :].rearrange("p (b t) -> p b t", t=4)
                vvw = VTn[:, g, c, :].rearrange("p (b t) -> p b t", t=4)
                for dl in range(4 if g == 0 else 1):
                    n = 4 - dl
                    pv3 = prodb[:, 0:SB_ * n].rearrange("p (b t) -> p b t", t=n)
                    S.op("dve", lambda e, qv=qv, kvw=kvw, dl=dl, n=n, pv3=pv3: e.tensor_tensor(
                        out=pv3, in0=qv[:, :, dl:4], in1=kvw[:, :, 0:n], op=ALU.mult),
                        reads=[r_q[g][c], r_kt[g][c]], writes=[r_samp])
                    S.op("pe", lambda e, n=n: e.matmul(nsps[:, 0:SB_ * n], lhsT=blk64[:], rhs=prodb[:, 0:SB_ * n], start=True, stop=True),
                         reads=[r_samp, r_misc], writes=[r_ns])
                    S.op("act", lambda e, n=n: e.activation(out=pn[:, 0:SB_ * n], in_=nsps[:, 0:SB_ * n], func=AF.Exp, scale=0.125),
                         reads=[r_ns], writes=[r_samp])
                    pn3 = pn[:, 0:SB_ * n].rearrange("p (b t) -> p b t", t=n)
                    S.op("dve", lambda e, g=g, c=c, dl=dl, pn3=pn3: e.tensor_scalar(
                        out=pn3, in0=pn3, scalar1=ebn[:, g * 4 + c, dl:dl + 1], scalar2=None, op0=ALU.mult),
                        reads=[r_samp], writes=[r_samp])
                    snv = sn[:, c, :].rearrange("p (b t) -> p b t", t=4)[:, :, dl:4]
                    acv = accn[:, c, :].rearrange("p (b t) -> p b t", t=4)[:, :, dl:4]
                    S.op("dve", lambda e, snv=snv, pn3=pn3: e.tensor_tensor(out=snv, in0=snv, in1=pn3, op=ALU.add),
                         reads=[r_samp], writes=[r_samp])
                    S.op("dve", lambda e, pn3=pn3, vvw=vvw, n=n: e.tensor_tensor(out=pn3, in0=pn3, in1=vvw[:, :, 0:n], op=ALU.mult),
                         reads=[r_samp, r_vt[g][c]], writes=[r_samp])
                    S.op("dve", lambda e, acv=acv, pn3=pn3: e.tensor_tensor(out=acv, in0=acv, in1=pn3, op=ALU.add),
                         reads=[r_samp], writes=[r_samp])
        K2f = KT[2][:].rearrange("p c w -> p (c w)")
        V2f = VT[2][:].rearrange("p c w -> p (c w)")
        kts = [K2f[:, i * 4608:(i + 1) * 4608].rearrange("p (c k j) -> p c k j", c=4, k=9) for i in range(2)]
        vts = [V2f[:, i * 4608:(i + 1) * 4608].rearrange("p (k f) -> p k f", k=9) for i in range(2)]
        r_kts = [R("kts0"), R("kts1")]
        r_vts = [R("vts0"), R("vts1")]
        for i in range(2):
            for c in range(4):
                inherit(r_kts[i], [r_kt[2][c]])
                inherit(r_vts[i], [r_vt[2][c]])
        for b in range(SB_):
            i = b % 2
            for g in range(3):
                nt = 1 if g == 0 else 4
                k0 = 0 if g == 0 else (1 if g == 1 else 5)
                S.op("pool", lambda e, g=g, b=b, i=i, k0=k0, nt=nt: e.dma_start(
                    out=kts[i][:, :, k0:k0 + nt, :], in_=kcT[g][b]),
                    writes=[r_kts[i]], dma=True)
                d = DIL[g]
                S.op("pool", lambda e, g=g, b=b, i=i, k0=k0, nt=nt, d=d: e.dma_start(
                    out=vts[i][:, k0:k0 + nt, :], in_=cv[g][b].rearrange("(j t) f -> j t f", t=d)[:, 0:nt, :]),
                    writes=[r_vts[i]], dma=True)
            st = nxt("st", 2)

            def fs(e, b=b, i=i, st=st):
                for kap in range(9):
                    g = 0 if kap == 0 else (1 if kap < 5 else 2)
                    t = 0 if kap == 0 else (kap - 1) % 4
                    if g == 0:
                        col, ncol = 0, 32
                    else:
                        col, ncol = 32 * g + 8 * t, 8
                    for c in range(4):
                        if g == 0:
                            rhs = Qbd_ap(0)[:, c, 4 * b:4 * b + 4, :]
                        else:
                            rhs = Qbd_ap(g)[:, c, 4 * b + t, :]
                        ins = e.matmul(stps[st][:, col:col + ncol], lhsT=kts[i][:, c, kap, :], rhs=rhs,
                                       start=(c == 0), stop=(c == 3))
                return ins
            S.op("pe", fs, reads=[r_kts[i], r_qbd], writes=[r_st[st]])
            ei = nxt("es", 2)
            S.op("act", lambda e, st=st, ei=ei: e.activation(out=es_[ei][:, 0:96], in_=stps[st][:, 0:96], func=AF.Exp, scale=0.125),
                 reads=[r_st[st]], writes=[r_es[ei]])
            S.op("dve", lambda e, ei=ei: e.tensor_tensor(out=pt_[ei][:, 0:96], in0=es_[ei][:, 0:96], in1=Es[:], op=ALU.mult),
                 reads=[r_es[ei], r_samp], writes=[r_pt[ei]])

            def fp(e, i=i, ei=ei):
                for c in range(4):
                    for kap in range(9):
                        g = 0 if kap == 0 else (1 if kap < 5 else 2)
                        t = 0 if kap == 0 else (kap - 1) % 4
                        col, ncol = (0, 32) if g == 0 else (32 * g + 8 * t, 8)
                        e.matmul(pvps[:, c * 96 + col: c * 96 + col + ncol], lhsT=vts[i][:, kap, c * 128:(c + 1) * 128],
                                 rhs=pt_[ei][:, col:col + ncol], start=True, stop=True)
                return e.matmul(pvps[:, 384:480], lhsT=onesb[:], rhs=pt_[ei][:, 0:96], start=True, stop=True)
            S.op("pe", fp, reads=[r_vts[i], r_pt[ei], r_misc], writes=[r_pv])
            for c in range(4):
                for hh in range(2):
                    pr = slice(hh * 64, hh * 64 + 64)
                    h = 2 * c + hh
                    srcA = pvps[pr, c * 96 + h: c * 96 + h + 96].rearrange("p (g t h) -> p t g h", g=3, t=4)[:, :, :, 0]
                    srcS = pvps[pr, 384 + h: 384 + h + 96].rearrange("p (g t h) -> p t g h", g=3, t=4)[:, :, :, 0]
                    S.op("dve", lambda e, pr=pr, c=c, b=b, srcA=srcA: e.tensor_reduce(
                        out=accb[pr, c, 4 * b:4 * b + 4], in_=srcA, axis=mybir.AxisListType.X, op=ALU.add),
                        reads=[r_pv], writes=[r_samp])
                    S.op("dve", lambda e, pr=pr, c=c, b=b, srcS=srcS: e.tensor_reduce(
                        out=sbb[pr, c, 4 * b:4 * b + 4], in_=srcS, axis=mybir.AxisListType.X, op=ALU.add),
                        reads=[r_pv], writes=[r_samp])
        S.op("dve", lambda e: e.tensor_tensor(out=accb[:], in0=accb[:], in1=accn[:], op=ALU.add), reads=[r_samp], writes=[r_samp])
        S.op("dve", lambda e: e.tensor_tensor(out=sbb[:], in0=sbb[:], in1=sn[:], op=ALU.add), reads=[r_samp], writes=[r_samp])
        S.op("dve", lambda e: e.reciprocal(out=sbb[:], in_=sbb[:]), reads=[r_samp], writes=[r_samp])
        for c in range(4):
            S.op("dve", lambda e, c=c: e.tensor_tensor(out=oT[:, c, 0:N], in0=accb[:, c, :], in1=sbb[:, c, :], op=ALU.mult),
                 reads=[r_samp], writes=[r_o[c]])
        for g in range(3):
            for which, buf in ((0, KTn), (1, VTn)):
                i = nxt("tf", 2)
                for c in range(4):
                    ti_ = nxt("tr", 2)
                    S.op("pe", lambda e, buf=buf, g=g, c=c, ti_=ti_: e.transpose(trps[0:N, ti_ * 512: ti_ * 512 + 128], buf[:, g, c, :], identb[:]),
                         reads=[r_kt[g][c], r_vt[g][c], r_misc], writes=[r_tr[ti_]])
                    S.op("act", lambda e, i=i, c=c, ti_=ti_: e.activation(
                        out=tmpf[i][0:N, c * 128:(c + 1) * 128], in_=trps[0:N, ti_ * 512: ti_ * 512 + 128], func=AF.Copy),
                        reads=[r_tr[ti_]], writes=[r_tmpf[i]])
                dst = (sk if which == 0 else sv)[g]
                for b in range(SB_):
                    out_ops.append(S.op("sp", lambda e, dst=dst, g=g, i=i, b=b: e.dma_start(
                        out=dst[b, WIN[g] - 4:WIN[g], :], in_=tmpf[i][4 * b:4 * b + 4, :]), reads=[r_tmpf[i]], dma=True))

    def full_tile(ti, sample):
        N = NS if sample else T
        mi = 0
        if sample:
            load_x(xsT, N)
            dump(xTs[:, 0, 0:N], [r_x[0]], "x0")
            norm_stage(N, None, None, True, A1, 0)
            dump(rstd[:, 0:N], [r_rstd], "rstd1")
            dump(xnT[:, 0, 0:N], [r_xn[0]], "xn0", True)
            dump(xnT[:, 7, 0:N], [r_xn[7]], "xn7", True)
            dump(A1[:, 0, :], [r_mod], "A1_0")
            dump(modT[:, 0, :], [r_mod], "shift1_0")
        else:
            load_x(xT[:, NH + ti * T: NH + (ti + 1) * T], N)
            norm_stage(N, A1, modT[:, 0:8, :], False)
        for half in range(2):
            wv, wr = wload("in", 15 + half)
            for j in range(4):
                cc = half * 4 + j
                b = mm_group(wv, j, 8, lambda c: xnT[:, c, 0:N], r_xn, wr, N)
                S.op("act", lambda e, b=b, cc=cc: e.activation(out=mixT[:, cc, 0:N], in_=mmps[b][:, 0:N], func=AF.Sigmoid),
                     reads=[r_mm[b]], writes=[r_mix[cc]])
        for j in range(4):
            inherit(r_h[j], [r_hid[20 + 2 * j], r_hid[21 + 2 * j]])
        for cc in range(8):
            inherit(r_by[cc], [r_hid[12 + cc]])
        conv_branch(N, sample)
        for half in range(2):
            wv, wr = wload("co", half)
            for j in range(4):
                cc = half * 4 + j
                b = mm_group(wv, j, 8, lambda c: by_ap(c)[:, 0:N], r_by, wr, N)
                S.op("dve", lambda e, b=b, cc=cc: e.tensor_tensor(out=mixT[:, cc, 0:N], in0=mmps[b][:, 0:N], in1=mixT[:, cc, 0:N], op=ALU.mult),
                     reads=[r_mm[b]], writes=[r_mix[cc]])
        for g in range(3):
            for c in range(4):
                inherit(r_q[g][c], [r_hid[g * 4 + c]] + ([r_sg[g * 4 + c]] if g * 4 + c < 8 else []))
        for g in range(3):
            wv, wr = wload("in", 6 + g)
            qk_block(wv, wr, g, 0, N, sample)
            wv, wr = wload("in", 9 + g)
            qk_block(wv, wr, g, 1, N, sample)
            wv, wr = wload("in", 12 + g)
            v_block(wv, wr, g, N, sample, False)
        flush_pending()
        if sample:
            dump(mixT[:, 0, 0:N], [r_mix[0]], "mix0_after_conv", True)
            dump(qT_ap(0, 0)[:, 0:N], [r_q[0][0]], "q00", True)
            dump(KTn[:, 0, 0, :], [r_kt[0][0]], "k00", True)
            dump(VTn[:, 0, 0, :], [r_vt[0][0]], "v00", True)
            dump(qT_ap(2, 1)[:, 0:N], [r_q[2][1]], "q21", True)
            dump(KTn[:, 2, 1, :], [r_kt[2][1]], "k21", True)
            dump(VTn[:, 2, 1, :], [r_vt[2][1]], "v21", True)
            sample_attention()
            dump(accn[:, 0, :], [r_samp], "accn0")
            dump(sn[:, 0, :], [r_samp], "sn0")
            dump(accb[:, 0, :], [r_samp], "accb0(final)")
            dump(sbb[:, 0, :], [r_samp], "rsbb0(final recip)")
            dump(oT[:, 0, 0:N], [r_o[0]], "o0", True)
            dump(oT[:, 3, 0:N], [r_o[3]], "o3", True)
        else:
            attention_tile(ti)
            shift_hist()
        for k in range(8):
            inherit(r_sg[k], [r_q[k // 4][k % 4]])
        for half in range(2):
            wv, wr = wload("in", 17 + half)
            for j in range(4):
                cc = half * 4 + j
                b = mm_group(wv, j, 8, lambda c: xnT[:, c, 0:N], r_xn, wr, N)
                S.op("act", lambda e, b=b, cc=cc: e.activation(out=sgatt_ap(cc)[:, 0:N], in_=mmps[b][:, 0:N], func=AF.Sigmoid),
                     reads=[r_mm[b]], writes=[r_sg[cc]])
        wv, wr = wload("ao", 0)
        for jj in range(8):
            b = mm_group(wv, jj, 4, lambda c: oT[:, c, 0:N], r_o, wr, N)
            i = nxt("tf", 2)
            S.op("dve", lambda e, b=b, jj=jj, i=i: e.tensor_tensor(out=tmpf[i][:, 0:N], in0=mmps[b][:, 0:N], in1=sgatt_ap(jj)[:, 0:N], op=ALU.mult),
                 reads=[r_mm[b], r_sg[jj]], writes=[r_tmpf[i]])
            S.op("pool", lambda e, jj=jj, i=i: e.tensor_tensor(out=mixT[:, jj, 0:N], in0=mixT[:, jj, 0:N], in1=tmpf[i][:, 0:N], op=ALU.add),
                 reads=[r_tmpf[i]], writes=[r_mix[jj]])
        for half in range(2):
            wv, wr = wload("o", half)
            for j in range(4):
                jj = half * 4 + j
                b = mm_group(wv, j, 8, lambda c: mixT[:, c, 0:N], r_mix, wr, N)
                if not sample:
                    S.op("dve", lambda e, b=b, jj=jj: e.scalar_tensor_tensor(out=xTs[:, jj, 0:N], in0=mmps[b][:, 0:N], scalar=modT[:, 16 + jj, 0:1],
                                                                            in1=xTs[:, jj, 0:N], op0=ALU.mult, op1=ALU.add),
                         reads=[r_mm[b], r_mod], writes=[r_x[jj]])
                else:
                    i = nxt("tf", 2)
                    S.op("dve", lambda e, b=b, jj=jj, i=i: e.tensor_tensor(out=v3(tmpf[i][:, 0:N]), in0=v3(mmps[b][:, 0:N]), in1=mb(modT, 16 + jj), op=ALU.mult),
                         reads=[r_mm[b], r_mod], writes=[r_tmpf[i]])
                    S.op("dve", lambda e, jj=jj, i=i: e.tensor_tensor(out=xTs[:, jj, 0:N], in0=xTs[:, jj, 0:N], in1=tmpf[i][:, 0:N], op=ALU.add),
                         reads=[r_tmpf[i]], writes=[r_x[jj]])
        if sample:
            dump(mixT[:, 0, 0:N], [r_mix[0]], "mix0_final", True)
            dump(xTs[:, 0, 0:N], [r_x[0]], "x1_0")
        if sample:
            norm_stage(N, None, None, True, A2, 24)
        else:
            norm_stage(N, A2, modT[:, 24:32, :], False)
        for k in range(32):
            inherit(r_hid[k], hid_alias(k))
        for blk in range(8):
            wv, wr = wload("mi", blk)
            for j in range(4):
                k = blk * 4 + j
                b = mm_group(wv, j, 8, lambda c: xnT[:, c, 0:N], r_xn, wr, N)
                i = nxt("tf", 2)
                S.op("act", lambda e, b=b, i=i: e.activation(out=tmpf[i][:, 0:N], in_=mmps[b][:, 0:N], func=AF.Relu),
                     reads=[r_mm[b]], writes=[r_tmpf[i]])
                S.op("pool", lambda e, k=k, i=i: e.tensor_tensor(out=hid_ap(k)[:, 0:N], in0=tmpf[i][:, 0:N], in1=tmpf[i][:, 0:N], op=ALU.mult),
                     reads=[r_tmpf[i]], writes=[r_hid[k]])
        for jj in range(8):
            wv, wr = wload("mo", jj)
            b = mm_group(wv, 0, 32, lambda c: hid_ap(c)[:, 0:N], r_hid, wr, N)
            if not sample:
                S.op("dve", lambda e, b=b, jj=jj: e.scalar_tensor_tensor(out=xTs[:, jj, 0:N], in0=mmps[b][:, 0:N], scalar=modT[:, 40 + jj, 0:1],
                                                                        in1=xTs[:, jj, 0:N], op0=ALU.mult, op1=ALU.add),
                     reads=[r_mm[b], r_mod], writes=[r_x[jj]])
                dst = yT[jj * 128:(jj + 1) * 128, ti * T:(ti + 1) * T]
            else:
                i = nxt("tf", 2)
                S.op("dve", lambda e, b=b, jj=jj, i=i: e.tensor_tensor(out=v3(tmpf[i][:, 0:N]), in0=v3(mmps[b][:, 0:N]), in1=mb(modT, 40 + jj), op=ALU.mult),
                     reads=[r_mm[b], r_mod], writes=[r_tmpf[i]])
                S.op("dve", lambda e, jj=jj, i=i: e.tensor_tensor(out=xTs[:, jj, 0:N], in0=xTs[:, jj, 0:N], in1=tmpf[i][:, 0:N], op=ALU.add),
                     reads=[r_tmpf[i]], writes=[r_x[jj]])
                dst = ysT[jj * 128:(jj + 1) * 128, :]
            out_ops.append(S.op("act", lambda e, jj=jj, dst=dst: e.dma_start(out=dst, in_=xTs[:, jj, 0:N]), reads=[r_x[jj]], dma=True))

    for hi in range(NHT if (STAGE >= 1 and not NOPROMPT) else 0):
        load_x(xT[:, hi * T:(hi + 1) * T], T)
        norm_stage(T, A1, modT[:, 0:8, :], False)
        groups = (0, 1, 2) if hi == NHT - 1 else (2,)
        for g in groups:
            wv, wr = wload("in", 9 + g)
            qk_block(wv, wr, g, 1, T, False)
            wv, wr = wload("in", 12 + g)
            v_block(wv, wr, g, T, False, True)
        flush_pending()
        if hi == NHT - 1:
            conv_branch(T, False, halo_only=True)
        shift_hist()
    for ti in range(0 if NOPROMPT else (NT if STAGE >= 3 else (1 if STAGE == 2 else 0))):
        full_tile(ti, False)
    out_ops.append(S.op("sp", lambda e: e.dma_start(out=pconvT[:, :, :], in_=ucarry[:]), reads=[r_uc], dma=True))
    for g in range(3):
        d = DIL[g]
        for buf, rr, dst in ((KT, r_kt, pkT[g]), (VT, r_vt, pvT[g])):
            for c in range(4):
                srcv = buf[g][:, c, :].rearrange("p (r l) -> p r l", r=d)[:, :, 0:128] if g > 0 else buf[g][:, c, 0:128].unsqueeze(1)
                out_ops.append(S.op("pool", lambda e, srcv=srcv, dst=dst, c=c: e.dma_start(out=dst[c * 128:(c + 1) * 128, :, :], in_=srcv),
                                    reads=[rr[g][c]], dma=True))
    if STAGE >= 4:
        S.strict = True
        full_tile(0, True)
        S.strict = False
    if STAGE >= 5:
        issue_copies(len(copy_jobs))
    S.op("sp", None, reads=[], writes=[])
    fin = S.q["sp"][-1]
    for o in out_ops:
        if o not in fin.deps:
            fin.deps.append(o)
            o.flag = True
    fin.fn = lambda e: e.nop()

    with nc.Block() as block:
        S.emit(nc, es, block)
    es.close()
    return nc


def _consts():
    onehot = np.zeros((32, 387), np.float32)
    for g in range(3):
        k = np.arange(129)
        bk = t5_bucket(k * DIL[g])
        onehot[bk, 129 * g + k] = 1.0
    sel = np.zeros((24, 12, 128), np.float32)
    for g in range(3):
        for c in range(4):
            for p in range(128):
                sel[8 * g + 2 * c + p // 64, g * 4 + c, p] = 1.0
    return onehot, sel


_NC_CACHE = {}


def kernel(x_prompt, x_sample, c_prompt, c_sample, state_conv, cache_k1, cache_v1, cache_k2, cache_v2,
           cache_k3, cache_v3, rel_bias, norm1_g, norm2_g, w_ada, b_ada, w_in, conv_w, q_norm_g, k_norm_g,
           w_conv_out, w_attn_out, w_o, w_mlp_in, w_mlp_out):
    f32 = np.float32
    A = lambda a: np.ascontiguousarray(np.asarray(a, dtype=f32))
    x_prompt, x_sample = A(x_prompt), A(x_sample)
    c_prompt, c_sample, state_conv = A(c_prompt), A(c_sample), A(state_conv)
    caches_k = [A(cache_k1)[0], A(cache_k2)[0], A(cache_k3)[0]]
    caches_v = [A(cache_v1)[0], A(cache_v2)[0], A(cache_v3)[0]]
    if "nc" not in _NC_CACHE:
        _NC_CACHE["nc"] = build_program()
    nc = _NC_CACHE["nc"]
    onehot, sel = _consts()

    def fm(v, n):
        return np.ascontiguousarray(A(v).reshape(n, 128).T)

    shared = {
        "b_adaT": fm(b_ada[0], 48), "n1g": fm(norm1_g[0], 8), "n2g": fm(norm2_g[0], 8),
        "convw": np.ascontiguousarray(A(conv_w[0]).reshape(3, 8, 128).transpose(2, 1, 0)),
        "qkg": np.ascontiguousarray(np.stack([np.tile(A(q_norm_g[0]), 2), np.tile(A(k_norm_g[0]), 2)], axis=1)),
        "onehot": onehot, "rel_bias": A(rel_bias), "ident": np.eye(128, dtype=f32), "sel": sel,
        "w_ada": A(w_ada[0]), "w_in": A(w_in[0]), "w_conv_out": A(w_conv_out[0]), "w_attn_out": A(w_attn_out[0]),
        "w_o": A(w_o[0]), "w_mlp_in": A(w_mlp_in[0]), "w_mlp_out": A(w_mlp_out[0]),
    }
    in_maps = []
    for core in range(NCORE):
        b, half = divmod(core, 2)
        m = dict(shared)
        xt = np.zeros((D, NH + NTOK), f32)
        xt[:, NH:] = x_prompt[b, half * NTOK:(half + 1) * NTOK].T
        if half == 1:
            xt[:, :NH] = x_prompt[b, NTOK - NH:NTOK].T
        m["xT"] = xt
        sl = slice(core * SB_, (core + 1) * SB_)
        m["xsT"] = np.ascontiguousarray(x_sample[sl].reshape(NS, D).T)
        crow = np.concatenate([c_prompt[b:b + 1], c_sample[sl]], axis=0)
        m["cT"] = np.ascontiguousarray(crow.reshape(17, 8, 128).transpose(2, 1, 0))
        m["stT"] = np.ascontiguousarray(state_conv[0, sl].reshape(SB_, 2, 8, 128).transpose(3, 2, 0, 1))
        m["flagv"] = np.full((128, 1), float(half), f32)
        vo = np.ones((128, 5, 64), f32)
        for ti in range(4):
            vo[:128 - 32 * ti, ti, :] = float(half)
        m["vones"] = vo
        for g in range(3):
            m["ck%d" % g] = np.ascontiguousarray(caches_k[g][sl].reshape(SB_, WIN[g], 512))
            m["cv%d" % g] = np.ascontiguousarray(caches_v[g][sl].reshape(SB_, WIN[g], 512))
            kk = m["ck%d" % g]
            d = DIL[g]
            nt = 1 if g == 0 else 4
            kr = kk.reshape(SB_, 128, d, 512)[:, :, 0:nt, :]
            m["kc%dT" % g] = np.ascontiguousarray(kr.reshape(SB_, 128, nt, 4, 128).transpose(0, 4, 3, 2, 1))
        in_maps.append(m)

    res = run_bass_kernel_spmd(nc, in_maps, core_ids=list(range(NCORE)))
    R_ = res.results
    yp = np.empty((4, 8192, D), f32)
    ys = np.empty((128, 4, D), f32)
    p_conv = np.empty((1, 4, 2, D), f32)
    pk = [np.empty((1, 4, WIN[g], 8, 64), f32) for g in range(3)]
    pv = [np.empty((1, 4, WIN[g], 8, 64), f32) for g in range(3)]
    s_conv = np.empty((1, 128, 2, D), f32)
    skk = [np.empty((1, 128, WIN[g], 8, 64), f32) for g in range(3)]
    svv = [np.empty((1, 128, WIN[g], 8, 64), f32) for g in range(3)]
    for core in range(NCORE):
        b, half = divmod(core, 2)
        r = R_[core]
        yp[b, half * NTOK:(half + 1) * NTOK] = r["yT"].T
        sl = slice(core * SB_, (core + 1) * SB_)
        ys[sl] = r["ysT"].T.reshape(SB_, 4, D)
        s_conv[0, sl] = r["sconvT"].transpose(2, 3, 1, 0).reshape(SB_, 2, D)
        for g in range(3):
            skk[g][0, sl] = r["sk%d" % g].reshape(SB_, WIN[g], 8, 64)
            svv[g][0, sl] = r["sv%d" % g].reshape(SB_, WIN[g], 8, 64)
        if half == 1:
            p_conv[0, b] = r["pconvT"].transpose(2, 1, 0).reshape(2, D)
            for g in range(3):
                pk[g][0, b] = r["pk%dT" % g].transpose(2, 1, 0).reshape(WIN[g], 8, 64)
                pv[g][0, b] = r["pv%dT" % g].transpose(2, 1, 0).reshape(WIN[g], 8, 64)
    return (yp, ys, p_conv, pk[0], pv[0], pk[1], pv[1], pk[2], pv[2],
            s_conv, skk[0], svv[0], skk[1], svv[1], skk[2], svv[2])
```

```python
import math
from contextlib import ExitStack
import numpy as np
import concourse.bass as bass
import concourse.mybir as mybir
from concourse.ap import AP
from concourse.bass_utils import run_bass_kernel_spmd

F32, BF16 = mybir.dt.float32, mybir.dt.bfloat16
AF = mybir.ActivationFunctionType
ALU = mybir.AluOpType

NCORE = 8
D = 1024
T = 512
NH = 2048
NTOK = 4096
NT = NTOK // T
NHT = NH // T
SB_ = 16
NS = 64
DPROJ = 9728
EPS = 1e-6
WIN = (128, 512, 2048)
DIL = (1, 4, 16)


class Res:
    __slots__ = ("name", "w", "r", "const")

    def __init__(self, name, const=False):
        self.name, self.w, self.r, self.const = name, None, [], const


class Op:
    __slots__ = ("eng", "fn", "deps", "flag", "rank", "dma", "sem", "semval")


ENGS = ["pe", "act", "dve", "pool", "sp"]


class Sched:
    def __init__(self):
        self.q = {e: [] for e in ENGS}
        self.strict = False

    def op(self, eng, fn, reads=(), writes=(), dma=False):
        o = Op()
        o.eng, o.fn, o.dma, o.flag, o.deps, o.rank, o.sem, o.semval = eng, fn, dma, False, [], 0, None, 0
        dl = []
        for R in reads:
            if R.w is not None:
                dl.append((R.w, True))
        for R in writes:
            if R.w is not None:
                dl.append((R.w, False))
            dl.extend((x, False) for x in R.r)
        for d, raw in dl:
            if d is o:
                continue
            if d.eng == eng and not d.dma and not dma:
                if not (self.strict and raw and eng != "pe"):
                    continue
            if d not in o.deps:
                o.deps.append(d)
                d.flag = True
        for R in reads:
            if not R.const:
                R.r.append(o)
        for R in writes:
            R.w = o
            R.r = []
        self.q[eng].append(o)
        return o

    def emit(self, nc, es, block):
        NDS = 12
        csem = {e: es.enter_context(nc.semaphore("c_" + e)) for e in ENGS}
        dsem = {e: [es.enter_context(nc.semaphore("d_%s%d" % (e, i))) for i in range(NDS)] for e in ENGS}
        for e in ENGS:
            k = 0
            kd = 0
            for o in self.q[e]:
                if o.dma:
                    o.flag = True
                    o.sem = dsem[e][kd % NDS]
                    o.semval = 16 * (kd // NDS + 1)
                    kd += 1
                elif o.flag:
                    k += 1
                    o.rank = k

        def run(e, eng):
            waited = {}
            for o in self.q[e]:
                for d in o.deps:
                    if d.dma:
                        key, sem, val = id(d.sem), d.sem, d.semval
                    else:
                        key, sem, val = d.eng, csem[d.eng], d.rank
                    if waited.get(key, 0) >= val:
                        continue
                    waited[key] = val
                    eng.wait_ge(sem, val)
                if o.fn is None:
                    continue
                ins = o.fn(eng)
                if o.flag:
                    if o.dma:
                        ins.then_inc(o.sem, 16)
                    else:
                        ins.then_inc(csem[e], 1)

        block.tensor(lambda eng: run("pe", eng))
        block.scalar(lambda eng: run("act", eng))
        block.vector(lambda eng: run("dve", eng))
        block.gpsimd(lambda eng: run("pool", eng))
        block.sync(lambda eng: run("sp", eng))


def t5_bucket(dist):
    dist = np.asarray(dist, np.int64)
    max_exact = 16
    ratio = np.maximum(dist, 1).astype(np.float32) / np.float32(max_exact)
    large = max_exact + (np.log(ratio) / np.float32(math.log(2048 / max_exact)) * np.float32(32 - max_exact)).astype(np.int32)
    large = np.minimum(large, 31)
    return np.where(dist < max_exact, dist, large)


def build_program():
    import os
    STAGE = int(os.environ.get("KSTAGE", "9"))
    NOPROMPT = int(os.environ.get("KNOPROMPT", "0"))
    out_ops = []
    nc = bass.Bass("TRN2", target_bir_lowering=False)
    S = Sched()
    es = ExitStack()

    def din(name, shape):
        return nc.dram_tensor(name, list(shape), F32, kind="ExternalInput").ap()

    def dout(name, shape):
        return nc.dram_tensor(name, list(shape), F32, kind="ExternalOutput").ap()

    def dtmp(name, shape, dt):
        return nc.dram_tensor(name, list(shape), dt, kind="Internal").ap()

    def sb(name, shape, dt):
        return es.enter_context(nc.sbuf_tensor(name, list(shape), dt))

    def psum(name, shape, dt=F32):
        return es.enter_context(nc.psum_tensor(name, list(shape), dt))

    xT = din("xT", [D, NH + NTOK])
    xsT = din("xsT", [D, NS])
    cT = din("cT", [128, 8, 17])
    stT = din("stT", [128, 8, SB_, 2])
    b_adaT = din("b_adaT", [128, 48])
    n1g_d = din("n1g", [128, 8])
    n2g_d = din("n2g", [128, 8])
    convw_d = din("convw", [128, 8, 3])
    qkg_d = din("qkg", [128, 2])
    flag_d = din("flagv", [128, 1])
    vones_d = din("vones", [128, 5, 64])
    onehot_d = din("onehot", [32, 387])
    relb_d = din("rel_bias", [32, 24])
    ident_d = din("ident", [128, 128])
    sel_d = din("sel", [24, 12, 128])
    w_ada = din("w_ada", [D, 6144])
    w_in = din("w_in", [D, DPROJ])
    w_co = din("w_conv_out", [D, D])
    w_ao = din("w_attn_out", [512, D])
    w_o = din("w_o", [D, D])
    w_mi = din("w_mlp_in", [D, 4096])
    w_mo = din("w_mlp_out", [4096, D])
    ck = [din("ck%d" % g, [SB_, WIN[g], 512]) for g in range(3)]
    cv = [din("cv%d" % g, [SB_, WIN[g], 512]) for g in range(3)]
    kcT = [din("kc0T", [SB_, 128, 4, 1, 128]), din("kc1T", [SB_, 128, 4, 4, 128]), din("kc2T", [SB_, 128, 4, 4, 128])]

    yT = dout("yT", [D, NTOK])
    ysT = dout("ysT", [D, NS])
    pconvT = dout("pconvT", [128, 8, 2])
    sconvT = dout("sconvT", [128, 8, SB_, 2])
    pkT = [dout("pk%dT" % g, [512, DIL[g], 128]) for g in range(3)]
    pvT = [dout("pv%dT" % g, [512, DIL[g], 128]) for g in range(3)]
    sk = [dout("sk%d" % g, [SB_, WIN[g], 512]) for g in range(3)]
    sv = [dout("sv%d" % g, [SB_, WIN[g], 512]) for g in range(3)]

    DBG = int(os.environ.get("KDBG", "0"))
    dbg = dout("dbg", [128, 32, 64]) if DBG else None
    dbg_n = [0]

    def dump(ap, res, name, cast=False):
        if not DBG:
            return
        k = dbg_n[0]
        dbg_n[0] += 1
        n = ap.shape[1]
        print("DBGSLOT", k, name, n)
        out_ops.append(S.op("pool" if cast else "sp", lambda e: e.dma_start(out=dbg[0:ap.shape[0], k, 0:n], in_=ap), reads=list(res), dma=True))

    wb_in = dtmp("wb_in", [19, 128, 8 * 512], BF16)
    wb_co = dtmp("wb_co", [2, 128, 8 * 512], BF16)
    wb_ao = dtmp("wb_ao", [1, 128, 4 * 1024], BF16)
    wb_o = dtmp("wb_o", [2, 128, 8 * 512], BF16)
    wb_mi = dtmp("wb_mi", [8, 128, 8 * 512], BF16)
    wb_mo = dtmp("wb_mo", [8, 128, 32 * 128], BF16)
    ebz = dtmp("ebz", [24, 128 * 512], F32)

    identb = sb("identb", [128, 128], BF16)
    onesb = sb("onesb", [128, 128], BF16)
    blk64 = sb("blk64", [128, 128], BF16)
    vones = sb("vones_s", [128, 5, 64], BF16)
    flagv = sb("flag_s", [128, 1], F32)
    epsv = sb("epsv", [128, 2], F32)
    Eall = sb("Eall", [128, 24, 256], BF16)
    modT = sb("modT", [128, 48, 17], F32)
    A1 = sb("A1", [128, 8, 17], F32)
    A2 = sb("A2", [128, 8, 17], F32)
    n1g = sb("n1g_s", [128, 8], F32)
    n2g = sb("n2g_s", [128, 8], F32)
    convw = sb("convw_s", [128, 8, 3], F32)
    qkg = sb("qkg_s", [128, 2], F32)
    badat = sb("badat", [128, 48], F32)
    ebn = sb("ebn", [128, 12, 4], F32)
    ucarry = sb("ucarry", [128, 8, 2], F32)
    KTW = (640, 1024, 2560)
    KT = [sb("KT%d" % g, [128, 4, KTW[g]], BF16) for g in range(3)]
    VT = [sb("VT%d" % g, [128, 4, KTW[g]], BF16) for g in range(3)]
    xTs = sb("xTs", [128, 8, T], F32)
    xnT = sb("xnT", [128, 8, T], BF16)
    sqb = [sb("sqb%d" % i, [128, T], BF16) for i in range(2)]
    rstd = sb("rstd", [128, T], F32)
    tmpf = [sb("tmpf%d" % i, [128, T], F32) for i in range(2)]
    A1a = sb("arenaA1", [128, 12 * T], BF16)
    A2a = sb("arenaA2", [128, 8 * T], BF16)
    A3a = sb("arenaA3", [128, 4 * T], F32)
    A4a = sb("arenaA4", [128, 4 * T], BF16)
    ue = [sb("ue0", [128, T + 2], F32)]
    mixT = sb("mixT", [128, 8, T], BF16)
    es_ = [sb("es%d" % i, [128, 512], F32) for i in range(2)]
    pt_ = [sb("pt%d" % i, [128, 512], BF16) for i in range(2)]
    vt_ = [sb("vt%d" % i, [128, 128], BF16) for i in range(4)]
    AS = [sb("AS0", [128, 2, T], F32)]
    oT = A2a[:, 0:4 * T].rearrange("p (c t) -> p c t", c=4)
    NWB = 3
    wring = [sb("wring%d" % i, [128, 4096], BF16) for i in range(NWB)]
    A1f = A1a[:].bitcast(F32)
    A2f = A2a[:].bitcast(F32)
    selS = A1f[0:24, 0:1536].rearrange("p (a b) -> p a b", a=12)
    oneh = A1f[0:32, 1536:1923]
    ebs = A1f[0:24, 1924:2311]
    zt = A1f[0:24, 2312:2824]
    cTs = A2f[:, 0:136].rearrange("p (a b) -> p a b", a=8)
    relb = A2f[0:32, 136:160]
    csil = A2a[:, 400:536].rearrange("p (a b) -> p a b", a=8)
    ues = sb("ues", [128, SB_, 6], F32)
    Es = sb("Es", [128, 96], BF16)
    KTn = sb("KTn", [128, 3, 4, NS], BF16)
    VTn = sb("VTn", [128, 3, 4, NS], BF16)
    pn = sb("pn", [128, NS], F32)
    prodb = sb("prodb", [128, NS], BF16)
    ASf = AS[0][:].rearrange("p a t -> p (a t)")
    accn = ASf[:, 0:256].rearrange("p (c n) -> p c n", c=4)
    sn = ASf[:, 256:512].rearrange("p (c n) -> p c n", c=4)
    accb = ASf[:, 512:768].rearrange("p (c n) -> p c n", c=4)
    sbb = ASf[:, 768:1024].rearrange("p (c n) -> p c n", c=4)
    QbdA = KT[1][:].rearrange("p c w -> p (c w)")
    QbdB = VT[1][:].rearrange("p c w -> p (c w)")

    def Qbd_ap(g):
        base = QbdA[:, g * 2048:(g + 1) * 2048] if g < 2 else QbdB[:, 0:2048]
        return base.rearrange("p (c n h) -> p c n h", c=4, h=8)

    big = psum("bigps", [128, 8, 512])
    mmps = [big[:, i, :] for i in range(3)]
    nsps = big[:, 3, :]
    stp = big[:, 4:6, :]
    stps = [stp[:, 0, :], stp[:, 0, :]]
    pvps = big[:, 6, :]
    trps = big[:, 7, :].bitcast(BF16)
    st_sets = [big[:, 4:6, :], big[:, 0:2, :]]
    pv_sets = [big[:, 6, :], big[:, 2, :]]
    tr_sets = [big[:, 7, :].bitcast(BF16), big[:, 3, :].bitcast(BF16)]

    def R(n, const=False):
        return Res(n, const)

    r_const = R("const", True)
    r_mm = [R("mm%d" % i) for i in range(3)]
    r_ns = R("ns")
    r_st0 = R("st")
    r_st = [r_st0, r_st0]
    r_pv = R("pv")
    r_tr0 = R("tr")
    r_tr = [r_tr0, r_tr0]
    r_w = [R("w%d" % i) for i in range(NWB)]
    r_x = [R("x%d" % c) for c in range(8)]
    r_xn = [R("xn%d" % c) for c in range(8)]
    r_sq = [R("sq0"), R("sq1")]
    r_rstd = R("rstd")
    r_tmpf = [R("tf0"), R("tf1")]
    r_q = [[R("q%d_%d" % (g, c)) for c in range(4)] for g in range(3)]
    r_kt = [[R("kt%d_%d" % (g, c)) for c in range(4)] for g in range(3)]
    r_vt = [[R("vt%d_%d" % (g, c)) for c in range(4)] for g in range(3)]
    r_sg = [R("sg%d" % c) for c in range(8)]
    r_by = [R("by%d" % c) for c in range(8)]
    r_h = [R("h%d" % c) for c in range(4)]
    r_hid = [R("hid%d" % c) for c in range(32)]
    r_ue = [R("ue0")]
    r_mix = [R("mix%d" % c) for c in range(8)]
    r_es = [R("es0"), R("es1")]
    r_pt = [R("pt0"), R("pt1")]
    r_vtile = [R("vtile%d" % i) for i in range(4)]
    r_as = [R("as0")]
    r_o = [R("o%d" % c) for c in range(4)]
    r_uc = R("ucarry")
    r_mod = R("mod")
    r_E = R("E")
    r_misc = R("misc")
    r_wbt = {}
    r_ebz = R("ebz")
    r_samp = R("samp")
    r_qbd = R("qbd")
    r_init = R("init")
    cnt = {"mm": 0, "st": 0, "w": 0, "sq": 0, "tf": 0, "es": 0, "vt": 0, "tr": 0, "as": 0, "ue": 0, "bal": 0}

    def nxt(key, n):
        v = cnt[key] % n
        cnt[key] += 1
        return v

    def qT_ap(g, c):
        return A1a[:, (g * 4 + c) * T:(g * 4 + c + 1) * T]

    def sgatt_ap(c):
        return A1a[:, c * T:(c + 1) * T]

    def by_ap(c):
        return A2a[:, c * T:(c + 1) * T]

    def h_ap(c):
        return A3a[:, c * T:(c + 1) * T]

    def hid_ap(k):
        if k < 12:
            return A1a[:, k * T:(k + 1) * T]
        if k < 20:
            return A2a[:, (k - 12) * T:(k - 11) * T]
        if k < 28:
            kk = k - 20
            return A3a[:, (kk // 2) * T:(kk // 2 + 1) * T].bitcast(BF16)[:, (kk % 2) * T:(kk % 2 + 1) * T]
        return A4a[:, (k - 28) * T:(k - 27) * T]

    def hid_alias(k):
        if k < 12:
            g, c = divmod(k, 4)
            return [r_q[g][c]] + ([r_sg[k]] if k < 8 else [])
        if k < 20:
            return [r_by[k - 12]] + ([r_o[k - 12]] if k < 16 else [])
        if k < 28:
            return [r_h[(k - 20) // 2]]
        return []

    def inherit(newR, oldRs):
        for o in oldRs:
            if o.w is not None:
                newR.r.append(o.w)
            newR.r.extend(o.r)

    wsrc = {"in": (w_in, wb_in, 8, 512), "co": (w_co, wb_co, 8, 512), "ao": (w_ao, wb_ao, 4, 1024), "o": (w_o, wb_o, 8, 512),
            "mi": (w_mi, wb_mi, 8, 512), "mo": (w_mo, wb_mo, 32, 128)}

    def cast_tile(kind, idx):
        src_t, dst_t, nc_, nf = wsrc[kind]
        src = src_t[:, idx * nf:(idx + 1) * nf].rearrange("(c p) f -> p c f", p=128)
        dst = dst_t[idx].rearrange("p (c f) -> p c f", c=nc_)
        rr = Res("wb_%s%d" % (kind, idx))
        r_wbt[(kind, idx)] = rr
        S.op("pool", lambda e: e.dma_start(out=dst, in_=src), writes=[rr], dma=True)

    def ld(eng, dst, src, wres, rres=()):
        return S.op(eng, lambda e: e.dma_start(out=dst, in_=src), reads=list(rres), writes=list(wres), dma=True)

    ld("sp", n1g[:], n1g_d[:, :], [r_misc])
    ld("sp", n2g[:], n2g_d[:, :], [r_misc])
    ld("sp", convw[:], convw_d[:, :, :], [r_misc])
    ld("sp", qkg[:], qkg_d[:, :], [r_misc])
    ld("sp", flagv[:], flag_d[:, :], [r_misc])
    ld("sp", badat[:], b_adaT[:, :], [r_misc])
    ld("pool", identb[:], ident_d[:, :], [r_misc])
    ld("pool", vones[:], vones_d[:, :, :], [r_misc])
    S.op("pool", lambda e: e.memset(onesb[:], 1.0), writes=[r_misc])
    S.op("pool", lambda e: e.memset(epsv[:, 0:1], float(D * EPS)), writes=[r_misc])
    S.op("pool", lambda e: e.memset(epsv[:, 1:2], float(64 * EPS)), writes=[r_misc])
    S.op("pool", lambda e: e.memset(blk64[:], 0.0), writes=[r_misc])
    S.op("pool", lambda e: e.memset(blk64[0:64, 0:64], 1.0), writes=[r_misc])
    S.op("pool", lambda e: e.memset(blk64[64:128, 64:128], 1.0), writes=[r_misc])
    S.op("pool", lambda e: e.memset(zt[:], 0.0), writes=[r_misc])
    S.op("pool", lambda e: e.memset(ucarry[:], 0.0), writes=[r_uc])
    S.op("dve", lambda e: e.tensor_scalar(out=qkg[:], in0=qkg[:], scalar1=8.0, scalar2=None, op0=ALU.mult),
         reads=[r_misc], writes=[r_misc])

    for blk in (11, 14, 10, 13, 9, 12, 0, 4, 1, 5):
        cast_tile("in", blk)

    ld("sp", relb, relb_d[:, :], [r_misc])
    ld("sp", oneh[:], onehot_d[:, :], [r_misc])
    ld("sp", selS[:], sel_d[:, :, :], [r_misc])
    S.op("pe", lambda e: e.matmul(mmps[0][0:24, 0:387], lhsT=relb, rhs=oneh[:], start=True, stop=True),
         reads=[r_misc], writes=[r_mm[0]])
    S.op("act", lambda e: e.activation(out=ebs[:], in_=mmps[0][0:24, 0:387], func=AF.Exp),
         reads=[r_mm[0]], writes=[r_E])
    for g in range(3):
        S.op("sp", lambda e, g=g: e.dma_start(out=zt[8 * g:8 * g + 8, 128:257],
                                             in_=ebs[8 * g:8 * g + 8, 129 * g:129 * g + 129]),
             reads=[r_E, r_misc], writes=[r_init], dma=True)
    S.op("sp", lambda e: e.dma_start(out=ebz[:, :].rearrange("a (j c) -> a j c", c=512),
                                     in_=zt.unsqueeze(1).to_broadcast([24, 128, 512])),
         reads=[r_init], writes=[r_ebz], dma=True)
    for g in range(3):
        src = AP(ebz.tensor, 8 * g * 65536 + 128, [[511, 128], [65536, 8], [1, 256]])
        S.op("pool", lambda e, g=g, src=src: e.dma_start(out=Eall[:, 8 * g:8 * g + 8, :], in_=src),
             reads=[r_ebz], writes=[r_E], dma=True)
    for g in range(3):
        for c in range(4):
            S.op("pe", lambda e, g=g, c=c: e.matmul(nsps[:, (g * 4 + c) * 4:(g * 4 + c) * 4 + 4],
                                                     lhsT=selS[:, g * 4 + c, :], rhs=ebs[:, 129 * g:129 * g + 4],
                                                     start=True, stop=True),
                 reads=[r_E, r_misc], writes=[r_ns])
    S.op("act", lambda e: e.activation(out=ebn[:].rearrange("p a b -> p (a b)"), in_=nsps[:, 0:48], func=AF.Copy),
         reads=[r_ns], writes=[r_samp])

    ld("sp", cTs[:], cT[:, :, :], [r_misc])
    S.op("act", lambda e: e.activation(out=csil[:], in_=cTs[:], func=AF.Silu), reads=[r_misc], writes=[r_mod])
    for blk in range(12):
        slot = nxt("w", NWB)
        wt = wring[slot][:].rearrange("p (c f) -> p c f", c=8)
        src = w_ada[:, blk * 512:(blk + 1) * 512].rearrange("(c p) f -> p c f", p=128)
        S.op("pool", lambda e, wt=wt, src=src: e.dma_start(out=wt, in_=src), writes=[r_w[slot]], dma=True)
        for j in range(4):
            k = blk * 4 + j
            b = nxt("mm", 3)

            def f(e, wt=wt, j=j, b=b):
                for c in range(8):
                    ins = e.matmul(mmps[b][:, 0:17], lhsT=wt[:, c, j * 128:(j + 1) * 128], rhs=csil[:, c, :],
                                   start=(c == 0), stop=(c == 7))
                return ins
            S.op("pe", f, reads=[r_w[slot], r_mod], writes=[r_mm[b]])
            S.op("act", lambda e, k=k, b=b: e.activation(out=modT[:, k, :], in_=mmps[b][:, 0:17], func=AF.Identity,
                                                        bias=badat[:, k:k + 1], scale=1.0),
                 reads=[r_mm[b], r_misc], writes=[r_mod])
    for (Ax, gx, off) in ((A1, n1g, 8), (A2, n2g, 32)):
        S.op("dve", lambda e, Ax=Ax, off=off: e.tensor_scalar(out=Ax[:], in0=modT[:, off:off + 8, :], scalar1=1.0,
                                                             scalar2=32.0, op0=ALU.add, op1=ALU.mult),
             reads=[r_mod], writes=[r_mod])
        S.op("dve", lambda e, Ax=Ax, gx=gx: e.tensor_tensor(out=Ax[:], in0=Ax[:],
                                                           in1=gx[:].unsqueeze(2).to_broadcast([128, 8, 17]),
                                                           op=ALU.mult),
             reads=[r_mod, r_misc], writes=[r_mod])
    for blk in (15, 16, 2, 3):
        cast_tile("in", blk)
    for i in range(2):
        cast_tile("co", i)
    for blk in (6, 7, 8, 17, 18):
        cast_tile("in", blk)
    cast_tile("ao", 0)
    for i in range(2):
        cast_tile("o", i)
    for i in range(8):
        cast_tile("mi", i)
    for i in range(8):
        cast_tile("mo", i)

    copy_jobs = []
    for g in (2, 1, 0):
        for b in range(SB_):
            for (dst, src) in ((sk[g], ck[g]), (sv[g], cv[g])):
                copy_jobs.append((dst[b, 0:WIN[g] - 4, :], src[b, 4:WIN[g], :]))

    def issue_copies(n):
        for _ in range(n):
            if copy_jobs:
                d_, s_ = copy_jobs.pop(0)
                out_ops.append(S.op("sp", lambda e, d_=d_, s_=s_: e.dma_start(out=d_, in_=s_), dma=True))

    def wload(kind, idx):
        slot = nxt("w", NWB)
        src_t, dst_t, nc_, nf = wsrc[kind]
        src = dst_t[idx]
        view = wring[slot][:].rearrange("p (c f) -> p c f", c=nc_)
        flat = wring[slot][:]
        S.op("sp", lambda e, flat=flat, src=src: e.dma_start(out=flat, in_=src),
             reads=[r_wbt[(kind, idx)]], writes=[r_w[slot]], dma=True)
        if STAGE >= 5:
            issue_copies(1 if cnt["w"] % 4 == 0 else 0)
        return view, r_w[slot]

    def mm_group(wview, j, nk, rhs_fn, rhs_res, wres, N):
        b = nxt("mm", 3)

        def f(e):
            for c in range(nk):
                ins = e.matmul(mmps[b][:, 0:N], lhsT=wview[:, c, j * 128:(j + 1) * 128], rhs=rhs_fn(c),
                               start=(c == 0), stop=(c == nk - 1))
            return ins
        S.op("pe", f, reads=[wres] + list(rhs_res), writes=[r_mm[b]])
        return b

    def v3(ap):
        return ap.rearrange("p (b t) -> p b t", t=4)

    def mb(tile, k):
        return tile[:, k, 1:17].unsqueeze(2).to_broadcast([128, SB_, 4])

    def norm_stage(N, A_sc, shift_sc, sample, Atok=None, shtok=None):
        for c in range(8):
            i = nxt("sq", 2)
            S.op("act", lambda e, c=c, i=i: e.activation(out=sqb[i][:, 0:N], in_=xTs[:, c, 0:N], func=AF.Square),
                 reads=[r_x[c]], writes=[r_sq[i]])
            S.op("pe", lambda e, c=c, i=i: e.matmul(nsps[:, 0:N], lhsT=onesb[:], rhs=sqb[i][:, 0:N],
                                                     start=(c == 0), stop=(c == 7)),
                 reads=[r_sq[i], r_misc], writes=[r_ns] if c == 0 else [r_ns])
        S.op("act", lambda e: e.activation(out=rstd[:, 0:N], in_=nsps[:, 0:N], func=AF.Ln, bias=epsv[:, 0:1], scale=1.0),
             reads=[r_ns, r_misc], writes=[r_rstd])
        S.op("act", lambda e: e.activation(out=rstd[:, 0:N], in_=rstd[:, 0:N], func=AF.Exp, scale=-0.5), reads=[r_rstd], writes=[r_rstd])
        for c in range(8):
            i = nxt("tf", 2)
            if not sample:
                S.op("dve", lambda e, c=c, i=i: e.scalar_tensor_tensor(out=tmpf[i][:, 0:N], in0=xTs[:, c, 0:N],
                                                                       scalar=A_sc[:, c, 0:1], in1=rstd[:, 0:N],
                                                                       op0=ALU.mult, op1=ALU.mult),
                     reads=[r_x[c], r_rstd, r_mod], writes=[r_tmpf[i]])
                S.op("act", lambda e, c=c, i=i: e.activation(out=xnT[:, c, 0:N], in_=tmpf[i][:, 0:N], func=AF.Identity,
                                                             bias=shift_sc[:, c, 0:1], scale=1.0),
                     reads=[r_tmpf[i], r_mod], writes=[r_xn[c]])
            else:
                S.op("dve", lambda e, c=c, i=i: e.tensor_tensor(out=tmpf[i][:, 0:N], in0=xTs[:, c, 0:N],
                                                                in1=rstd[:, 0:N], op=ALU.mult),
                     reads=[r_x[c], r_rstd], writes=[r_tmpf[i]])
                S.op("dve", lambda e, c=c, i=i: e.tensor_tensor(out=v3(tmpf[i][:, 0:N]), in0=v3(tmpf[i][:, 0:N]),
                                                                in1=mb(Atok, c), op=ALU.mult),
                     reads=[r_mod], writes=[r_tmpf[i]])
                S.op("dve", lambda e, c=c, i=i: e.tensor_tensor(out=v3(xnT[:, c, 0:N]), in0=v3(tmpf[i][:, 0:N]),
                                                                in1=mb(modT, shtok + c), op=ALU.add),
                     reads=[r_tmpf[i], r_mod], writes=[r_xn[c]])

    halo_off = [None]

    def kv_dst(buf, g, c, N, sample):
        if sample:
            return (KTn if buf is KT else VTn)[:, g, c, :], None
        if g == 0:
            return buf[0][:, c, 128:128 + T], None
        d = DIL[g]
        L = T // d
        hist = 128
        if halo_off[0] is not None:
            hist = halo_off[0] * L if g == 2 else 0
        v = buf[g][:, c, :].rearrange("p (r l) -> p r l", r=d)[:, :, hist:hist + L]
        return v.rearrange("p r l -> p l r"), (L, d)

    def src_view(ap2d, perm):
        if perm is None:
            return ap2d
        L, d = perm
        return ap2d.rearrange("p (l r) -> p l r", r=d)

    pending = []

    def flush_pending():
        while pending:
            pending.pop(0)()

    def qk_block(wv, wres, g, which, N, sample, qdst=None):
        for j in range(4):
            b = mm_group(wv, j, 8, lambda c: xnT[:, c, 0:N], r_xn, wres, N)
            flush_pending()
            i = nxt("sq", 2)
            S.op("act", lambda e, b=b, i=i: e.activation(out=sqb[i][:, 0:N], in_=mmps[b][:, 0:N], func=AF.Square),
                 reads=[r_mm[b]], writes=[r_sq[i]])
            if which == 0:
                if sample:
                    dst, perm = qT_ap(g, j)[:, 0:N], None
                else:
                    if g == 0:
                        dst, perm = qT_ap(g, j), None
                    else:
                        d = DIL[g]
                        dst = qT_ap(g, j).rearrange("p (r l) -> p l r", r=d)
                        perm = (T // d, d)
                wr = [r_q[g][j]]
            else:
                dst, perm = kv_dst(KT, g, j, N, sample)
                wr = [r_kt[g][j]]

            def fin(b=b, i=i, dst=dst, perm=perm, wr=wr):
                S.op("pe", lambda e: e.matmul(nsps[:, 0:N], lhsT=blk64[:], rhs=sqb[i][:, 0:N], start=True, stop=True),
                     reads=[r_sq[i], r_misc], writes=[r_ns])
                S.op("act", lambda e: e.activation(out=rstd[:, 0:N], in_=nsps[:, 0:N], func=AF.Ln, bias=epsv[:, 1:2], scale=1.0),
                     reads=[r_ns, r_misc], writes=[r_rstd])
                S.op("act", lambda e: e.activation(out=rstd[:, 0:N], in_=rstd[:, 0:N], func=AF.Exp, scale=-0.5),
                     reads=[r_rstd], writes=[r_rstd])
                S.op("dve", lambda e: e.scalar_tensor_tensor(
                    out=dst, in0=src_view(mmps[b][:, 0:N], perm), scalar=qkg[:, which:which + 1],
                    in1=src_view(rstd[:, 0:N], perm), op0=ALU.mult, op1=ALU.mult),
                    reads=[r_mm[b], r_rstd, r_misc], writes=wr)
            pending.append(fin)

    def v_block(wv, wres, g, N, sample, halo):
        for j in range(4):
            b = mm_group(wv, j, 8, lambda c: xnT[:, c, 0:N], r_xn, wres, N)
            flush_pending()
            dst, perm = kv_dst(VT, g, j, N, sample)
            if halo:
                S.op("act", lambda e, b=b, dst=dst, perm=perm: e.activation(out=dst, in_=src_view(mmps[b][:, 0:N], perm),
                                                                            func=AF.Identity, scale=flagv[:, 0:1]),
                     reads=[r_mm[b], r_misc], writes=[r_vt[g][j]])
            else:
                S.op("act", lambda e, b=b, dst=dst, perm=perm: e.activation(out=dst, in_=src_view(mmps[b][:, 0:N], perm),
                                                                            func=AF.Copy),
                     reads=[r_mm[b]], writes=[r_vt[g][j]])

    def shift_hist(groups=(0, 1, 2)):
        for g in groups:
            for buf, rr in ((KT, r_kt), (VT, r_vt)):
                for c in range(4):
                    if g == 0:
                        S.op("pool", lambda e, buf=buf, c=c: e.tensor_copy(out=buf[0][:, c, 0:128], in_=buf[0][:, c, T:T + 128]),
                             reads=[rr[0][c]], writes=[rr[0][c]])
                    else:
                        d = DIL[g]
                        L = T // d
                        v = buf[g][:, c, :].rearrange("p (r l) -> p r l", r=d)
                        S.op("pool", lambda e, v=v, L=L: e.tensor_copy(out=v[:, :, 0:128], in_=v[:, :, L:L + 128]),
                             reads=[rr[g][c]], writes=[rr[g][c]])

    def load_x(src_ap, N):
        for c in range(8):
            S.op("act", lambda e, c=c: e.dma_start(out=xTs[:, c, 0:N], in_=src_ap[c * 128:(c + 1) * 128, :]),
                 writes=[r_x[c]], dma=True)

    def conv_branch(N, sample, halo_only=False):
        for half in range(2):
            wv_h, wr_h = wload("in", 0 + half)
            hb = []
            for j in range(4):
                if halo_only:
                    b = mm_group(wv_h, j, 8, lambda c: xnT[:, c, T - 2:T], r_xn, wr_h, 2)
                    S.op("act", lambda e, b=b, j=j: e.activation(out=h_ap(j)[:, 0:2], in_=mmps[b][:, 0:2], func=AF.Copy),
                         reads=[r_mm[b]], writes=[r_h[j]])
                else:
                    b = mm_group(wv_h, j, 8, lambda c: xnT[:, c, 0:N], r_xn, wr_h, N)
                    S.op("act", lambda e, b=b, j=j: e.activation(out=h_ap(j)[:, 0:N], in_=mmps[b][:, 0:N], func=AF.Copy),
                         reads=[r_mm[b]], writes=[r_h[j]])
            wv_c, wr_c = wload("in", 4 + half)
            for j in range(4):
                cc = half * 4 + j
                if halo_only:
                    b = mm_group(wv_c, j, 8, lambda c: xnT[:, c, T - 2:T], r_xn, wr_c, 2)
                    S.op("dve", lambda e, b=b, j=j, cc=cc: e.scalar_tensor_tensor(
                        out=ucarry[:, cc, :], in0=mmps[b][:, 0:2], scalar=flagv[:, 0:1], in1=h_ap(j)[:, 0:2],
                        op0=ALU.mult, op1=ALU.mult), reads=[r_mm[b], r_h[j], r_misc], writes=[r_uc])
                    continue
                b = mm_group(wv_c, j, 8, lambda c: xnT[:, c, 0:N], r_xn, wr_c, N)
                if not sample:
                    u = nxt("ue", 1)
                    S.op("pool", lambda e, u=u, cc=cc: e.tensor_copy(out=ue[u][:, 0:2], in_=ucarry[:, cc, :]),
                         reads=[r_uc], writes=[r_ue[u]])
                    S.op("dve", lambda e, b=b, j=j, u=u: e.tensor_tensor(out=ue[u][:, 2:2 + N], in0=mmps[b][:, 0:N],
                                                                         in1=h_ap(j)[:, 0:N], op=ALU.mult),
                         reads=[r_mm[b], r_h[j]], writes=[r_ue[u]])
                    S.op("pool", lambda e, u=u, cc=cc: e.tensor_copy(out=ucarry[:, cc, :], in_=ue[u][:, N:N + 2]),
                         reads=[r_ue[u]], writes=[r_uc])
                    y = h_ap(j)[:, 0:N]
                    S.op("dve", lambda e, u=u, cc=cc, y=y: e.tensor_scalar(out=y, in0=ue[u][:, 0:N], scalar1=convw[:, cc, 0:1],
                                                                            scalar2=None, op0=ALU.mult),
                         reads=[r_ue[u], r_misc], writes=[r_h[j]])
                    for tap in (1, 2):
                        S.op("dve", lambda e, u=u, cc=cc, y=y, tap=tap: e.scalar_tensor_tensor(
                            out=y, in0=ue[u][:, tap:tap + N], scalar=convw[:, cc, tap:tap + 1], in1=y,
                            op0=ALU.mult, op1=ALU.add), reads=[r_ue[u], r_misc], writes=[r_h[j]])
                else:
                    S.op("sp", lambda e, cc=cc: e.dma_start(out=ues[:, :, 0:2], in_=stT[:, cc, :, :]),
                         writes=[r_ue[0]], dma=True)
                    S.op("dve", lambda e, b=b, j=j: e.tensor_tensor(
                        out=ues[:, :, 2:6], in0=mmps[b][:, 0:N].rearrange("p (b t) -> p b t", t=4),
                        in1=h_ap(j)[:, 0:N].rearrange("p (b t) -> p b t", t=4), op=ALU.mult),
                        reads=[r_mm[b], r_h[j]], writes=[r_ue[0]])
                    out_ops.append(S.op("sp", lambda e, cc=cc: e.dma_start(out=sconvT[:, cc, :, :], in_=ues[:, :, 4:6]),
                                        reads=[r_ue[0]], dma=True))
                    y = h_ap(j)[:, 0:N].rearrange("p (b t) -> p b t", t=4)
                    S.op("pool", lambda e, cc=cc, y=y: e.tensor_scalar(out=y, in0=ues[:, :, 0:4], scalar1=convw[:, cc, 0:1],
                                                                       scalar2=None, op0=ALU.mult),
                         reads=[r_ue[0], r_misc], writes=[r_h[j]])
                    for tap in (1, 2):
                        S.op("dve", lambda e, cc=cc, y=y, tap=tap: e.scalar_tensor_tensor(
                            out=y, in0=ues[:, :, tap:tap + 4], scalar=convw[:, cc, tap:tap + 1], in1=y,
                            op0=ALU.mult, op1=ALU.add), reads=[r_ue[0], r_misc], writes=[r_h[j]])
            if halo_only:
                continue
            wv_b, wr_b = wload("in", 2 + half)
            for j in range(4):
                cc = half * 4 + j
                b = mm_group(wv_b, j, 8, lambda c: xnT[:, c, 0:N], r_xn, wr_b, N)
                S.op("dve", lambda e, b=b, j=j, cc=cc: e.tensor_tensor(out=by_ap(cc)[:, 0:N], in0=mmps[b][:, 0:N],
                                                                      in1=h_ap(j)[:, 0:N], op=ALU.mult),
                     reads=[r_mm[b], r_h[j]], writes=[r_by[cc]])

    def evac_vt(src_ps, dst, rsrc, rdst, npart):
        if nxt("bal", 2) == 0:
            S.op("act", lambda e: e.activation(out=dst, in_=src_ps, func=AF.Copy), reads=[rsrc], writes=[rdst])
        else:
            S.op("dve", lambda e: e.tensor_copy(out=dst, in_=src_ps), reads=[rsrc], writes=[rdst])

    def attn_stages(idx, g, c, qap, kchunks, vchunks, nq, as_i, as_view, first, ones_list):
        sset = idx % 2
        stt = st_sets[sset]
        pvt = pv_sets[sset]
        trt = tr_sets[sset]
        rs_st = [r_st0] if sset == 0 else [r_mm[0], r_mm[1]]
        rs_pv = [r_pv] if sset == 0 else [r_mm[2]]
        rs_tr = [r_tr0] if sset == 0 else [r_ns]
        W = 2 * nq
        ei = sset
        nk0 = kchunks[0][1]
        stv = stt[:, :, 0:W].rearrange("p h (k q) -> p h k q", k=2)
        esv = es_[ei][:, 0:2 * W].rearrange("p (h k q) -> p h k q", h=2, k=2)
        ptv = pt_[ei][:, 0:2 * W].rearrange("p (h k q) -> p h k q", h=2, k=2)
        gh = 8 * g + 2 * c
        vts = [(2 * sset + ci, nk) for ci, (vap, nk) in enumerate(vchunks)]

        def stageA():
            def fs(e):
                for hh in range(2):
                    pr = slice(hh * 64, hh * 64 + 64)
                    for ci, (kap, nk) in enumerate(kchunks):
                        ins = e.matmul(stt[0:nk, hh, ci * nq: ci * nq + nq], lhsT=kap[pr, :], rhs=qap[pr, :],
                                       start=True, stop=True)
                return ins
            S.op("pe", fs, reads=[r_q[g][c], r_kt[g][c]], writes=rs_st)
            for ci, (vap, nk) in enumerate(vchunks):
                S.op("pe", lambda e, vap=vap, nk=nk, ci=ci: e.transpose(trt[0:nk, ci * 512: ci * 512 + 128], vap, identb[:]),
                     reads=[r_vt[g][c], r_misc], writes=rs_tr)

        def stageB1():
            use_act = (idx % 2 == 0)
            if nk0 == 128:
                S.op("act", lambda e: e.activation(out=es_[ei][:, 0:2 * W].rearrange("p (h w) -> p h w", h=2), in_=stt[:, :, 0:W],
                                                   func=AF.Exp, scale=0.125),
                     reads=rs_st, writes=[r_es[ei]])
            else:
                S.op("act", lambda e: e.activation(out=esv[:, :, 1, :], in_=stv[:, :, 1, :], func=AF.Exp, scale=0.125),
                     reads=rs_st, writes=[r_es[ei]])
                S.op("act", lambda e: e.activation(out=esv[0:nk0, :, 0, :], in_=stv[0:nk0, :, 0, :], func=AF.Exp, scale=0.125),
                     reads=rs_st, writes=[r_es[ei]])
            for ci, (vap, nk) in enumerate(vchunks):
                vi = vts[ci][0]
                src_ps = trt[0:nk, ci * 512: ci * 512 + 128]
                dstv = vt_[vi][0:nk, :]
                if use_act:
                    S.op("act", lambda e, src_ps=src_ps, dstv=dstv: e.activation(out=dstv, in_=src_ps, func=AF.Copy),
                         reads=rs_tr, writes=[r_vtile[vi]])
                else:
                    S.op("dve", lambda e, src_ps=src_ps, dstv=dstv: e.tensor_copy(out=dstv, in_=src_ps),
                         reads=rs_tr, writes=[r_vtile[vi]])

        def stageB():
            if nk0 == 128:
                Ev = Eall[:, gh:gh + 2, :].rearrange("p h (k q) -> p h k q", k=2)[:, :, :, 0:nq]
                S.op("dve", lambda e: e.tensor_tensor(out=ptv, in0=esv, in1=Ev, op=ALU.mult),
                     reads=[r_es[ei], r_E], writes=[r_pt[ei]])
            else:
                S.op("dve", lambda e: e.tensor_tensor(out=ptv[:, :, 1, :], in0=esv[:, :, 1, :],
                                                      in1=Eall[:, gh:gh + 2, 128:128 + nq], op=ALU.mult),
                     reads=[r_es[ei], r_E], writes=[r_pt[ei]])
                S.op("dve", lambda e: e.tensor_tensor(out=ptv[0:nk0, :, 0, :], in0=esv[0:nk0, :, 0, :],
                                                      in1=Eall[0:nk0, gh:gh + 2, 0:nq], op=ALU.mult),
                     reads=[r_es[ei], r_E], writes=[r_pt[ei]])

        def stageC():
            def fpv(e):
                for hh in range(2):
                    pr = slice(hh * 64, hh * 64 + 64)
                    for ci, (vi, nk) in enumerate(vts):
                        e.matmul(pvt[pr, 0:nq], lhsT=vt_[vi][0:nk, pr], rhs=ptv[0:nk, hh, ci, :],
                                 start=(ci == 0), stop=(ci == 1))
                    for ci, (vi, nk) in enumerate(vts):
                        ins = e.matmul(pvt[pr, 128:128 + nq], lhsT=ones_list[ci][0:nk, :], rhs=ptv[0:nk, hh, ci, :],
                                       start=(ci == 0), stop=(ci == 1))
                return ins
            S.op("pe", fpv, reads=[r_pt[ei], r_misc] + [r_vtile[vi] for vi, _ in vts], writes=rs_pv)

        def stageD():
            pvv = pvt[:, 0:256].rearrange("p (a q) -> p a q", a=2)[:, :, 0:nq]
            if first:
                S.op("act", lambda e: e.activation(out=as_view, in_=pvv, func=AF.Copy), reads=rs_pv, writes=[r_as[as_i]])
            else:
                S.op("dve", lambda e: e.tensor_tensor(out=as_view, in0=pvv, in1=as_view, op=ALU.add),
                     reads=rs_pv, writes=[r_as[as_i]])
        return stageA, stageB, stageC, stageD, stageB1

    def attention_tile(ti):
        blocks = []
        for c in range(4):
            ai = 0
            asb = AS[ai]
            for blk in range(4):
                q = qT_ap(0, c)[:, blk * 128:(blk + 1) * 128]
                kc = KT[0][:, c, 128 + blk * 128: 256 + blk * 128]
                kp = KT[0][:, c, blk * 128: 128 + blk * 128]
                vc = VT[0][:, c, 128 + blk * 128: 256 + blk * 128]
                vp = VT[0][:, c, blk * 128: 128 + blk * 128]
                halo = (ti == 0 and blk == 0)
                blocks.append((0, c, q, [(kc, 128), (kp, 128)], [(vc, 128), (vp, 128)], 128, ai,
                               asb[:, :, blk * 128:(blk + 1) * 128], True,
                               [vones[:, 4, :], vones[:, 0, :] if halo else vones[:, 4, :]]))
            for r in range(4):
                q = qT_ap(1, c)[:, r * 128:(r + 1) * 128]
                kv = KT[1][:, c, :].rearrange("p (r l) -> p r l", r=4)
                vv = VT[1][:, c, :].rearrange("p (r l) -> p r l", r=4)
                halo = (ti == 0)
                blocks.append((1, c, q, [(kv[:, r, 128:256], 128), (kv[:, r, 0:128], 128)],
                               [(vv[:, r, 128:256], 128), (vv[:, r, 0:128], 128)], 128, ai,
                               asb[:].rearrange("p a (l r) -> p a l r", r=4)[:, :, :, r], False,
                               [vones[:, 4, :], vones[:, 0, :] if halo else vones[:, 4, :]]))
            for r in range(16):
                q = qT_ap(2, c)[:, r * 32:(r + 1) * 32]
                kv = KT[2][:, c, :].rearrange("p (r l) -> p r l", r=16)
                vv = VT[2][:, c, :].rearrange("p (r l) -> p r l", r=16)
                blocks.append((2, c, q, [(kv[:, r, 128:160], 32), (kv[:, r, 0:128], 128)],
                               [(vv[:, r, 128:160], 32), (vv[:, r, 0:128], 128)], 32, ai,
                               asb[:].rearrange("p a (l r) -> p a l r", r=16)[:, :, :, r], False,
                               [vones[:, 4, :], vones[:, min(ti, 4), :]]))
        stages = [attn_stages(i, *blk) for i, blk in enumerate(blocks)]
        nb = len(stages)
        stages[0][0]()
        stages[1][0]()
        for i in range(nb):
            stages[i][4]()
            if i + 2 < nb:
                stages[i + 2][0]()
            stages[i][1]()
            stages[i][2]()
            stages[i][3]()
            if i % 24 == 23:
                c = i // 24
                asb = AS[0]
                S.op("dve", lambda e, asb=asb: e.reciprocal(out=asb[:, 1, :], in_=asb[:, 1, :]), reads=[r_as[0]], writes=[r_as[0]])
                S.op("dve", lambda e, asb=asb, c=c: e.tensor_tensor(out=oT[:, c, :], in0=asb[:, 0, :], in1=asb[:, 1, :], op=ALU.mult),
                     reads=[r_as[0]], writes=[r_o[c]])

    def sample_attention():
        N = NS
        for c in range(4):
            inherit(r_qbd, [r_kt[1][c], r_vt[1][c]])
        inherit(r_samp, [r_as[0]])
        S.op("pool", lambda e: e.memset(QbdA[:, 0:4096], 0.0), writes=[r_qbd])
        S.op("pool", lambda e: e.memset(QbdB[:, 0:2048], 0.0), writes=[r_qbd])
        for g in range(3):
            for c in range(4):
                for hh in range(2):
                    pr = slice(hh * 64, hh * 64 + 64)
                    S.op("pool", lambda e, g=g, c=c, hh=hh, pr=pr: e.tensor_copy(out=Qbd_ap(g)[pr, c, :, 2 * c + hh],
                                                                                 in_=qT_ap(g, c)[pr, 0:N]),
                         reads=[r_q[g][c]], writes=[r_qbd])
        S.op("pool", lambda e: e.tensor_copy(out=Es[:, 0:32].rearrange("p (t h) -> p t h", h=8),
                                             in_=Eall[:, 0:8, 128:132].rearrange("p h t -> p t h")),
             reads=[r_E], writes=[r_samp])
        for g in (1, 2):
            S.op("pool", lambda e, g=g: e.tensor_copy(out=Es[:, 32 * g:32 * g + 32].rearrange("p (t h) -> p t h", h=8),
                                                      in_=Eall[:, 8 * g:8 * g + 8, 128:129].rearrange("p h t -> p t h").to_broadcast([128, 4, 8])),
                 reads=[r_E], writes=[r_samp])
        S.op("pool", lambda e: e.memset(accn[:], 0.0), writes=[r_samp])
        S.op("pool", lambda e: e.memset(sn[:], 0.0), writes=[r_samp])
        for g in range(3):
            for c in range(4):
                qv = qT_ap(g, c)[:, 0:N].rearrange("p (b t) -> p b t", t=4)
                kvw = KTn[:, g, c, :].rearrange("p (b t) -> p b t", t=4)
                vvw = VTn[:, g, c, :].rearrange("p (b t) -> p b t", t=4)
                for dl in range(4 if g == 0 else 1):
                    n = 4 - dl
                    pv3 = prodb[:, 0:SB_ * n].rearrange("p (b t) -> p b t", t=n)
                    S.op("dve", lambda e, qv=qv, kvw=kvw, dl=dl, n=n, pv3=pv3: e.tensor_tensor(
                        out=pv3, in0=qv[:, :, dl:4], in1=kvw[:, :, 0:n], op=ALU.mult),
                        reads=[r_q[g][c], r_kt[g][c]], writes=[r_samp])
                    S.op("pe", lambda e, n=n: e.matmul(nsps[:, 0:SB_ * n], lhsT=blk64[:], rhs=prodb[:, 0:SB_ * n], start=True, stop=True),
                         reads=[r_samp, r_misc], writes=[r_ns])
                    S.op("act", lambda e, n=n: e.activation(out=pn[:, 0:SB_ * n], in_=nsps[:, 0:SB_ * n], func=AF.Exp, scale=0.125),
                         reads=[r_ns], writes=[r_samp])
                    pn3 = pn[:, 0:SB_ * n].rearrange("p (b t) -> p b t", t=n)
                    S.op("dve", lambda e, g=g, c=c, dl=dl, pn3=pn3: e.tensor_scalar(
                        out=pn3, in0=pn3, scalar1=ebn[:, g * 4 + c, dl:dl + 1], scalar2=None, op0=ALU.mult),
                        reads=[r_samp], writes=[r_samp])
                    snv = sn[:, c, :].rearrange("p (b t) -> p b t", t=4)[:, :, dl:4]
                    acv = accn[:, c, :].rearrange("p (b t) -> p b t", t=4)[:, :, dl:4]
                    S.op("dve", lambda e, snv=snv, pn3=pn3: e.tensor_tensor(out=snv, in0=snv, in1=pn3, op=ALU.add),
                         reads=[r_samp], writes=[r_samp])
                    S.op("dve", lambda e, pn3=pn3, vvw=vvw, n=n: e.tensor_tensor(out=pn3, in0=pn3, in1=vvw[:, :, 0:n], op=ALU.mult),
                         reads=[r_samp, r_vt[g][c]], writes=[r_samp])
                    S.op("dve", lambda e, acv=acv, pn3=pn3: e.tensor_tensor(out=acv, in0=acv, in1=pn3, op=ALU.add),
                         reads=[r_samp], writes=[r_samp])
        K2f = KT[2][:].rearrange("p c w -> p (c w)")
        V2f = VT[2][:].rearrange("p c w -> p (c w)")
        kts = [K2f[:, i * 4608:(i + 1) * 4608].rearrange("p (c k j) -> p c k j", c=4, k=9) for i in range(2)]
        vts = [V2f[:, i * 4608:(i + 1) * 4608].rearrange("p (k f) -> p k f", k=9) for i in range(2)]
        r_kts = [R("kts0"), R("kts1")]
        r_vts = [R("vts0"), R("vts1")]
        for i in range(2):
            for c in range(4):
                inherit(r_kts[i], [r_kt[2][c]])
                inherit(r_vts[i], [r_vt[2][c]])
        for b in range(SB_):
            i = b % 2
            for g in range(3):
                nt = 1 if g == 0 else 4
                k0 = 0 if g == 0 else (1 if g == 1 else 5)
                S.op("pool", lambda e, g=g, b=b, i=i, k0=k0, nt=nt: e.dma_start(
                    out=kts[i][:, :, k0:k0 + nt, :], in_=kcT[g][b]),
                    writes=[r_kts[i]], dma=True)
                d = DIL[g]
                S.op("pool", lambda e, g=g, b=b, i=i, k0=k0, nt=nt, d=d: e.dma_start(
                    out=vts[i][:, k0:k0 + nt, :], in_=cv[g][b].rearrange("(j t) f -> j t f", t=d)[:, 0:nt, :]),
                    writes=[r_vts[i]], dma=True)
            st = nxt("st", 2)

            def fs(e, b=b, i=i, st=st):
                for kap in range(9):
                    g = 0 if kap == 0 else (1 if kap < 5 else 2)
                    t = 0 if kap == 0 else (kap - 1) % 4
                    if g == 0:
                        col, ncol = 0, 32
                    else:
                        col, ncol = 32 * g + 8 * t, 8
                    for c in range(4):
                        if g == 0:
                            rhs = Qbd_ap(0)[:, c, 4 * b:4 * b + 4, :]
                        else:
                            rhs = Qbd_ap(g)[:, c, 4 * b + t, :]
                        ins = e.matmul(stps[st][:, col:col + ncol], lhsT=kts[i][:, c, kap, :], rhs=rhs,
                                       start=(c == 0), stop=(c == 3))
                return ins
            S.op("pe", fs, reads=[r_kts[i], r_qbd], writes=[r_st[st]])
            ei = nxt("es", 2)
            S.op("act", lambda e, st=st, ei=ei: e.activation(out=es_[ei][:, 0:96], in_=stps[st][:, 0:96], func=AF.Exp, scale=0.125),
                 reads=[r_st[st]], writes=[r_es[ei]])
            S.op("dve", lambda e, ei=ei: e.tensor_tensor(out=pt_[ei][:, 0:96], in0=es_[ei][:, 0:96], in1=Es[:], op=ALU.mult),
                 reads=[r_es[ei], r_samp], writes=[r_pt[ei]])

            def fp(e, i=i, ei=ei):
                for c in range(4):
                    for kap in range(9):
                        g = 0 if kap == 0 else (1 if kap < 5 else 2)
                        t = 0 if kap == 0 else (kap - 1) % 4
                        col, ncol = (0, 32) if g == 0 else (32 * g + 8 * t, 8)
                        e.matmul(pvps[:, c * 96 + col: c * 96 + col + ncol], lhsT=vts[i][:, kap, c * 128:(c + 1) * 128],
                                 rhs=pt_[ei][:, col:col + ncol], start=True, stop=True)
                return e.matmul(pvps[:, 384:480], lhsT=onesb[:], rhs=pt_[ei][:, 0:96], start=True, stop=True)
            S.op("pe", fp, reads=[r_vts[i], r_pt[ei], r_misc], writes=[r_pv])
            for c in range(4):
                for hh in range(2):
                    pr = slice(hh * 64, hh * 64 + 64)
                    h = 2 * c + hh
                    srcA = pvps[pr, c * 96 + h: c * 96 + h + 96].rearrange("p (g t h) -> p t g h", g=3, t=4)[:, :, :, 0]
                    srcS = pvps[pr, 384 + h: 384 + h + 96].rearrange("p (g t h) -> p t g h", g=3, t=4)[:, :, :, 0]
                    S.op("dve", lambda e, pr=pr, c=c, b=b, srcA=srcA: e.tensor_reduce(
                        out=accb[pr, c, 4 * b:4 * b + 4], in_=srcA, axis=mybir.AxisListType.X, op=ALU.add),
                        reads=[r_pv], writes=[r_samp])
                    S.op("dve", lambda e, pr=pr, c=c, b=b, srcS=srcS: e.tensor_reduce(
                        out=sbb[pr, c, 4 * b:4 * b + 4], in_=srcS, axis=mybir.AxisListType.X, op=ALU.add),
                        reads=[r_pv], writes=[r_samp])
        S.op("dve", lambda e: e.tensor_tensor(out=accb[:], in0=accb[:], in1=accn[:], op=ALU.add), reads=[r_samp], writes=[r_samp])
        S.op("dve", lambda e: e.tensor_tensor(out=sbb[:], in0=sbb[:], in1=sn[:], op=ALU.add), reads=[r_samp], writes=[r_samp])
        S.op("dve", lambda e: e.reciprocal(out=sbb[:], in_=sbb[:]), reads=[r_samp], writes=[r_samp])
        for c in range(4):
            S.op("dve", lambda e, c=c: e.tensor_tensor(out=oT[:, c, 0:N], in0=accb[:, c, :], in1=sbb[:, c, :], op=ALU.mult),
                 reads=[r_samp], writes=[r_o[c]])
        for g in range(3):
            for which, buf in ((0, KTn), (1, VTn)):
                i = nxt("tf", 2)
                for c in range(4):
                    ti_ = nxt("tr", 2)
                    S.op("pe", lambda e, buf=buf, g=g, c=c, ti_=ti_: e.transpose(trps[0:N, ti_ * 512: ti_ * 512 + 128], buf[:, g, c, :], identb[:]),
                         reads=[r_kt[g][c], r_vt[g][c], r_misc], writes=[r_tr[ti_]])
                    S.op("act", lambda e, i=i, c=c, ti_=ti_: e.activation(
                        out=tmpf[i][0:N, c * 128:(c + 1) * 128], in_=trps[0:N, ti_ * 512: ti_ * 512 + 128], func=AF.Copy),
                        reads=[r_tr[ti_]], writes=[r_tmpf[i]])
                dst = (sk if which == 0 else sv)[g]
                for b in range(SB_):
                    out_ops.append(S.op("sp", lambda e, dst=dst, g=g, i=i, b=b: e.dma_start(
                        out=dst[b, WIN[g] - 4:WIN[g], :], in_=tmpf[i][4 * b:4 * b + 4, :]), reads=[r_tmpf[i]], dma=True))

    def full_tile(ti, sample):
        N = NS if sample else T
        mi = 0
        if sample:
            load_x(xsT, N)
            dump(xTs[:, 0, 0:N], [r_x[0]], "x0")
            norm_stage(N, None, None, True, A1, 0)
            dump(rstd[:, 0:N], [r_rstd], "rstd1")
            dump(xnT[:, 0, 0:N], [r_xn[0]], "xn0", True)
            dump(xnT[:, 7, 0:N], [r_xn[7]], "xn7", True)
            dump(A1[:, 0, :], [r_mod], "A1_0")
            dump(modT[:, 0, :], [r_mod], "shift1_0")
        else:
            load_x(xT[:, NH + ti * T: NH + (ti + 1) * T], N)
            norm_stage(N, A1, modT[:, 0:8, :], False)
        for half in range(2):
            wv, wr = wload("in", 15 + half)
            for j in range(4):
                cc = half * 4 + j
                b = mm_group(wv, j, 8, lambda c: xnT[:, c, 0:N], r_xn, wr, N)
                S.op("act", lambda e, b=b, cc=cc: e.activation(out=mixT[:, cc, 0:N], in_=mmps[b][:, 0:N], func=AF.Sigmoid),
                     reads=[r_mm[b]], writes=[r_mix[cc]])
        for j in range(4):
            inherit(r_h[j], [r_hid[20 + 2 * j], r_hid[21 + 2 * j]])
        for cc in range(8):
            inherit(r_by[cc], [r_hid[12 + cc]])
        conv_branch(N, sample)
        for half in range(2):
            wv, wr = wload("co", half)
            for j in range(4):
                cc = half * 4 + j
                b = mm_group(wv, j, 8, lambda c: by_ap(c)[:, 0:N], r_by, wr, N)
                S.op("dve", lambda e, b=b, cc=cc: e.tensor_tensor(out=mixT[:, cc, 0:N], in0=mmps[b][:, 0:N], in1=mixT[:, cc, 0:N], op=ALU.mult),
                     reads=[r_mm[b]], writes=[r_mix[cc]])
        for g in range(3):
            for c in range(4):
                inherit(r_q[g][c], [r_hid[g * 4 + c]] + ([r_sg[g * 4 + c]] if g * 4 + c < 8 else []))
        for g in range(3):
            wv, wr = wload("in", 6 + g)
            qk_block(wv, wr, g, 0, N, sample)
            wv, wr = wload("in", 9 + g)
            qk_block(wv, wr, g, 1, N, sample)
            wv, wr = wload("in", 12 + g)
            v_block(wv, wr, g, N, sample, False)
        flush_pending()
        if sample:
            dump(mixT[:, 0, 0:N], [r_mix[0]], "mix0_after_conv", True)
            dump(qT_ap(0, 0)[:, 0:N], [r_q[0][0]], "q00", True)
            dump(KTn[:, 0, 0, :], [r_kt[0][0]], "k00", True)
            dump(VTn[:, 0, 0, :], [r_vt[0][0]], "v00", True)
            dump(qT_ap(2, 1)[:, 0:N], [r_q[2][1]], "q21", True)
            dump(KTn[:, 2, 1, :], [r_kt[2][1]], "k21", True)
            dump(VTn[:, 2, 1, :], [r_vt[2][1]], "v21", True)
            sample_attention()
            dump(accn[:, 0, :], [r_samp], "accn0")
            dump(sn[:, 0, :], [r_samp], "sn0")
            dump(accb[:, 0, :], [r_samp], "accb0(final)")
            dump(sbb[:, 0, :], [r_samp], "rsbb0(final recip)")
            dump(oT[:, 0, 0:N], [r_o[0]], "o0", True)
            dump(oT[:, 3, 0:N], [r_o[3]], "o3", True)
        else:
            attention_tile(ti)
        for k in range(8):
            inherit(r_sg[k], [r_q[k // 4][k % 4]])
        for half in range(2):
            wv, wr = wload("in", 17 + half)
            for j in range(4):
                cc = half * 4 + j
                b = mm_group(wv, j, 8, lambda c: xnT[:, c, 0:N], r_xn, wr, N)
                S.op("act", lambda e, b=b, cc=cc: e.activation(out=sgatt_ap(cc)[:, 0:N], in_=mmps[b][:, 0:N], func=AF.Sigmoid),
                     reads=[r_mm[b]], writes=[r_sg[cc]])
        wv, wr = wload("ao", 0)
        for jj in range(8):
            b = mm_group(wv, jj, 4, lambda c: oT[:, c, 0:N], r_o, wr, N)
            i = nxt("tf", 2)
            S.op("dve", lambda e, b=b, jj=jj, i=i: e.tensor_tensor(out=tmpf[i][:, 0:N], in0=mmps[b][:, 0:N], in1=sgatt_ap(jj)[:, 0:N], op=ALU.mult),
                 reads=[r_mm[b], r_sg[jj]], writes=[r_tmpf[i]])
            S.op("pool", lambda e, jj=jj, i=i: e.tensor_tensor(out=mixT[:, jj, 0:N], in0=mixT[:, jj, 0:N], in1=tmpf[i][:, 0:N], op=ALU.add),
                 reads=[r_tmpf[i]], writes=[r_mix[jj]])
        for half in range(2):
            wv, wr = wload("o", half)
            for j in range(4):
                jj = half * 4 + j
                b = mm_group(wv, j, 8, lambda c: mixT[:, c, 0:N], r_mix, wr, N)
                if not sample:
                    S.op("dve", lambda e, b=b, jj=jj: e.scalar_tensor_tensor(out=xTs[:, jj, 0:N], in0=mmps[b][:, 0:N], scalar=modT[:, 16 + jj, 0:1],
                                                                            in1=xTs[:, jj, 0:N], op0=ALU.mult, op1=ALU.add),
                         reads=[r_mm[b], r_mod], writes=[r_x[jj]])
                else:
                    i = nxt("tf", 2)
                    S.op("dve", lambda e, b=b, jj=jj, i=i: e.tensor_tensor(out=v3(tmpf[i][:, 0:N]), in0=v3(mmps[b][:, 0:N]), in1=mb(modT, 16 + jj), op=ALU.mult),
                         reads=[r_mm[b], r_mod], writes=[r_tmpf[i]])
                    S.op("dve", lambda e, jj=jj, i=i: e.tensor_tensor(out=xTs[:, jj, 0:N], in0=xTs[:, jj, 0:N], in1=tmpf[i][:, 0:N], op=ALU.add),
                         reads=[r_tmpf[i]], writes=[r_x[jj]])
        if sample:
            dump(mixT[:, 0, 0:N], [r_mix[0]], "mix0_final", True)
            dump(xTs[:, 0, 0:N], [r_x[0]], "x1_0")
        if sample:
            norm_stage(N, None, None, True, A2, 24)
        else:
            norm_stage(N, A2, modT[:, 24:32, :], False)
        for k in range(32):
            inherit(r_hid[k], hid_alias(k))
        for blk in range(8):
            wv, wr = wload("mi", blk)
            for j in range(4):
                k = blk * 4 + j
                b = mm_group(wv, j, 8, lambda c: xnT[:, c, 0:N], r_xn, wr, N)
                i = nxt("tf", 2)
                S.op("act", lambda e, b=b, i=i: e.activation(out=tmpf[i][:, 0:N], in_=mmps[b][:, 0:N], func=AF.Relu),
                     reads=[r_mm[b]], writes=[r_tmpf[i]])
                S.op("pool", lambda e, k=k, i=i: e.tensor_tensor(out=hid_ap(k)[:, 0:N], in0=tmpf[i][:, 0:N], in1=tmpf[i][:, 0:N], op=ALU.mult),
                     reads=[r_tmpf[i]], writes=[r_hid[k]])
        for jj in range(8):
            wv, wr = wload("mo", jj)
            b = mm_group(wv, 0, 32, lambda c: hid_ap(c)[:, 0:N], r_hid, wr, N)
            if not sample:
                S.op("dve", lambda e, b=b, jj=jj: e.scalar_tensor_tensor(out=xTs[:, jj, 0:N], in0=mmps[b][:, 0:N], scalar=modT[:, 40 + jj, 0:1],
                                                                        in1=xTs[:, jj, 0:N], op0=ALU.mult, op1=ALU.add),
                     reads=[r_mm[b], r_mod], writes=[r_x[jj]])
                dst = yT[jj * 128:(jj + 1) * 128, ti * T:(ti + 1) * T]
            else:
                i = nxt("tf", 2)
                S.op("dve", lambda e, b=b, jj=jj, i=i: e.tensor_tensor(out=v3(tmpf[i][:, 0:N]), in0=v3(mmps[b][:, 0:N]), in1=mb(modT, 40 + jj), op=ALU.mult),
                     reads=[r_mm[b], r_mod], writes=[r_tmpf[i]])
                S.op("dve", lambda e, jj=jj, i=i: e.tensor_tensor(out=xTs[:, jj, 0:N], in0=xTs[:, jj, 0:N], in1=tmpf[i][:, 0:N], op=ALU.add),
                     reads=[r_tmpf[i]], writes=[r_x[jj]])
                dst = ysT[jj * 128:(jj + 1) * 128, :]
            out_ops.append(S.op("act", lambda e, jj=jj, dst=dst: e.dma_start(out=dst, in_=xTs[:, jj, 0:N]), reads=[r_x[jj]], dma=True))
        if not sample:
            shift_hist()

    for hi in range(NHT if (STAGE >= 1 and not NOPROMPT) else 0):
        halo_off[0] = hi
        load_x(xT[:, hi * T:(hi + 1) * T], T)
        norm_stage(T, A1, modT[:, 0:8, :], False)
        groups = (0, 1, 2) if hi == NHT - 1 else (2,)
        for g in groups:
            wv, wr = wload("in", 9 + g)
            qk_block(wv, wr, g, 1, T, False)
            wv, wr = wload("in", 12 + g)
            v_block(wv, wr, g, T, False, True)
        flush_pending()
        if hi == NHT - 1:
            conv_branch(T, False, halo_only=True)
            shift_hist((0,))
    halo_off[0] = None
    for ti in range(0 if NOPROMPT else (NT if STAGE >= 3 else (1 if STAGE == 2 else 0))):
        full_tile(ti, False)
    out_ops.append(S.op("sp", lambda e: e.dma_start(out=pconvT[:, :, :], in_=ucarry[:]), reads=[r_uc], dma=True))
    for g in range(3):
        d = DIL[g]
        for buf, rr, dst in ((KT, r_kt, pkT[g]), (VT, r_vt, pvT[g])):
            for c in range(4):
                srcv = buf[g][:, c, :].rearrange("p (r l) -> p r l", r=d)[:, :, 0:128] if g > 0 else buf[g][:, c, 0:128].unsqueeze(1)
                out_ops.append(S.op("pool", lambda e, srcv=srcv, dst=dst, c=c: e.dma_start(out=dst[c * 128:(c + 1) * 128, :, :], in_=srcv),
                                    reads=[rr[g][c]], dma=True))
    if STAGE >= 4:
        S.strict = True
        full_tile(0, True)
        S.strict = False
    if STAGE >= 5:
        issue_copies(len(copy_jobs))
    S.op("sp", None, reads=[], writes=[])
    fin = S.q["sp"][-1]
    for o in out_ops:
        if o not in fin.deps:
            fin.deps.append(o)
            o.flag = True
    fin.fn = lambda e: e.nop()

    with nc.Block() as block:
        S.emit(nc, es, block)
    es.close()
    return nc


def _consts():
    onehot = np.zeros((32, 387), np.float32)
    for g in range(3):
        k = np.arange(129)
        bk = t5_bucket(k * DIL[g])
        onehot[bk, 129 * g + k] = 1.0
    sel = np.zeros((24, 12, 128), np.float32)
    for g in range(3):
        for c in range(4):
            for p in range(128):
                sel[8 * g + 2 * c + p // 64, g * 4 + c, p] = 1.0
    return onehot, sel


_NC_CACHE = {}


def kernel(x_prompt, x_sample, c_prompt, c_sample, state_conv, cache_k1, cache_v1, cache_k2, cache_v2,
           cache_k3, cache_v3, rel_bias, norm1_g, norm2_g, w_ada, b_ada, w_in, conv_w, q_norm_g, k_norm_g,
           w_conv_out, w_attn_out, w_o, w_mlp_in, w_mlp_out):
    f32 = np.float32
    A = lambda a: np.ascontiguousarray(np.asarray(a, dtype=f32))
    x_prompt, x_sample = A(x_prompt), A(x_sample)
    c_prompt, c_sample, state_conv = A(c_prompt), A(c_sample), A(state_conv)
    caches_k = [A(cache_k1)[0], A(cache_k2)[0], A(cache_k3)[0]]
    caches_v = [A(cache_v1)[0], A(cache_v2)[0], A(cache_v3)[0]]
    if "nc" not in _NC_CACHE:
        _NC_CACHE["nc"] = build_program()
    nc = _NC_CACHE["nc"]
    onehot, sel = _consts()

    def fm(v, n):
        return np.ascontiguousarray(A(v).reshape(n, 128).T)

    shared = {
        "b_adaT": fm(b_ada[0], 48), "n1g": fm(norm1_g[0], 8), "n2g": fm(norm2_g[0], 8),
        "convw": np.ascontiguousarray(A(conv_w[0]).reshape(3, 8, 128).transpose(2, 1, 0)),
        "qkg": np.ascontiguousarray(np.stack([np.tile(A(q_norm_g[0]), 2), np.tile(A(k_norm_g[0]), 2)], axis=1)),
        "onehot": onehot, "rel_bias": A(rel_bias), "ident": np.eye(128, dtype=f32), "sel": sel,
        "w_ada": A(w_ada[0]), "w_in": A(w_in[0]), "w_conv_out": A(w_conv_out[0]), "w_attn_out": A(w_attn_out[0]),
        "w_o": A(w_o[0]), "w_mlp_in": A(w_mlp_in[0]), "w_mlp_out": A(w_mlp_out[0]),
    }
    in_maps = []
    for core in range(NCORE):
        b, half = divmod(core, 2)
        m = dict(shared)
        xt = np.zeros((D, NH + NTOK), f32)
        xt[:, NH:] = x_prompt[b, half * NTOK:(half + 1) * NTOK].T
        if half == 1:
            xt[:, :NH] = x_prompt[b, NTOK - NH:NTOK].T
        m["xT"] = xt
        sl = slice(core * SB_, (core + 1) * SB_)
        m["xsT"] = np.ascontiguousarray(x_sample[sl].reshape(NS, D).T)
        crow = np.concatenate([c_prompt[b:b + 1], c_sample[sl]], axis=0)
        m["cT"] = np.ascontiguousarray(crow.reshape(17, 8, 128).transpose(2, 1, 0))
        m["stT"] = np.ascontiguousarray(state_conv[0, sl].reshape(SB_, 2, 8, 128).transpose(3, 2, 0, 1))
        m["flagv"] = np.full((128, 1), float(half), f32)
        vo = np.ones((128, 5, 64), f32)
        for ti in range(4):
            vo[:128 - 32 * ti, ti, :] = float(half)
        m["vones"] = vo
        for g in range(3):
            m["ck%d" % g] = np.ascontiguousarray(caches_k[g][sl].reshape(SB_, WIN[g], 512))
            m["cv%d" % g] = np.ascontiguousarray(caches_v[g][sl].reshape(SB_, WIN[g], 512))
            kk = m["ck%d" % g]
            d = DIL[g]
            nt = 1 if g == 0 else 4
            kr = kk.reshape(SB_, 128, d, 512)[:, :, 0:nt, :]
            m["kc%dT" % g] = np.ascontiguousarray(kr.reshape(SB_, 128, nt, 4, 128).transpose(0, 4, 3, 2, 1))
        in_maps.append(m)

    res = run_bass_kernel_spmd(nc, in_maps, core_ids=list(range(NCORE)))
    R_ = res.results
    yp = np.empty((4, 8192, D), f32)
    ys = np.empty((128, 4, D), f32)
    p_conv = np.empty((1, 4, 2, D), f32)
    pk = [np.empty((1, 4, WIN[g], 8, 64), f32) for g in range(3)]
    pv = [np.empty((1, 4, WIN[g], 8, 64), f32) for g in range(3)]
    s_conv = np.empty((1, 128, 2, D), f32)
    skk = [np.empty((1, 128, WIN[g], 8, 64), f32) for g in range(3)]
    svv = [np.empty((1, 128, WIN[g], 8, 64), f32) for g in range(3)]
    for core in range(NCORE):
        b, half = divmod(core, 2)
        r = R_[core]
        yp[b, half * NTOK:(half + 1) * NTOK] = r["yT"].T
        sl = slice(core * SB_, (core + 1) * SB_)
        ys[sl] = r["ysT"].T.reshape(SB_, 4, D)
        s_conv[0, sl] = r["sconvT"].transpose(2, 3, 1, 0).reshape(SB_, 2, D)
        for g in range(3):
            skk[g][0, sl] = r["sk%d" % g].reshape(SB_, WIN[g], 8, 64)
            svv[g][0, sl] = r["sv%d" % g].reshape(SB_, WIN[g], 8, 64)
        if half == 1:
            p_conv[0, b] = r["pconvT"].transpose(2, 1, 0).reshape(2, D)
            for g in range(3):
                pk[g][0, b] = r["pk%dT" % g].transpose(2, 1, 0).reshape(WIN[g], 8, 64)
                pv[g][0, b] = r["pv%dT" % g].transpose(2, 1, 0).reshape(WIN[g], 8, 64)
    return (yp, ys, p_conv, pk[0], pv[0], pk[1], pv[1], pk[2], pv[2],
            s_conv, skk[0], svv[0], skk[1], svv[1], skk[2], svv[2])
```
